# Optimizing a Trainium2 kernel written in Bass

```python
import jax, jax.numpy as jnp
from jax import lax
import numpy as np

D_MODEL = 2048
BATCH = 16
SEQ = 256
DEPTH = 2
DEC_BATCH = 8
DEC_SEQ = 4096
PAST_LEN = 256

GRID_W = 64
EPS = 1e-6
N_HEADS = 16
QK_NOPE = 64
QK_ROPE = 32
QK_HEAD = QK_NOPE + QK_ROPE
V_HEAD = 64
Q_LORA = 768
KV_LORA = 256
ROPE_THETA = 10000.0
AXIS_FREQS = QK_ROPE // 4
Q_BLOCK = 128
ATTN_SCALE = QK_HEAD ** -0.5
ATTN_W = N_HEADS * V_HEAD
POOL_WINDOWS = (2, 4, 8, 16)
POOL_GROUP_W = 128
POOL_W = POOL_GROUP_W * len(POOL_WINDOWS)
CHUNK = 128
SGU_GROUPS = 4
SGU_W = 512
SGU_GROUP_W = SGU_W // SGU_GROUPS
CONV_W = 512
CONV_K = 3
N_BRANCH = 4
D_FF = -(-8 * D_MODEL // (3 * 256)) * 256
OFF_CKV = Q_LORA
OFF_KPE = OFF_CKV + KV_LORA
OFF_POOL = OFF_KPE + QK_ROPE
OFF_SGU = OFF_POOL + POOL_W
OFF_CONV = OFF_SGU + 2 * SGU_W
OFF_GATE = OFF_CONV + 3 * CONV_W
N_IN = OFF_GATE + N_BRANCH * D_MODEL

kernel_name = "hybrid_diffusion_prefix_mla_pool_sgu_conv_step"


def rmsnorm(x, w):
    xf = x.astype(jnp.float32)
    y = xf * lax.rsqrt(jnp.mean(xf * xf, axis=-1, keepdims=True) + EPS)
    return (y * w.astype(jnp.float32)).astype(x.dtype)


def ada_params(cond, w_mod, b_mod):
    m = jax.nn.silu(cond) @ w_mod + b_mod
    m = m.reshape(m.shape[:-1] + (6, D_MODEL))
    return [m[..., i, :] for i in range(6)]


def modulate(x, norm_w, shift, scale):
    return rmsnorm(x, norm_w) * (1 + scale) + shift


def axial_rope_tables(T):
    rows = T // GRID_W
    row = jnp.repeat(jnp.arange(rows), GRID_W).astype(jnp.float32)
    col = jnp.tile(jnp.arange(GRID_W), rows).astype(jnp.float32)
    inv = ROPE_THETA ** (-jnp.arange(AXIS_FREQS, dtype=jnp.float32) / AXIS_FREQS)
    ang_r = row[:, None] * inv
    ang_c = col[:, None] * inv
    ang = jnp.stack([ang_r, ang_r, ang_c, ang_c], axis=1).reshape(T, QK_ROPE)
    return jnp.cos(ang), jnp.sin(ang)


def rope_rotary_part(x, cos, sin):
    xr = x[..., QK_NOPE:].astype(jnp.float32)
    xa = xr.reshape(xr.shape[:-1] + (2, 2, AXIS_FREQS))
    rot = jnp.stack([-xa[..., 1, :], xa[..., 0, :]], axis=-2).reshape(xr.shape)
    xr = xr * cos[None, :, None, :] + rot * sin[None, :, None, :]
    return jnp.concatenate([x[..., :QK_NOPE], xr.astype(x.dtype)], axis=-1)


def mla_latents(z, lw):
    c_q = rmsnorm(z[..., :OFF_CKV], lw["q_a_norm_w"])
    c_kv = rmsnorm(z[..., OFF_CKV:OFF_KPE], lw["kv_a_norm_w"])
    k_pe = z[..., OFF_KPE:OFF_POOL]
    return c_q, c_kv, k_pe


def mla_queries(c_q, w_uq, q_norm_w):
    B, T, _ = c_q.shape
    q = (c_q @ w_uq).reshape(B, T, N_HEADS, QK_HEAD)
    return rmsnorm(q, q_norm_w)


def mla_keys_values(c_kv, k_pe, w_ukv, k_norm_w):
    B, T, _ = c_kv.shape
    kv = (c_kv @ w_ukv).reshape(B, T, N_HEADS, QK_NOPE + V_HEAD)
    k_nope, v = kv[..., :QK_NOPE], kv[..., QK_NOPE:]
    k_rope = jnp.broadcast_to(k_pe[:, :, None, :], (B, T, N_HEADS, QK_ROPE))
    k = rmsnorm(jnp.concatenate([k_nope, k_rope], axis=-1), k_norm_w)
    return k, v


def attend_block(qb, k, v):
    s = jnp.einsum("bqhd,bkhd->bhqk", qb, k).astype(jnp.float32) * ATTN_SCALE
    p = jax.nn.softmax(s, axis=-1)
    return jnp.einsum("bhqk,bkhv->bqhv", p.astype(v.dtype), v)


def attend_blocked(q, k, v):
    B, T, H, Dh = q.shape
    nb = T // Q_BLOCK
    qb = q.reshape(B, nb, Q_BLOCK, H, Dh).transpose(1, 0, 2, 3, 4)
    ob = lax.map(lambda qi: attend_block(qi, k, v), qb)
    return ob.transpose(1, 0, 2, 3, 4).reshape(B, T, H * V_HEAD)


def multi_scale_pool(xp, w_pool, pool_scale):
    B, T, C = xp.shape
    xf = xp.astype(jnp.float32)
    S = jnp.concatenate([jnp.zeros((B, 1, C), jnp.float32), jnp.cumsum(xf, axis=1)], axis=1)
    t = jnp.arange(T)
    outs = []
    for g, w in enumerate(POOL_WINDOWS):
        lo = jnp.clip(t - w // 2, 0, T - 1)
        hi = jnp.clip(t + w // 2 - 1, 0, T - 1)
        sl = slice(g * POOL_GROUP_W, (g + 1) * POOL_GROUP_W)
        Sg = S[..., sl]
        cnt = (hi - lo + 1).astype(jnp.float32)[None, :, None]
        pooled = (Sg[:, hi + 1] - Sg[:, lo]) / cnt - xf[..., sl]
        outs.append(pooled.astype(xp.dtype) @ w_pool[g])
    return jnp.concatenate(outs, axis=-1) * pool_scale


def spatial_gating(zs, sgu_norm_w, w_spatial, b_spatial):
    zs = jax.nn.gelu(zs)
    u, v = zs[..., :SGU_W], zs[..., SGU_W:]
    v = rmsnorm(v, sgu_norm_w)
    B, T, _ = v.shape
    vc = v.reshape(B, T // CHUNK, CHUNK, SGU_GROUPS, SGU_GROUP_W)
    mixed = jnp.einsum("gqp,bnpgc->bnqgc", w_spatial, vc) + b_spatial.T[:, :, None]
    return u * mixed.reshape(B, T, SGU_W)


def short_conv(zc, conv_w):
    b_g = zc[..., :CONV_W]
    c_g = zc[..., CONV_W:2 * CONV_W]
    xc = zc[..., 2 * CONV_W:]
    y = c_g * xc
    yp = jnp.pad(y, ((0, 0), (1, 1), (0, 0)))
    conv = conv_w[0] * yp[:, :-2] + conv_w[1] * yp[:, 1:-1] + conv_w[2] * yp[:, 2:]
    return b_g * conv


def local_branches(z, lw):
    pool_o = multi_scale_pool(z[..., OFF_POOL:OFF_SGU], lw["w_pool"], lw["pool_scale"])
    sgu_o = spatial_gating(z[..., OFF_SGU:OFF_CONV], lw["sgu_norm_w"], lw["w_spatial"], lw["b_spatial"])
    conv_o = short_conv(z[..., OFF_CONV:OFF_GATE], lw["conv_w"])
    return pool_o, sgu_o, conv_o


def merge_branches(z, attn_o, pool_o, sgu_o, conv_o, lw):
    B, T, _ = z.shape
    g = jax.nn.sigmoid(z[..., OFF_GATE:] + lw["b_gate"]).reshape(B, T, N_BRANCH, D_MODEL)
    m = (g[..., 0, :] * (attn_o @ lw["w_br_attn"]) + g[..., 1, :] * (pool_o @ lw["w_br_pool"])
         + g[..., 2, :] * (sgu_o @ lw["w_br_sgu"]) + g[..., 3, :] * (conv_o @ lw["w_br_conv"]))
    return m @ lw["w_out"]


def swiglu(h, w_ffn_in, w_ffn_out):
    a = h @ w_ffn_in
    return (jax.nn.silu(a[..., :D_FF]) * a[..., D_FF:]) @ w_ffn_out


def context_layer(x, c_ctx, lw):
    sh_a, sc_a, g_a, sh_f, sc_f, g_f = ada_params(c_ctx, lw["w_mod"], lw["b_mod"])
    h = modulate(x, lw["norm_mix_w"], sh_a, sc_a)
    z = h @ lw["w_in"]
    c_q, c_kv, k_pe = mla_latents(z, lw)
    q = mla_queries(c_q, lw["w_uq"], lw["q_norm_w"])
    k, v = mla_keys_values(c_kv, k_pe, lw["w_ukv"], lw["k_norm_w"])
    attn_o = attend_blocked(q, k, v)
    mix = merge_branches(z, attn_o, *local_branches(z, lw), lw)
    x = x + g_a * mix
    x = x + g_f * swiglu(modulate(x, lw["norm_ffn_w"], sh_f, sc_f), lw["w_ffn_in"], lw["w_ffn_out"])
    return x, c_kv, k_pe


def latent_layer(x, cond, ckv_ctx, kpe_ctx, lw, cos, sin):
    sh_a, sc_a, g_a, sh_f, sc_f, g_f = ada_params(cond[:, None, :], lw["w_mod"], lw["b_mod"])
    h = modulate(x, lw["norm_mix_w"], sh_a, sc_a)
    z = h @ lw["w_in"]
    c_q, c_kv, k_pe = mla_latents(z, lw)
    q = rope_rotary_part(mla_queries(c_q, lw["w_uq"], lw["q_norm_w"]), cos, sin)
    k, v = mla_keys_values(c_kv, k_pe, lw["w_ukv"], lw["k_norm_w"])
    k = rope_rotary_part(k, cos, sin)
    k_c, v_c = mla_keys_values(ckv_ctx, kpe_ctx, lw["w_ukv"], lw["k_norm_w"])
    k_all = jnp.concatenate([k, k_c.astype(k.dtype)], axis=1)
    v_all = jnp.concatenate([v, v_c.astype(v.dtype)], axis=1)
    attn_o = attend_blocked(q, k_all, v_all)
    mix = merge_branches(z, attn_o, *local_branches(z, lw), lw)
    x = x + g_a * mix
    x = x + g_f * swiglu(modulate(x, lw["norm_ffn_w"], sh_f, sc_f), lw["w_ffn_in"], lw["w_ffn_out"])
    return x


def setup_inputs(seed: int = 0) -> dict:
    key = jax.random.key(seed)
    ks = iter(jax.random.split(key, 40))
    f32 = jnp.float32

    def nrm(shape, scale):
        return jax.random.normal(next(ks), shape, f32) * scale

    def gain(shape):
        return 1.0 + 0.02 * jax.random.normal(next(ks), shape, f32)

    L = DEPTH
    return {
        "x_prompt": nrm((BATCH, SEQ, D_MODEL), 1.0),
        "x_sample": nrm((DEC_BATCH, DEC_SEQ, D_MODEL), 1.0),
        "cache_ckv": nrm((DEC_BATCH, DEPTH, PAST_LEN, KV_LORA), 1.0),
        "cache_kpe": nrm((DEC_BATCH, DEPTH, PAST_LEN, QK_ROPE), 1.0),
        "c": nrm((DEC_BATCH, D_MODEL), 1.0),
        "c_ctx": nrm((D_MODEL,), 1.0),
        "w_mod": nrm((L, D_MODEL, 6 * D_MODEL), 0.5 * D_MODEL ** -0.5),
        "b_mod": nrm((L, 6 * D_MODEL), 0.02),
        "norm_mix_w": gain((L, D_MODEL)),
        "norm_ffn_w": gain((L, D_MODEL)),
        "w_in": nrm((L, D_MODEL, N_IN), D_MODEL ** -0.5),
        "b_gate": nrm((L, N_BRANCH * D_MODEL), 0.02),
        "q_a_norm_w": gain((L, Q_LORA)),
        "kv_a_norm_w": gain((L, KV_LORA)),
        "w_uq": nrm((L, Q_LORA, N_HEADS * QK_HEAD), Q_LORA ** -0.5),
        "w_ukv": nrm((L, KV_LORA, N_HEADS * (QK_NOPE + V_HEAD)), KV_LORA ** -0.5),
        "q_norm_w": gain((L, QK_HEAD)),
        "k_norm_w": gain((L, QK_HEAD)),
        "w_pool": nrm((L, len(POOL_WINDOWS), POOL_GROUP_W, POOL_GROUP_W), POOL_GROUP_W ** -0.5),
        "pool_scale": gain((L, POOL_W)),
        "sgu_norm_w": gain((L, SGU_W)),
        "w_spatial": nrm((L, SGU_GROUPS, CHUNK, CHUNK), CHUNK ** -0.5),
        "b_spatial": gain((L, SGU_GROUPS, CHUNK)),
        "conv_w": nrm((L, CONV_K, CONV_W), CONV_K ** -0.5),
        "w_br_attn": nrm((L, ATTN_W, D_MODEL), ATTN_W ** -0.5),
        "w_br_pool": nrm((L, POOL_W, D_MODEL), POOL_W ** -0.5),
        "w_br_sgu": nrm((L, SGU_W, D_MODEL), SGU_W ** -0.5),
        "w_br_conv": nrm((L, CONV_W, D_MODEL), CONV_W ** -0.5),
        "w_out": nrm((L, D_MODEL, D_MODEL), D_MODEL ** -0.5),
        "w_ffn_in": nrm((L, D_MODEL, 2 * D_FF), D_MODEL ** -0.5),
        "w_ffn_out": nrm((L, D_FF, D_MODEL), D_FF ** -0.5),
    }


def reference(x_prompt, x_sample, cache_ckv, cache_kpe, c, c_ctx, w_mod, b_mod, norm_mix_w, norm_ffn_w,
              w_in, b_gate, q_a_norm_w, kv_a_norm_w, w_uq, w_ukv, q_norm_w, k_norm_w, w_pool, pool_scale,
              sgu_norm_w, w_spatial, b_spatial, conv_w, w_br_attn, w_br_pool, w_br_sgu, w_br_conv, w_out,
              w_ffn_in, w_ffn_out):
    cos, sin = axial_rope_tables(x_sample.shape[1])
    xp = x_prompt
    xs = x_sample
    ckv_list = []
    kpe_list = []
    for l in range(DEPTH):
        lw = dict(w_mod=w_mod[l], b_mod=b_mod[l], norm_mix_w=norm_mix_w[l], norm_ffn_w=norm_ffn_w[l],
                  w_in=w_in[l], b_gate=b_gate[l], q_a_norm_w=q_a_norm_w[l], kv_a_norm_w=kv_a_norm_w[l],
                  w_uq=w_uq[l], w_ukv=w_ukv[l], q_norm_w=q_norm_w[l], k_norm_w=k_norm_w[l],
                  w_pool=w_pool[l], pool_scale=pool_scale[l], sgu_norm_w=sgu_norm_w[l],
                  w_spatial=w_spatial[l], b_spatial=b_spatial[l], conv_w=conv_w[l],
                  w_br_attn=w_br_attn[l], w_br_pool=w_br_pool[l], w_br_sgu=w_br_sgu[l],
                  w_br_conv=w_br_conv[l], w_out=w_out[l], w_ffn_in=w_ffn_in[l], w_ffn_out=w_ffn_out[l])
        xp, ckv_l, kpe_l = context_layer(xp, c_ctx, lw)
        ckv_list.append(ckv_l)
        kpe_list.append(kpe_l)
        xs = latent_layer(xs, c, cache_ckv[:, l], cache_kpe[:, l], lw, cos, sin)
    state_ckv = jnp.stack(ckv_list, axis=1)
    state_kpe = jnp.stack(kpe_list, axis=1)
    return (xp, xs, state_ckv, state_kpe)
```

```python
import numpy as np
from contextlib import ExitStack
import concourse.bass as bass
import concourse.mybir as mybir
from concourse.bass_utils import run_bass_kernel_spmd

F32, BF16 = mybir.dt.float32, mybir.dt.bfloat16
AF = mybir.ActivationFunctionType
ALU = mybir.AluOpType
AX = mybir.AxisListType

D = 2048; L = 2; NT = 9; TT = 512; NTOK = 4608; NS = 4096
NIN = 12320; DFF = 5632; H = 16
NKEY = 4864; NKB = 38
EPS = 1e-6
ATTN_SCALE = 96 ** -0.5
OFF_POOL, OFF_SGU, OFF_CONV, OFF_GATE = 1056, 1568, 2592, 4128
NDS = 8
SAME_ENGINE_SYNC = True


class Buf:
    __slots__ = ("lw", "rd", "excl")

    def __init__(self, excl=False):
        self.lw = None
        self.rd = {}
        self.excl = excl


class Op:
    __slots__ = ("eng", "fn", "deps", "inc", "dma", "sem", "val", "epoch")


class Eng:
    pass


class _Stop(Exception):
    pass


class Sched:
    def __init__(self, nc, es):
        self.nc = nc
        self.E = {}
        for name, obj in (("pe", nc.tensor), ("act", nc.scalar), ("dve", nc.vector),
                          ("pool", nc.gpsimd), ("sp", nc.sync)):
            e = Eng()
            e.name = name; e.obj = obj
            e.sem = es.enter_context(nc.semaphore("s_" + name))
            e.count = 0; e.known = {}
            e.dsems = [es.enter_context(nc.semaphore("d_%s%d" % (name, i))) for i in range(NDS)] \
                if name in ("pool", "sp") else []
            e.duse = [0] * NDS; e.dcount = 0
            self.E[name] = e
        self.ops = []
        self.epoch = 0
        self.nid = 0

    def op(self, eng, fn, reads=(), writes=(), dma=False):
        E = self.E[eng]
        X = Op()
        X.eng = E; X.fn = fn; X.inc = False; X.dma = dma; X.sem = None; X.val = 0; X.epoch = self.epoch
        deps = []
        xr = [b for b in reads if b.excl]
        if xr:
            reads = [b for b in reads if not b.excl]
            writes = list(writes) + xr
        for b in reads:
            if b.lw is not None:
                deps.append(b.lw)
        for b in writes:
            if b.lw is not None:
                deps.append(b.lw)
            deps.extend(b.rd.values())
        out = []
        seen = set()
        for P in deps:
            if P.epoch != self.epoch or id(P) in seen or P is X:
                continue
            seen.add(id(P))
            if (not P.dma) and (not dma) and P.eng is E:
                if eng != "dve" or not SAME_ENGINE_SYNC:
                    continue
            if not P.dma:
                P.inc = True
            out.append(P)
        X.deps = out
        if dma:
            self.nid += 1
            key = ("d", self.nid)
        else:
            key = eng
        for b in reads:
            b.rd[key] = X
        for b in writes:
            b.lw = X
            b.rd = {}
        self.ops.append(X)
        return X

    def flush(self):
        last = {}
        for X in self.ops:
            if not X.dma:
                last[X.eng.name] = X
        for X in last.values():
            X.inc = True
        for X in self.ops:
            E = X.eng
            for P in X.deps:
                if E.known.get(P.sem, 0) < P.val:
                    E.obj.wait_ge(P.sem, P.val)
                    E.known[P.sem] = P.val
            if X.dma:
                slot = E.dcount % NDS
                E.dcount += 1
                sem = E.dsems[slot]
                E.duse[slot] += 1
                val = 16 * E.duse[slot]
                if val > 16 and E.known.get(sem, 0) < val - 16:
                    E.obj.wait_ge(sem, val - 16)
                    E.known[sem] = val - 16
                X.fn().then_inc(sem, 16)
                X.sem = sem; X.val = val
            else:
                ins = X.fn()
                if X.inc:
                    E.count += 1
                    ins.then_inc(E.sem, 1)
                    X.sem = E.sem; X.val = E.count
        self.ops = []
        for E in self.E.values():
            for P in self.E.values():
                if P is not E and P.count > 0 and E.known.get(P.sem, 0) < P.count:
                    E.obj.wait_ge(P.sem, P.count)
                    E.known[P.sem] = P.count
                for s in range(NDS if P.dsems else 0):
                    v = 16 * P.duse[s]
                    if v > 0 and E.known.get(P.dsems[s], 0) < v:
                        E.obj.wait_ge(P.dsems[s], v)
                        E.known[P.dsems[s]] = v
        self.epoch += 1


def build(dbg=None):
    nc = bass.Bass("TRN2", target_bir_lowering=False)

    def din(name, shape):
        return nc.dram_tensor(name, list(shape), F32, kind="ExternalInput").ap()

    def dout(name, shape):
        return nc.dram_tensor(name, list(shape), F32, kind="ExternalOutput").ap()

    def dscr(name, shape, dt):
        kind = "ExternalOutput" if (dbg and name in dbg) else "Internal"
        return nc.dram_tensor(name, list(shape), dt, kind=kind).ap()

    xT_in = din("xT", [D, NTOK])
    cT_in = din("cT", [128, 16, 2])
    cckvT_in = din("cckvT", [L, 256, 256])
    ckpe_in = din("ckpe", [L, 256, 32])
    ckpeT_in = din("ckpeT", [L, 32, 256])
    w_mod = din("w_mod", [L, D, 6 * D])
    b_modT = din("b_modT", [L, 128, 96])
    nmixT = din("nmixT", [L, 128, 16])
    nffnT = din("nffnT", [L, 128, 16])
    w_in = din("w_in", [L, D, NIN])
    b_gateT = din("b_gateT", [L, 128, 64])
    qawT = din("qawT", [L, 128, 6])
    kvawT = din("kvawT", [L, 128, 2])
    kvaw_row = din("kvaw_row", [L, 256])
    w_uq = din("w_uq", [L, 768, 1536])
    w_ukv = din("w_ukv", [L, 256, 2048])
    qnw = din("qnw", [L, 96, 1])
    knw = din("knw", [L, 96, 1])
    w_pool = din("w_pool", [L, 4, 128, 128])
    pscaleT = din("pscaleT", [L, 128, 4])
    sguwT = din("sguwT", [L, 128, 4])
    w_spT = din("w_spT", [L, 4, 128, 128])
    b_sp = din("b_sp", [L, 512])
    convwT = din("convwT", [L, 128, 12])
    w_br = [din("w_br_attn", [L, 1024, D]), din("w_br_pool", [L, 512, D]),
            din("w_br_sgu", [L, 512, D]), din("w_br_conv", [L, 512, D])]
    w_out = din("w_out", [L, D, D])
    w_ffn_in = din("w_ffn_in", [L, D, 2 * DFF])
    w_ffn_out = din("w_ffn_out", [L, DFF, D])
    cos_in = din("cosT", [96, NTOK])
    sin_in = din("sinT", [96, NTOK])
    prot_in = din("prot", [96, 96])
    sel_in = din("sel", [65, 64])
    inv_in = din("invtab", [4, 128, 4, 512])

    yT = dout("yT", [D, NTOK])
    st_ckv = dout("st_ckv", [2, L, 256, 256])
    st_kpe = dout("st_kpe", [2, L, 256, 32])

    XT1 = dscr("XT1", [D, NTOK], F32)
    Qsc = dscr("Qsc", [96, H, NTOK], BF16)
    Ksc = dscr("Ksc", [96, H, NKEY], BF16)
    Vsc = dscr("Vsc", [NKB, 128, H * 65], BF16)
    SCsc = dscr("SCsc", [NKB, 128, H], F32)
    ATTNsc = dscr("ATTNsc", [1024, NTOK], BF16)
    POOLsc = dscr("POOLsc", [512, NTOK], F32)
    SGUsc = dscr("SGUsc", [512, NTOK], BF16)
    CONVy = dscr("CONVy", [512, NTOK], F32)
    CONVb = dscr("CONVb", [512, NTOK], F32)
    WG = dscr("WG", [D, 4 * D], BF16)
    WBR = dscr("WBR", [2560, D], BF16)
    WO = dscr("WO", [D, D], BF16)
    WFI = dscr("WFI", [D, 2 * DFF], BF16)
    WFO = dscr("WFO", [DFF, D], BF16)
    BR_OFF = [0, 1024, 1536, 2048]

    stop_after = (dbg or {}).get("stop_after", None)
    nlayers = (dbg or {}).get("nlayers", L)

    with ExitStack() as top:
        S = Sched(nc, top)
        op = S.op

        uid = [0]

        def T(es, name, shape, dt):
            uid[0] += 1
            return es.enter_context(nc.sbuf_tensor("sb_%s_%d" % (name, uid[0]), list(shape), dt))

        def pe(fn, r, w):
            return op("pe", fn, r, w)

        def mm(out, lhsT, rhs, start, stop, r, w):
            return op("pe", lambda: nc.tensor.matmul(out, lhsT=lhsT, rhs=rhs, start=start, stop=stop), r, w)

        def act(fn, r, w):
            return op("act", fn, r, w)

        def dve(fn, r, w):
            return op("dve", fn, r, w)

        def spdma(out, in_, r, w):
            return op("sp", lambda: nc.sync.dma_start(out=out, in_=in_), r, w, dma=True)

        def pldma(out, in_, r, w):
            return op("pool", lambda: nc.gpsimd.dma_start(out=out, in_=in_), r, w, dma=True)

        MV = T(top, "MV", [128, L * 2 * 6, 16], F32)
        MVb = Buf()
        ones_bf = T(top, "ones_bf", [128, 128], BF16)
        prot_bf = T(top, "prot_bf", [96, 96], BF16)
        sel_f = T(top, "sel_f", [65, 64], F32)
        constb = Buf()
        dve(lambda: nc.vector.memset(ones_bf[:], 1.0), [], [constb])
        pldma(prot_bf[:], prot_in, [], [constb])
        spdma(sel_f[:], sel_in, [], [constb])

        def mvec(l, g, i):
            k = (l * 2 + g) * 6 + i
            return MV[:, k, :]

        class PsumPool:
            def __init__(self, es, n=8):
                uid[0] += 1
                self.t = [es.enter_context(nc.psum_tensor("ps%d_%d" % (i, uid[0]), [128, 512], F32)) for i in range(n)]
                self.b = [Buf(excl=True) for _ in range(n)]
                self.i = 0

            def get(self):
                k = self.i % len(self.t)
                self.i += 1
                return self.t[k], self.b[k]

        class WPool:
            def __init__(self, es, n, dt=BF16, cols=512, name="wslot"):
                self.t = [T(es, "%s%d" % (name, i), [128, 16, cols], dt) for i in range(n)]
                self.b = [Buf() for _ in range(n)]
                self.i = 0

            def load(self, src, nk, ncols):
                k = self.i % len(self.t)
                self.i += 1
                t, b = self.t[k], self.b[k]
                pldma(t[:, 0:nk, 0:ncols], src.rearrange("(kc p) n -> p kc n", p=128), [], [b])
                return t, b

        def rstd_from_psum(ps_ap, out_ap, n, psb, outb):
            act(lambda: nc.scalar.activation(out=out_ap, in_=ps_ap, func=AF.Sqrt, scale=1.0 / n, bias=EPS), [psb], [outb])
            dve(lambda: nc.vector.reciprocal(out=out_ap, in_=out_ap), [outb], [outb])

        def phase_mod(l):
            with ExitStack() as es:
                pp = PsumPool(es, 1)
                psm, psmb = pp.get()
                cT = T(es, "cT", [128, 16, 2], F32); cTb = Buf()
                sT = T(es, "sT", [128, 16, 2], F32); sTb = Buf()
                bm = T(es, "bm", [128, 96], F32); bmb = Buf()
                nm = T(es, "nm", [128, 16], F32); nf = T(es, "nf", [128, 16], F32); nb_ = Buf()
                modT = T(es, "modT", [128, 2, 96], F32); modb = Buf()
                wp = WPool(es, 3, dt=F32, cols=256, name="wm")
                spdma(cT[:], cT_in, [], [cTb])
                spdma(bm[:], b_modT[l], [], [bmb])
                spdma(nm[:], nmixT[l], [], [nb_])
                spdma(nf[:], nffnT[l], [], [nb_])
                act(lambda: nc.scalar.activation(out=sT[:], in_=cT[:], func=AF.Silu), [cTb], [sTb])
                for cb in range(48):
                    wt, wb = wp.load(w_mod[l][:, cb * 256:(cb + 1) * 256], 16, 256)
                    for jj in range(2):
                        j = cb * 2 + jj
                        for kc in range(16):
                            mm(psm[:, 2 * j:2 * j + 2], wt[:, kc, jj * 128:(jj + 1) * 128], sT[:, kc, :],
                               kc == 0, kc == 15, [wb, sTb], [psmb])
                pv = psm[:, 0:192].rearrange("p (j g) -> p j g", g=2)
                for g in range(2):
                    dve(lambda g=g: nc.vector.tensor_tensor(out=modT[:, g, :], in0=pv[:, :, g], in1=bm[:], op=ALU.add),
                        [psmb, bmb], [modb])
                for g in range(2):
                    def mg(i, g=g):
                        return modT[:, g, i * 16:(i + 1) * 16]
                    dve(lambda g=g, mg=mg: nc.vector.scalar_tensor_tensor(out=mvec(l, g, 0), in0=mg(1), scalar=1.0, in1=nm[:], op0=ALU.add, op1=ALU.mult), [modb, nb_], [MVb])
                    dve(lambda g=g, mg=mg: nc.vector.tensor_copy(out=mvec(l, g, 1), in_=mg(0)), [modb], [MVb])
                    dve(lambda g=g, mg=mg: nc.vector.tensor_copy(out=mvec(l, g, 2), in_=mg(2)), [modb], [MVb])
                    dve(lambda g=g, mg=mg: nc.vector.scalar_tensor_tensor(out=mvec(l, g, 3), in0=mg(4), scalar=1.0, in1=nf[:], op0=ALU.add, op1=ALU.mult), [modb, nb_], [MVb])
                    dve(lambda g=g, mg=mg: nc.vector.tensor_copy(out=mvec(l, g, 4), in_=mg(3)), [modb], [MVb])
                    dve(lambda g=g, mg=mg: nc.vector.tensor_copy(out=mvec(l, g, 5), in_=mg(5)), [modb], [MVb])
                S.flush()

        def norm_mod(pp, xT, xb, hT, hb, a_ap, b_ap, sq, sqb, rs, rsb, tmpf, tmpb):
            ps, psb = pp.get()
            for kc in range(16):
                k2 = kc % 2
                act(lambda kc=kc, k2=k2: nc.scalar.activation(out=sq[k2][:], in_=xT[:, kc, :], func=AF.Square), [xb], [sqb[k2]])
                mm(ps[:, :], ones_bf[:, :], sq[k2][:], kc == 0, kc == 15, [sqb[k2], constb], [psb])
            rstd_from_psum(ps[:, :], rs[:], D, psb, rsb)
            for kc in range(16):
                k2 = kc % 2
                dve(lambda kc=kc, k2=k2: nc.vector.scalar_tensor_tensor(out=tmpf[k2][:], in0=xT[:, kc, :], scalar=a_ap[:, kc:kc + 1], in1=rs[:], op0=ALU.mult, op1=ALU.mult), [xb, rsb, MVb], [tmpb[k2]])
                act(lambda kc=kc, k2=k2: nc.scalar.activation(out=hT[:, kc, :], in_=tmpf[k2][:], func=AF.Identity, bias=b_ap[:, kc:kc + 1], scale=1.0), [tmpb[k2], MVb], [hb])

        def phase1(l):
            XIN = xT_in if l == 0 else XT1
            with ExitStack() as es:
                pp = PsumPool(es, 8)
                wp = WPool(es, 3)
                U = T(es, "U", [128, 8192], F32); Ub = Buf()
                xT = U[:, :].rearrange("p (k t) -> p k t", t=512)
                zq = U[:, 0:3072].rearrange("p (k t) -> p k t", t=512); zqb = Buf()
                u_t = U[:, 3072:5120].rearrange("p (k t) -> p k t", t=512); ub = Buf()
                cg = U[:, 5120:7168].rearrange("p (k t) -> p k t", t=512); cgb = Buf()
                zkv = U[:, 7168:8192].rearrange("p (k t) -> p k t", t=512); zkvb = Buf()
                hT = T(es, "hT", [128, 16, 512], BF16); hb = Buf()
                wuq = T(es, "wuq", [128, 6, 1536], BF16)
                wukv = T(es, "wukv", [128, 2, 2048], BF16)
                wspT = T(es, "wspT", [128, 4, 128], BF16)
                qaw = T(es, "qaw", [128, 6], F32); kvaw = T(es, "kvaw", [128, 2], F32)
                wq96 = T(es, "wq96", [96, 1], F32); wk96 = T(es, "wk96", [96, 1], F32)
                sguw = T(es, "sguw", [128, 4], F32)
                bspb = T(es, "bspb", [128, 512], F32)
                kvawb = T(es, "kvawb", [128, 256], F32)
                wres = Buf()
                pldma(wuq[:], w_uq[l].rearrange("(kc p) n -> p kc n", p=128), [], [wres])
                pldma(wukv[:], w_ukv[l].rearrange("(kc p) n -> p kc n", p=128), [], [wres])
                pldma(wspT[:], w_spT[l].rearrange("g p q -> p g q"), [], [wres])
                spdma(qaw[:], qawT[l], [], [wres]); spdma(kvaw[:], kvawT[l], [], [wres])
                spdma(wq96[:], qnw[l], [], [wres]); spdma(wk96[:], knw[l], [], [wres])
                spdma(sguw[:], sguwT[l], [], [wres])
                spdma(bspb[:], b_sp[l].partition_broadcast(128), [], [wres])
                spdma(kvawb[:], kvaw_row[l].partition_broadcast(128), [], [wres])

                sq = [T(es, "sq%d" % i, [128, 512], BF16) for i in range(2)]; sqb = [Buf(), Buf()]
                rs = T(es, "rs", [128, 512], F32); rsb = Buf()
                tmpf = [T(es, "tmpf%d" % i, [128, 512], F32) for i in range(2)]; tmpb = [Buf(), Buf()]
                cqT = T(es, "cqT", [128, 6, 512], BF16); cqb = Buf()
                ckvT = T(es, "ckvT", [128, 2, 512], BF16); ckvb = Buf()
                cosT = T(es, "cosT", [96, 512], F32); sinT = T(es, "sinT", [96, 512], F32); csb = Buf()
                kpw = T(es, "kpw", [96, 512], BF16); kpwb = Buf()
                krope = T(es, "krope", [96, 512], BF16); kropeb = Buf()
                t1 = [T(es, "t1%d" % i, [96, 512], F32) for i in range(2)]; t2 = [T(es, "t2%d" % i, [96, 512], F32) for i in range(2)]; t1b = [Buf(), Buf()]; t2b = [Buf(), Buf()]
                sq96 = [T(es, "sq96%d" % i, [96, 512], BF16) for i in range(2)]; sq96b = [Buf(), Buf()]
                qw = [T(es, "qw%d" % i, [96, 512], BF16) for i in range(2)]; qwb = [Buf(), Buf()]
                qn = [T(es, "qn%d" % i, [96, 512], BF16) for i in range(2)]; qnb = [Buf(), Buf()]
                r96 = [T(es, "r96%d" % i, [96, 512], F32) for i in range(2)]; r96b = [Buf(), Buf()]
                Qh = [T(es, "Qh%d" % i, [96, 512], BF16) for i in range(2)]; Qhb = [Buf(), Buf()]
                Kh = [T(es, "Kh%d" % i, [64, 512], BF16) for i in range(2)]; Khb = [Buf(), Buf()]
                Vp = [T(es, "Vp%d" % i, [128, 16, 65], BF16) for i in range(2)]; Vpb = [Buf(), Buf()]
                sqk = T(es, "sqk", [128, 4, 64], F32); sqkb = Buf()
                ssk = T(es, "ssk", [128, 16], F32); sskb = Buf()
                sck = [T(es, "sck%d" % i, [128, 16], F32) for i in range(2)]; sckb = [Buf(), Buf()]
                sspe = T(es, "sspe", [128, 4], F32); sspeb = Buf()
                tm32 = T(es, "tm32", [128, 32], F32); tm32b = Buf()
                tm256 = T(es, "tm256", [128, 256], F32); tm256b = Buf()
                ss1 = T(es, "ss1", [128, 1], F32); ss1b = Buf()
                stc = [T(es, "stc%d" % i, [128, 256], F32) for i in range(2)]; stcb = [Buf(), Buf()]
                stk = [T(es, "stk%d" % i, [128, 32], F32) for i in range(2)]; stkb = [Buf(), Buf()]
                ptmp = [T(es, "ptmp%d" % i, [128, 512], F32) for i in range(2)]; ptmpb = [Buf(), Buf()]
                vg = T(es, "vg", [128, 512], F32); vgb = Buf()
                vh = T(es, "vh", [128, 512], BF16); vhb = Buf()
                mx = T(es, "mx", [128, 128], F32); mxb = Buf()
                sgo = T(es, "sgo", [128, 4, 512], BF16); sgob = Buf()
                cckT = T(es, "cckT", [128, 2, 256], BF16); cckb = Buf()
                ckp96 = T(es, "ckp96", [96, 256], F32); ckp96b = Buf()
                ckpt = T(es, "ckpt", [128, 2, 32], F32); ckptb = Buf()

                cut = (dbg or {}).get("p1_cut", None)

                def stage(k):
                    if cut == k:
                        raise _Stop()
                dve(lambda: nc.vector.memset(kpw[:], 0.0), [], [kpwb])
                dve(lambda: nc.vector.memset(krope[:], 0.0), [], [kropeb])
                for i in range(2):
                    dve(lambda i=i: nc.vector.memset(Vp[i][:], 1.0), [], [Vpb[i]])
                cnt = {"q": 0, "k": 0, "v": 0, "p": 0, "s": 0}

                def gen_kv(ckv_ap, kr_t, kr_b, T_, key0, sspe_ap):
                    nsub = T_ // 128
                    import os
                    GK = os.environ.get("GK", "kvs")
                    for h in (range(H) if "k" in GK else []):
                        ps, psb = pp.get()
                        for kc in range(2):
                            mm(ps[0:64, 0:T_], wukv[:, kc, h * 128:h * 128 + 64], ckv_ap[:, kc, :], kc == 0, kc == 1, [wres, ckvb, cckb], [psb])
                        kb_ = cnt["k"] % 2; cnt["k"] += 1
                        act(lambda ps=ps, kb_=kb_: nc.scalar.activation(out=Kh[kb_][:, 0:T_], in_=ps[0:64, 0:T_], func=AF.Copy, scale=wk96[0:64, 0:1]), [psb, wres], [Khb[kb_]])
                        spdma(Ksc[0:64, h, key0:key0 + T_], Kh[kb_][:, 0:T_], [Khb[kb_]], [])
                        spdma(Ksc[64:96, h, key0:key0 + T_], kr_t[64:96, 0:T_], [kr_b], [])
                    for s in (range(nsub) if "v" in GK else []):
                        vb_ = cnt["v"] % 2; cnt["v"] += 1
                        for c in range(4):
                            ps, psb = pp.get()
                            for kc in range(2):
                                mm(ps[:, :], ckv_ap[:, kc, s * 128:(s + 1) * 128], wukv[:, kc, c * 512:(c + 1) * 512], kc == 0, kc == 1, [wres, ckvb, cckb], [psb])
                            pv = ps[:, :].rearrange("p (h d) -> p h d", d=128)
                            dve(lambda pv=pv, c=c, vb_=vb_: nc.vector.tensor_copy(out=Vp[vb_][:, 4 * c:4 * c + 4, 0:64], in_=pv[:, :, 64:128]), [psb], [Vpb[vb_]])
                            if "s" in GK:
                                act(lambda pv=pv: nc.scalar.activation(out=sqk[:], in_=pv[:, :, 0:64], func=AF.Square), [psb], [sqkb])
                                dve(lambda c=c: nc.vector.reduce_sum(out=ssk[:, 4 * c:4 * c + 4], in_=sqk[:], axis=AX.X), [sqkb], [sskb])
                        sb_ = cnt["s"] % 2; cnt["s"] += 1
                        if "s" not in GK:
                            kb = key0 // 128 + s
                            spdma(Vsc[kb], Vp[vb_][:].rearrange("p h c -> p (h c)"), [Vpb[vb_]], [])
                            continue
                        dve(lambda s=s: nc.vector.tensor_scalar(out=ssk[:], in0=ssk[:], scalar1=sspe_ap[:, s:s + 1], scalar2=None, op0=ALU.add), [sskb, sspeb], [sskb])
                        act(lambda: nc.scalar.activation(out=ssk[:], in_=ssk[:], func=AF.Sqrt, scale=1.0 / 96, bias=EPS), [sskb], [sskb])
                        dve(lambda: nc.vector.reciprocal(out=ssk[:], in_=ssk[:]), [sskb], [sskb])
                        dve(lambda sb_=sb_: nc.vector.tensor_scalar(out=sck[sb_][:], in0=ssk[:], scalar1=ATTN_SCALE, scalar2=None, op0=ALU.mult), [sskb], [sckb[sb_]])
                        kb = key0 // 128 + s
                        spdma(SCsc[kb], sck[sb_][:], [sckb[sb_]], [])
                        spdma(Vsc[kb], Vp[vb_][:].rearrange("p h c -> p (h c)"), [Vpb[vb_]], [])

                if cut != 0:
                    pldma(cckT[:], cckvT_in[l].rearrange("(kc p) t -> p kc t", p=128), [], [cckb])
                    spdma(ckp96[64:96, :], ckpeT_in[l], [], [ckp96b])
                    spdma(ckpt[:], ckpe_in[l].rearrange("(s p) d -> p s d", p=128), [], [ckptb])
                    import os
                    if "act" in os.environ.get("P1A", "act,sq"):
                        act(lambda: nc.scalar.activation(out=krope[64:96, 0:256], in_=ckp96[64:96, :], func=AF.Copy, scale=wk96[64:96, 0:1]), [ckp96b, wres, kropeb], [kropeb])
                    for s in (range(2) if "sq" in os.environ.get("P1A", "act,sq") else []):
                        act(lambda s=s: nc.scalar.activation(out=tm32[:], in_=ckpt[:, s, :], func=AF.Square), [ckptb], [tm32b])
                        dve(lambda s=s: nc.vector.reduce_sum(out=sspe[:, s:s + 1], in_=tm32[:], axis=AX.X), [tm32b], [sspeb])
                try:
                  stage(0)
                  stage(1)
                  gen_kv(cckT, krope, kropeb, 256, 4096, sspe)
                  stage(2)
                  for i in range(NT):
                    g = 0 if i < 8 else 1
                    tok0 = i * TT
                    key0 = tok0 if i < 8 else 4352
                    a1 = mvec(l, g, 0); b1 = mvec(l, g, 1)
                    spdma(xT, XIN[:, tok0:tok0 + TT].rearrange("(kc p) t -> p kc t", p=128), [], [Ub, zqb, ub, cgb, zkvb])
                    spdma(cosT[:], cos_in[:, tok0:tok0 + TT], [], [csb])
                    spdma(sinT[:], sin_in[:, tok0:tok0 + TT], [], [csb])
                    norm_mod(pp, xT, Ub, hT, hb, a1, b1, sq, sqb, rs, rsb, tmpf, tmpb)

                    stage(3)
                    def zgroup(wt, wb, c0, m, ps, psb, n=TT):
                        for kc in range(16):
                            mm(ps[0:m, 0:n], wt[:, kc, c0:c0 + m], hT[:, kc, :], kc == 0, kc == 15, [wb, hb], [psb])

                    wA, wAb = wp.load(w_in[l][:, 0:512], 16, 512)
                    wB, wBb = wp.load(w_in[l][:, 512:768], 16, 256)
                    pss, pssb = pp.get()
                    for j in range(6):
                        wt, wb, c0 = (wA, wAb, j * 128) if j < 4 else (wB, wBb, (j - 4) * 128)
                        ps, psb = pp.get()
                        zgroup(wt, wb, c0, 128, ps, psb)
                        k2 = j % 2
                        act(lambda ps=ps, k2=k2: nc.scalar.activation(out=sq[k2][:], in_=ps[:, :], func=AF.Square), [psb], [sqb[k2]])
                        dve(lambda ps=ps, j=j: nc.vector.tensor_copy(out=zq[:, j, :], in_=ps[:, :]), [psb], [zqb])
                        mm(pss[:, :], ones_bf[:, :], sq[k2][:], j == 0, j == 5, [sqb[k2], constb], [pssb])
                    rstd_from_psum(pss[:, :], rs[:], 768, pssb, rsb)
                    for j in range(6):
                        dve(lambda j=j: nc.vector.scalar_tensor_tensor(out=cqT[:, j, :], in0=zq[:, j, :], scalar=qaw[:, j:j + 1], in1=rs[:], op0=ALU.mult, op1=ALU.mult), [zqb, rsb, wres], [cqb])
                    stage(4)
                    wC, wCb = wp.load(w_in[l][:, 768:1056], 16, 288)
                    pss, pssb = pp.get()
                    for j in range(2):
                        ps, psb = pp.get()
                        zgroup(wC, wCb, j * 128, 128, ps, psb)
                        k2 = j % 2
                        act(lambda ps=ps, k2=k2: nc.scalar.activation(out=sq[k2][:], in_=ps[:, :], func=AF.Square), [psb], [sqb[k2]])
                        dve(lambda ps=ps, j=j: nc.vector.tensor_copy(out=zkv[:, j, :], in_=ps[:, :]), [psb], [zkvb])
                        mm(pss[:, :], ones_bf[:, :], sq[k2][:], j == 0, j == 1, [sqb[k2], constb], [pssb])
                    rstd_from_psum(pss[:, :], rs[:], 256, pssb, rsb)
                    for j in range(2):
                        dve(lambda j=j: nc.vector.scalar_tensor_tensor(out=ckvT[:, j, :], in0=zkv[:, j, :], scalar=kvaw[:, j:j + 1], in1=rs[:], op0=ALU.mult, op1=ALU.mult), [zkvb, rsb, wres], [ckvb])
                    ps, psb = pp.get()
                    zgroup(wC, wCb, 192, 96, ps, psb)
                    dve(lambda ps=ps: nc.vector.tensor_scalar(out=kpw[64:96, :], in0=ps[64:96, :], scalar1=wk96[64:96, 0:1], scalar2=None, op0=ALU.mult), [psb, wres], [kpwb])
                    ps2, ps2b = pp.get()
                    mm(ps2[0:96, :], prot_bf[:, :], kpw[:, :], True, True, [kpwb, constb], [ps2b])
                    dve(lambda: nc.vector.tensor_tensor(out=t1[0][64:96, :], in0=kpw[64:96, :], in1=cosT[64:96, :], op=ALU.mult), [kpwb, csb], [t1b[0]])
                    dve(lambda ps2=ps2: nc.vector.tensor_tensor(out=t2[0][64:96, :], in0=ps2[64:96, :], in1=sinT[64:96, :], op=ALU.mult), [ps2b, csb], [t2b[0]])
                    dve(lambda: nc.vector.tensor_tensor(out=krope[64:96, :], in0=t1[0][64:96, :], in1=t2[0][64:96, :], op=ALU.add), [t1b[0], t2b[0]], [kropeb])
                    stage(5)
                    for s in range(4):
                        ps, psb = pp.get()
                        for kc in range(16):
                            mm(ps[:, 0:288], hT[:, kc, s * 128:(s + 1) * 128], wC[:, kc, 0:288], kc == 0, kc == 15, [wCb, hb], [psb])
                        act(lambda ps=ps: nc.scalar.activation(out=tm32[:], in_=ps[:, 256:288], func=AF.Square), [psb], [tm32b])
                        dve(lambda s=s: nc.vector.reduce_sum(out=sspe[:, s:s + 1], in_=tm32[:], axis=AX.X), [tm32b], [sspeb])
                        if i == 8:
                            p2 = cnt["p"] % 2; cnt["p"] += 1
                            seq, pos0 = s // 2, (s % 2) * 128
                            act(lambda ps=ps: nc.scalar.activation(out=tm256[:], in_=ps[:, 0:256], func=AF.Square), [psb], [tm256b])
                            dve(lambda: nc.vector.reduce_sum(out=ss1[:], in_=tm256[:], axis=AX.X), [tm256b], [ss1b])
                            act(lambda: nc.scalar.activation(out=ss1[:], in_=ss1[:], func=AF.Sqrt, scale=1.0 / 256, bias=EPS), [ss1b], [ss1b])
                            dve(lambda: nc.vector.reciprocal(out=ss1[:], in_=ss1[:]), [ss1b], [ss1b])
                            dve(lambda ps=ps, p2=p2: nc.vector.scalar_tensor_tensor(out=stc[p2][:], in0=ps[:, 0:256], scalar=ss1[:, 0:1], in1=kvawb[:], op0=ALU.mult, op1=ALU.mult), [psb, ss1b, wres], [stcb[p2]])
                            dve(lambda ps=ps, p2=p2: nc.vector.tensor_copy(out=stk[p2][:], in_=ps[:, 256:288]), [psb], [stkb[p2]])
                            spdma(st_ckv[seq, l, pos0:pos0 + 128, :], stc[p2][:], [stcb[p2]], [])
                            spdma(st_kpe[seq, l, pos0:pos0 + 128, :], stk[p2][:], [stkb[p2]], [])
                    stage(6)
                    qst = {}

                    def qA(h):
                        q2 = h % 2
                        ps, psb = pp.get()
                        for kc in range(6):
                            mm(ps[0:96, :], wuq[:, kc, h * 96:(h + 1) * 96], cqT[:, kc, :], kc == 0, kc == 5, [wres, cqb], [psb])
                        act(lambda: nc.scalar.activation(out=sq96[q2][:], in_=ps[0:96, :], func=AF.Square), [psb], [sq96b[q2]])
                        dve(lambda: nc.vector.tensor_scalar(out=qw[q2][:], in0=ps[0:96, :], scalar1=wq96[:, 0:1], scalar2=None, op0=ALU.mult), [psb, wres], [qwb[q2]])

                    def qB(h):
                        q2 = h % 2
                        pq, pqb = pp.get()
                        mm(pq[0:96, :], ones_bf[0:96, 0:96], sq96[q2][:], True, True, [sq96b[q2], constb], [pqb])
                        rstd_from_psum(pq[0:96, :], r96[q2][:], 96, pqb, r96b[q2])
                        dve(lambda: nc.vector.tensor_tensor(out=qn[q2][:], in0=qw[q2][:], in1=r96[q2][:], op=ALU.mult), [qwb[q2], r96b[q2]], [qnb[q2]])

                    def qC(h, tok0=tok0):
                        q2 = h % 2
                        pr, prb = pp.get()
                        mm(pr[0:96, :], prot_bf[:, :], qn[q2][:], True, True, [qnb[q2], constb], [prb])
                        dve(lambda: nc.vector.tensor_tensor(out=t1[q2][:], in0=qn[q2][:], in1=cosT[:], op=ALU.mult), [qnb[q2], csb], [t1b[q2]])
                        dve(lambda: nc.vector.tensor_tensor(out=t2[q2][:], in0=pr[0:96, :], in1=sinT[:], op=ALU.mult), [prb, csb], [t2b[q2]])
                        dve(lambda: nc.vector.tensor_tensor(out=Qh[q2][:], in0=t1[q2][:], in1=t2[q2][:], op=ALU.add), [t1b[q2], t2b[q2]], [Qhb[q2]])
                        spdma(Qsc[:, h, tok0:tok0 + TT], Qh[q2][:], [Qhb[q2]], [])
                    for st in range(H + 2):
                        if st < H:
                            qA(st)
                        if 0 <= st - 1 < H:
                            qB(st - 1)
                        if 0 <= st - 2 < H:
                            qC(st - 2)
                    stage(7)
                    gen_kv(ckvT, krope, kropeb, TT, key0, sspe)
                    stage(8)
                    wD, wDb = wp.load(w_in[l][:, OFF_POOL:OFF_POOL + 512], 16, 512)
                    for j in range(4):
                        ps, psb = pp.get()
                        zgroup(wD, wDb, j * 128, 128, ps, psb)
                        p2 = j % 2
                        act(lambda ps=ps, p2=p2: nc.scalar.copy(out=ptmp[p2][:], in_=ps[:, :]), [psb], [ptmpb[p2]])
                        spdma(POOLsc[j * 128:(j + 1) * 128, tok0:tok0 + TT], ptmp[p2][:], [ptmpb[p2]], [])
                    stage(9)
                    wE, wEb = wp.load(w_in[l][:, OFF_SGU:OFF_SGU + 512], 16, 512)
                    for j in range(4):
                        ps, psb = pp.get()
                        zgroup(wE, wEb, j * 128, 128, ps, psb)
                        act(lambda ps=ps, j=j: nc.scalar.activation(out=u_t[:, j, :], in_=ps[:, :], func=AF.Gelu_apprx_tanh), [psb], [ub])
                    wF, wFb = wp.load(w_in[l][:, OFF_SGU + 512:OFF_SGU + 1024], 16, 512)
                    for s in range(4):
                        ps, psb = pp.get()
                        for kc in range(16):
                            mm(ps[:, :], hT[:, kc, s * 128:(s + 1) * 128], wF[:, kc, :], kc == 0, kc == 15, [wFb, hb], [psb])
                        act(lambda ps=ps: nc.scalar.activation(out=vg[:], in_=ps[:, :], func=AF.Gelu_apprx_tanh), [psb], [vgb])
                        dve(lambda: nc.vector.tensor_tensor(out=tmpf[0][:], in0=vg[:], in1=vg[:], op=ALU.mult), [vgb], [tmpb[0]])
                        dve(lambda: nc.vector.reduce_sum(out=ss1[:], in_=tmpf[0][:], axis=AX.X), [tmpb[0]], [ss1b])
                        act(lambda: nc.scalar.activation(out=ss1[:], in_=ss1[:], func=AF.Sqrt, scale=1.0 / 512, bias=EPS), [ss1b], [ss1b])
                        dve(lambda: nc.vector.reciprocal(out=ss1[:], in_=ss1[:]), [ss1b], [ss1b])
                        dve(lambda: nc.vector.tensor_scalar(out=vh[:], in0=vg[:], scalar1=ss1[:, 0:1], scalar2=None, op0=ALU.mult), [vgb, ss1b], [vhb])
                        pm, pmb = pp.get()
                        for gg in range(4):
                            mm(pm[:, gg * 128:(gg + 1) * 128], vh[:, gg * 128:(gg + 1) * 128], wspT[:, gg, :], True, True, [vhb, wres], [pmb])
                        for gg in range(4):
                            dve(lambda pm=pm, gg=gg: nc.vector.scalar_tensor_tensor(out=mx[:], in0=pm[:, gg * 128:(gg + 1) * 128], scalar=sguw[:, gg:gg + 1], in1=bspb[:, gg * 128:(gg + 1) * 128], op0=ALU.mult, op1=ALU.add), [pmb, wres], [mxb])
                            dve(lambda gg=gg, s=s: nc.vector.tensor_tensor(out=sgo[:, gg, s * 128:(s + 1) * 128], in0=mx[:], in1=u_t[:, gg, s * 128:(s + 1) * 128], op=ALU.mult), [mxb, ub], [sgob])
                    spdma(SGUsc[:, tok0:tok0 + TT].rearrange("(g p) t -> p g t", p=128), sgo[:], [sgob], [])
                    stage(10)
                    wG, wGb = wp.load(w_in[l][:, OFF_CONV:OFF_CONV + 512], 16, 512)
                    for j in range(4):
                        ps, psb = pp.get()
                        zgroup(wG, wGb, j * 128, 128, ps, psb)
                        p2 = j % 2
                        act(lambda ps=ps, p2=p2: nc.scalar.copy(out=ptmp[p2][:], in_=ps[:, :]), [psb], [ptmpb[p2]])
                        spdma(CONVb[j * 128:(j + 1) * 128, tok0:tok0 + TT], ptmp[p2][:], [ptmpb[p2]], [])
                    wH, wHb = wp.load(w_in[l][:, OFF_CONV + 512:OFF_CONV + 1024], 16, 512)
                    for j in range(4):
                        ps, psb = pp.get()
                        zgroup(wH, wHb, j * 128, 128, ps, psb)
                        act(lambda ps=ps, j=j: nc.scalar.copy(out=cg[:, j, :], in_=ps[:, :]), [psb], [cgb])
                    wI, wIb = wp.load(w_in[l][:, OFF_CONV + 1024:OFF_CONV + 1536], 16, 512)
                    for j in range(4):
                        ps, psb = pp.get()
                        zgroup(wI, wIb, j * 128, 128, ps, psb)
                        p2 = j % 2
                        dve(lambda ps=ps, p2=p2, j=j: nc.vector.tensor_tensor(out=ptmp[p2][:], in0=ps[:, :], in1=cg[:, j, :], op=ALU.mult), [psb, cgb], [ptmpb[p2]])
                        spdma(CONVy[j * 128:(j + 1) * 128, tok0:tok0 + TT], ptmp[p2][:], [ptmpb[p2]], [])
                    stage(11)
                except _Stop:
                  pass
                S.flush()

        brKk = [8, 4, 4, 4]

        def phase2(l):
            oc = 0
            with ExitStack() as es:
                ppS = PsumPool(es, 5)
                ppO = PsumPool(es, 2)
                ppD = PsumPool(es, 1)
                Vall = T(es, "Vall", [128, NKB, H * 65], BF16); Vb = Buf()
                SCall = T(es, "SCall", [128, NKB, H], F32); SCb = Buf()
                Kt = [T(es, "Kt%d" % i, [96, NKEY], BF16) for i in range(2)]; Ktb = [Buf(), Buf()]
                Qt = [T(es, "Qt%d" % i, [96, NTOK], BF16) for i in range(2)]; Qtb = [Buf(), Buf()]
                pT = [T(es, "pT%d" % i, [128, 512], BF16) for i in range(4)]; pTb = [Buf() for _ in range(4)]
                osb = [T(es, "osb%d" % i, [65, 512], F32) for i in range(2)]; osbb = [Buf(), Buf()]
                rden = T(es, "rden", [64, 512], F32); rdenb = Buf()
                on = [T(es, "on%d" % i, [64, 512], BF16) for i in range(2)]; onb = [Buf(), Buf()]
                for c in range(2):
                    spdma(Vall[:, c * 19:(c + 1) * 19, :], Vsc[c * 19:(c + 1) * 19].rearrange("kb p c -> p kb c"), [], [Vb])
                spdma(SCall[:], SCsc.rearrange("kb p h -> p kb h"), [], [SCb])
                for r0 in range(0, D, 128):
                    pldma(WG[r0:r0 + 128, :], w_in[l][r0:r0 + 128, OFF_GATE:OFF_GATE + 4 * D], [], [])
                for br in range(4):
                    for r0 in range(0, brKk[br] * 128, 128):
                        pldma(WBR[BR_OFF[br] + r0:BR_OFF[br] + r0 + 128, :], w_br[br][l][r0:r0 + 128, :], [], [])
                for r0 in range(0, D, 128):
                    pldma(WO[r0:r0 + 128, :], w_out[l][r0:r0 + 128, :], [], [])
                for r0 in range(0, D, 128):
                    pldma(WFI[r0:r0 + 128, :], w_ffn_in[l][r0:r0 + 128, :], [], [])
                for r0 in range(0, DFF, 128):
                    pldma(WFO[r0:r0 + 128, :], w_ffn_out[l][r0:r0 + 128, :], [], [])
                jobs = [(qt * 512, 512, list(range(34))) for qt in range(8)]
                jobs += [(4096 + 256 * s, 256, [34 + 2 * s, 35 + 2 * s]) for s in range(2)]
                pc = 0
                LA = 4
                pending = [None]

                def run_pending():
                    if pending[0] is not None:
                        pending[0]()
                        pending[0] = None
                spdma(Kt[0][:], Ksc[:, 0, :], [], [Ktb[0]])
                spdma(Qt[0][:], Qsc[:, 0, :], [], [Qtb[0]])
                for h in range(H):
                    hb2 = h % 2
                    for ji, (q0, nq, kbs) in enumerate(jobs):
                        if ji == 1 and h + 1 < H:
                            spdma(Kt[1 - hb2][:], Ksc[:, h + 1, :], [], [Ktb[1 - hb2]])
                            spdma(Qt[1 - hb2][:], Qsc[:, h + 1, :], [], [Qtb[1 - hb2]])
                        po, pob = ppO.get()
                        sps = {}

                        def smm(idx, kbs=kbs, q0=q0, nq=nq, hb2=hb2, sps=sps):
                            ps, psb = ppS.get()
                            kb = kbs[idx]
                            mm(ps[:, 0:nq], Kt[hb2][:, kb * 128:(kb + 1) * 128], Qt[hb2][:, q0:q0 + nq], True, True, [Ktb[hb2], Qtb[hb2]], [psb])
                            sps[idx] = (ps, psb)
                        for k0 in range(min(LA, len(kbs))):
                            smm(k0)
                        run_pending()
                        for idx, kb in enumerate(kbs):
                            ps, psb = sps.pop(idx)
                            r = pc % 4; pc += 1
                            act(lambda ps=ps, r=r, kb=kb, h=h, nq=nq: nc.scalar.activation(out=pT[r][:, 0:nq], in_=ps[:, 0:nq], func=AF.Exp, scale=SCall[:, kb, h:h + 1]), [psb, SCb], [pTb[r]])
                            mm(po[0:65, 0:nq], Vall[:, kb, h * 65:(h + 1) * 65], pT[r][:, 0:nq], idx == 0, idx == len(kbs) - 1, [Vb, pTb[r]], [pob])
                            if idx + LA < len(kbs):
                                smm(idx + LA)

                        def norm(po=po, pob=pob, nq=nq, q0=q0, h=h):
                            nonlocal oc
                            o2 = oc % 2; oc += 1
                            dve(lambda: nc.vector.tensor_copy(out=osb[o2][:, 0:nq], in_=po[0:65, 0:nq]), [pob], [osbb[o2]])
                            pd, pdb = ppD.get()
                            mm(pd[0:64, 0:nq], sel_f[:, :], osb[o2][:, 0:nq], True, True, [osbb[o2], constb], [pdb])
                            dve(lambda: nc.vector.reciprocal(out=rden[:, 0:nq], in_=pd[0:64, 0:nq]), [pdb], [rdenb])
                            dve(lambda: nc.vector.tensor_tensor(out=on[o2][:, 0:nq], in0=osb[o2][0:64, 0:nq], in1=rden[:, 0:nq], op=ALU.mult), [osbb[o2], rdenb], [onb[o2]])
                            spdma(ATTNsc[h * 64:(h + 1) * 64, q0:q0 + nq], on[o2][:, 0:nq], [onb[o2]], [])
                        pending[0] = norm
                run_pending()
                S.flush()

        def phase3(l):
            XIN = xT_in if l == 0 else XT1
            XOUT = XT1 if l == 0 else yT
            with ExitStack() as es:
                pp = PsumPool(es, 8)
                wp = WPool(es, 3)
                xT = T(es, "xT3", [128, 16, 512], F32); xb = Buf()
                hT = T(es, "hT3", [128, 16, 512], BF16); hb = Buf()
                sq = [T(es, "sq3%d" % i, [128, 512], BF16) for i in range(2)]; sqb = [Buf(), Buf()]
                rs = T(es, "rs3", [128, 512], F32); rsb = Buf()
                tmpf = [T(es, "tmpf3%d" % i, [128, 512], F32) for i in range(2)]; tmpb = [Buf(), Buf()]
                R = T(es, "R3", [128, 22800], BF16)
                actT = R[:, 0:22528].rearrange("p (k t) -> p k t", t=512); actb = Buf()
                Rf = R[:, :].bitcast(F32)
                zp = Rf[:, 0:2112].rearrange("p (g t) -> p g t", t=528); zpb = Buf()
                pa = [Rf[:, 2112 + k * 528:2112 + (k + 1) * 528] for k in range(4)]; pab = Buf()
                cy = Rf[:, 4224:6280].rearrange("p (g t) -> p g t", t=514); cyb = Buf()
                cb_ = Rf[:, 6280:8328].rearrange("p (g t) -> p g t", t=512); cbb = Buf()
                ctmp = Rf[:, 8328:8840]; ctb = Buf()
                ptm = Rf[:, 8840:9352]; ptmb = Buf()
                invt = Rf[:, 9352:11400].rearrange("p (g t) -> p g t", t=512); invb = Buf()
                brT_attn = T(es, "brA", [128, 8, 512], BF16)
                brT_pool = T(es, "brP", [128, 4, 512], BF16)
                brT_sgu = T(es, "brS", [128, 4, 512], BF16)
                brT_conv = T(es, "brC", [128, 4, 512], BF16)
                brb = [Buf(), Buf(), Buf(), Buf()]
                pl = T(es, "pl", [128, 512], BF16); plb = Buf()
                mT = T(es, "mT", [128, 16, 512], BF16); mb_ = Buf()
                macc = T(es, "macc", [128, 512], F32); maccb = Buf()
                maccs = [T(es, "maccs%d" % i, [128, 512], F32) for i in range(4)]; maccsb = [Buf() for _ in range(4)]
                gt = [T(es, "gt%d" % i, [128, 512], F32) for i in range(2)]; gtb = [Buf(), Buf()]
                wpl = T(es, "wpl", [128, 4, 128], BF16)
                psc = T(es, "psc", [128, 4], F32); cw = T(es, "cw", [128, 12], F32); bg = T(es, "bg", [128, 64], F32)
                wres = Buf()
                pldma(wpl[:], w_pool[l].rearrange("g i o -> i g o"), [], [wres])
                spdma(psc[:], pscaleT[l], [], [wres]); spdma(cw[:], convwT[l], [], [wres]); spdma(bg[:], b_gateT[l], [], [wres])
                brT = [brT_attn, brT_pool, brT_sgu, brT_conv]
                brK = [8, 4, 4, 4]
                gc = 0
                for i in range(NT):
                    g = 0 if i < 8 else 1
                    tok0 = i * TT
                    a1 = mvec(l, g, 0); b1 = mvec(l, g, 1); ga = mvec(l, g, 2)
                    a2 = mvec(l, g, 3); b2 = mvec(l, g, 4); gf = mvec(l, g, 5)
                    spdma(xT[:], XIN[:, tok0:tok0 + TT].rearrange("(kc p) t -> p kc t", p=128), [], [xb])
                    spdma(brT_attn[:], ATTNsc[:, tok0:tok0 + TT].rearrange("(kc p) t -> p kc t", p=128), [], [brb[0]])
                    spdma(brT_sgu[:], SGUsc[:, tok0:tok0 + TT].rearrange("(kc p) t -> p kc t", p=128), [], [brb[2]])
                    norm_mod(pp, xT, xb, hT, hb, a1, b1, sq, sqb, rs, rsb, tmpf, tmpb)
                    segs = [(tok0, 512, i > 0, i < 7)] if i < 8 else [(tok0, 256, False, False), (tok0 + 256, 256, False, False)]
                    kind = (0 if i == 0 else (2 if i == 7 else 1)) if i < 8 else 3
                    spdma(invt, inv_in[kind], [], [invb, actb])
                    for (c0, W, lh, rh) in segs:
                        so = c0 - tok0
                        dve(lambda: nc.vector.memset(zp, 0.0), [], [zpb, actb])
                        dve(lambda: nc.vector.memset(cy, 0.0), [], [cyb, actb])
                        lo = c0 - (8 if lh else 0); hi = c0 + W + (8 if rh else 0)
                        spdma(zp[:, :, 8 - (c0 - lo):8 + (hi - c0)], POOLsc[:, lo:hi].rearrange("(g p) t -> p g t", p=128), [], [zpb])
                        lo = c0 - (1 if lh else 0); hi = c0 + W + (1 if rh else 0)
                        spdma(cy[:, :, 1 - (c0 - lo):1 + (hi - c0)], CONVy[:, lo:hi].rearrange("(g p) t -> p g t", p=128), [], [cyb])
                        spdma(cb_[:, :, 0:W], CONVb[:, c0:c0 + W].rearrange("(g p) t -> p g t", p=128), [], [cbb, actb])
                        n = W + 16
                        for gg in range(4):
                            X = zp[:, gg, :]
                            dve(lambda X=X, n=n: nc.vector.tensor_tensor(out=pa[0][:, 1:n], in0=X[:, 1:n], in1=X[:, 0:n - 1], op=ALU.add), [zpb], [pab, actb])
                            E = pa[0][:, 8:8 + W]
                            if gg >= 1:
                                dve(lambda n=n: nc.vector.tensor_tensor(out=pa[1][:, 3:n], in0=pa[0][:, 3:n], in1=pa[0][:, 1:n - 2], op=ALU.add), [pab], [pab])
                                E = pa[1][:, 9:9 + W]
                            if gg >= 2:
                                dve(lambda n=n: nc.vector.tensor_tensor(out=pa[2][:, 7:n], in0=pa[1][:, 7:n], in1=pa[1][:, 3:n - 4], op=ALU.add), [pab], [pab])
                                E = pa[2][:, 11:11 + W]
                            if gg >= 3:
                                dve(lambda n=n: nc.vector.tensor_tensor(out=pa[3][:, 15:n], in0=pa[2][:, 15:n], in1=pa[2][:, 7:n - 8], op=ALU.add), [pab], [pab])
                                E = pa[3][:, 15:15 + W]
                            dve(lambda E=E, gg=gg, W=W, so=so: nc.vector.tensor_tensor(out=ptm[:, 0:W], in0=E, in1=invt[:, gg, so:so + W], op=ALU.mult), [pab, invb], [ptmb, actb])
                            dve(lambda X=X, W=W: nc.vector.tensor_tensor(out=pl[:, 0:W], in0=ptm[:, 0:W], in1=X[:, 8:8 + W], op=ALU.subtract), [ptmb, zpb], [plb])
                            ps, psb = pp.get()
                            mm(ps[:, 0:W], wpl[:, gg, :], pl[:, 0:W], True, True, [plb, wres], [psb])
                            act(lambda ps=ps, gg=gg, W=W, so=so: nc.scalar.activation(out=brT_pool[:, gg, so:so + W], in_=ps[:, 0:W], func=AF.Copy, scale=psc[:, gg:gg + 1]), [psb, wres], [brb[1]])
                        for j in range(4):
                            Y = cy[:, j, :]
                            dve(lambda Y=Y, j=j, W=W: nc.vector.tensor_scalar(out=ctmp[:, 0:W], in0=Y[:, 0:W], scalar1=cw[:, j:j + 1], scalar2=None, op0=ALU.mult), [cyb, wres], [ctb, actb])
                            dve(lambda Y=Y, j=j, W=W: nc.vector.scalar_tensor_tensor(out=ctmp[:, 0:W], in0=Y[:, 1:1 + W], scalar=cw[:, 4 + j:5 + j], in1=ctmp[:, 0:W], op0=ALU.mult, op1=ALU.add), [cyb, ctb, wres], [ctb])
                            dve(lambda Y=Y, j=j, W=W: nc.vector.scalar_tensor_tensor(out=ctmp[:, 0:W], in0=Y[:, 2:2 + W], scalar=cw[:, 8 + j:9 + j], in1=ctmp[:, 0:W], op0=ALU.mult, op1=ALU.add), [cyb, ctb, wres], [ctb])
                            dve(lambda j=j, W=W, so=so: nc.vector.tensor_tensor(out=brT_conv[:, j, so:so + W], in0=ctmp[:, 0:W], in1=cb_[:, j, 0:W], op=ALU.mult), [ctb, cbb], [brb[3]])
                    for jb in range(4):
                        for br in range(4):
                            c0 = br * D + jb * 512
                            wg, wgb = wp.load(WG[:, c0:c0 + 512], 16, 512)
                            wb_, wbb = wp.load(WBR[BR_OFF[br]:BR_OFF[br] + brK[br] * 128, jb * 512:(jb + 1) * 512], brK[br], 512)
                            for jj in range(4):
                                j = jb * 4 + jj
                                pg, pgb = pp.get()
                                for kc in range(16):
                                    mm(pg[:, :], wg[:, kc, jj * 128:(jj + 1) * 128], hT[:, kc, :], kc == 0, kc == 15, [wgb, hb], [pgb])
                                pb, pbb = pp.get()
                                for kc in range(brK[br]):
                                    mm(pb[:, :], wb_[:, kc, jj * 128:(jj + 1) * 128], brT[br][:, kc, :], kc == 0, kc == brK[br] - 1, [wbb, brb[br]], [pbb])
                                g2 = gc % 2; gc += 1
                                act(lambda pg=pg, g2=g2, br=br, j=j: nc.scalar.activation(out=gt[g2][:], in_=pg[:, :], func=AF.Sigmoid, bias=bg[:, br * 16 + j:br * 16 + j + 1], scale=1.0), [pgb, wres], [gtb[g2]])
                                if br == 0:
                                    dve(lambda pb=pb, g2=g2, jj=jj: nc.vector.tensor_tensor(out=maccs[jj][:], in0=gt[g2][:], in1=pb[:, :], op=ALU.mult), [gtb[g2], pbb], [maccsb[jj]])
                                else:
                                    dve(lambda pb=pb, g2=g2: nc.vector.tensor_tensor(out=macc[:], in0=gt[g2][:], in1=pb[:, :], op=ALU.mult), [gtb[g2], pbb], [maccb])
                                    if br < 3:
                                        dve(lambda jj=jj: nc.vector.tensor_tensor(out=maccs[jj][:], in0=maccs[jj][:], in1=macc[:], op=ALU.add), [maccb, maccsb[jj]], [maccsb[jj]])
                                    else:
                                        dve(lambda jj=jj, j=j: nc.vector.tensor_tensor(out=mT[:, j, :], in0=maccs[jj][:], in1=macc[:], op=ALU.add), [maccb, maccsb[jj]], [mb_])
                    for jb in range(4):
                        wo, wob = wp.load(WO[:, jb * 512:(jb + 1) * 512], 16, 512)
                        for jj in range(4):
                            j = jb * 4 + jj
                            ps, psb = pp.get()
                            for kc in range(16):
                                mm(ps[:, :], wo[:, kc, jj * 128:(jj + 1) * 128], mT[:, kc, :], kc == 0, kc == 15, [wob, mb_], [psb])
                            dve(lambda ps=ps, j=j, ga=ga: nc.vector.scalar_tensor_tensor(out=xT[:, j, :], in0=ps[:, :], scalar=ga[:, j:j + 1], in1=xT[:, j, :], op0=ALU.mult, op1=ALU.add), [psb, xb, MVb], [xb])
                    norm_mod(pp, xT, xb, hT, hb, a2, b2, sq, sqb, rs, rsb, tmpf, tmpb)
                    for jb in range(11):
                        wa, wab = wp.load(WFI[:, jb * 512:(jb + 1) * 512], 16, 512)
                        wl, wlb = wp.load(WFI[:, DFF + jb * 512:DFF + (jb + 1) * 512], 16, 512)
                        for jj in range(4):
                            j = jb * 4 + jj
                            pa_, pab_ = pp.get()
                            for kc in range(16):
                                mm(pa_[:, :], wa[:, kc, jj * 128:(jj + 1) * 128], hT[:, kc, :], kc == 0, kc == 15, [wab, hb], [pab_])
                            pl_, plb_ = pp.get()
                            for kc in range(16):
                                mm(pl_[:, :], wl[:, kc, jj * 128:(jj + 1) * 128], hT[:, kc, :], kc == 0, kc == 15, [wlb, hb], [plb_])
                            g2 = gc % 2; gc += 1
                            act(lambda pa_=pa_, g2=g2: nc.scalar.activation(out=gt[g2][:], in_=pa_[:, :], func=AF.Silu), [pab_], [gtb[g2]])
                            dve(lambda pl_=pl_, g2=g2, j=j: nc.vector.tensor_tensor(out=actT[:, j, :], in0=gt[g2][:], in1=pl_[:, :], op=ALU.mult), [gtb[g2], plb_], [actb, zpb, pab, cyb, cbb, ctb, ptmb, invb])
                    for jb in range(4):
                        accs = [pp.get() for _ in range(4)]
                        for kb0, nk in ((0, 16), (16, 16), (32, 12)):
                            wf, wfb = wp.load(WFO[kb0 * 128:(kb0 + nk) * 128, jb * 512:(jb + 1) * 512], nk, 512)
                            for jj in range(4):
                                ps, psb = accs[jj]
                                for kc in range(nk):
                                    kg = kb0 + kc
                                    mm(ps[:, :], wf[:, kc, jj * 128:(jj + 1) * 128], actT[:, kg, :], kg == 0, kg == 43, [wfb, actb], [psb])
                        for jj in range(4):
                            j = jb * 4 + jj
                            ps, psb = accs[jj]
                            dve(lambda ps=ps, j=j, gf=gf: nc.vector.scalar_tensor_tensor(out=xT[:, j, :], in0=ps[:, :], scalar=gf[:, j:j + 1], in1=xT[:, j, :], op0=ALU.mult, op1=ALU.add), [psb, xb, MVb], [xb])
                    spdma(XOUT[:, tok0:tok0 + TT].rearrange("(kc p) t -> p kc t", p=128), xT[:], [xb], [])
                S.flush()

        S.flush()
        for l in range(nlayers):
            phase_mod(l)
        if dbg and "dbg_MV" in dbg:
            dmv = nc.dram_tensor("dbg_MV", [128, L * 12, 16], F32, kind="ExternalOutput").ap()
            spdma(dmv, MV[:], [MVb], [])
            S.flush()
        if stop_after == "p0":
            nlayers = 0
        for l in range(nlayers):
            phase1(l)
            if stop_after == ("p1", l):
                break
            phase2(l)
            if stop_after == ("p2", l):
                break
            phase3(l)
            if stop_after == ("p3", l):
                break
    return nc


def _rope_tables():
    T_ = NS
    rows = T_ // 64
    row = np.repeat(np.arange(rows), 64).astype(np.float32)
    col = np.tile(np.arange(64), rows).astype(np.float32)
    inv = (10000.0 ** (-np.arange(8, dtype=np.float32) / 8)).astype(np.float32)
    ang_r = row[:, None] * inv
    ang_c = col[:, None] * inv
    ang = np.stack([ang_r, ang_r, ang_c, ang_c], axis=1).reshape(T_, 32)
    cos = np.ones((96, NTOK), np.float32)
    sin = np.zeros((96, NTOK), np.float32)
    cos[64:, :NS] = np.cos(ang).T
    sin[64:, :NS] = np.sin(ang).T
    return cos, sin


def _consts():
    cos, sin = _rope_tables()
    prot = np.zeros((96, 96), np.float32)
    for a in range(2):
        for f in range(8):
            i0 = 64 + a * 16 + f
            i1 = 64 + a * 16 + 8 + f
            prot[i1, i0] = -1.0
            prot[i0, i1] = 1.0
    sel = np.zeros((65, 64), np.float32)
    sel[64, :] = 1.0

    def inv_cnt(Tseq):
        t = np.arange(Tseq)
        out = np.zeros((4, Tseq), np.float32)
        for gi, w in enumerate((2, 4, 8, 16)):
            lo = np.clip(t - w // 2, 0, Tseq - 1)
            hi = np.clip(t + w // 2 - 1, 0, Tseq - 1)
            out[gi] = 1.0 / (hi - lo + 1).astype(np.float32)
        return out
    ic = inv_cnt(NS)
    ip = inv_cnt(256)
    tab = np.zeros((4, 4, 512), np.float32)
    tab[0] = ic[:, 0:512]
    tab[1] = ic[:, 512:1024]
    tab[2] = ic[:, NS - 512:NS]
    tab[3] = np.concatenate([ip, ip], axis=1)
    invtab = np.ascontiguousarray(np.broadcast_to(tab[:, None], (4, 128, 4, 512)))
    return cos, sin, prot, sel, invtab


def _vecT(v, n):
    return np.ascontiguousarray(v.reshape(v.shape[0], n, 128).transpose(0, 2, 1))


def make_in_maps(inp):
    f = lambda a: np.ascontiguousarray(np.asarray(a, dtype=np.float32))
    cos, sin, prot, sel, invtab = _consts()
    shared = {
        "w_mod": f(inp["w_mod"]), "b_modT": _vecT(f(inp["b_mod"]), 96),
        "nmixT": _vecT(f(inp["norm_mix_w"]), 16), "nffnT": _vecT(f(inp["norm_ffn_w"]), 16),
        "w_in": f(inp["w_in"]), "b_gateT": _vecT(f(inp["b_gate"]), 64),
        "qawT": _vecT(f(inp["q_a_norm_w"]), 6), "kvawT": _vecT(f(inp["kv_a_norm_w"]), 2),
        "kvaw_row": f(inp["kv_a_norm_w"]),
        "w_uq": f(inp["w_uq"]), "w_ukv": f(inp["w_ukv"]),
        "qnw": f(inp["q_norm_w"]).reshape(L, 96, 1), "knw": f(inp["k_norm_w"]).reshape(L, 96, 1),
        "w_pool": f(inp["w_pool"]), "pscaleT": _vecT(f(inp["pool_scale"]), 4),
        "sguwT": _vecT(f(inp["sgu_norm_w"]), 4),
        "w_spT": np.ascontiguousarray(f(inp["w_spatial"]).transpose(0, 1, 3, 2)),
        "b_sp": f(inp["b_spatial"]).reshape(L, 512),
        "convwT": np.ascontiguousarray(f(inp["conv_w"]).reshape(L, 3, 4, 128).transpose(0, 3, 1, 2).reshape(L, 128, 12)),
        "w_br_attn": f(inp["w_br_attn"]), "w_br_pool": f(inp["w_br_pool"]),
        "w_br_sgu": f(inp["w_br_sgu"]), "w_br_conv": f(inp["w_br_conv"]),
        "w_out": f(inp["w_out"]), "w_ffn_in": f(inp["w_ffn_in"]), "w_ffn_out": f(inp["w_ffn_out"]),
        "cosT": cos, "sinT": sin, "prot": prot, "sel": sel, "invtab": invtab,
    }
    xs = f(inp["x_sample"]); xp = f(inp["x_prompt"])
    cckv = f(inp["cache_ckv"]); ckpe = f(inp["cache_kpe"])
    c = f(inp["c"]); cctx = f(inp["c_ctx"])
    maps = []
    for b in range(8):
        xT = np.concatenate([xs[b].T, xp[2 * b].T, xp[2 * b + 1].T], axis=1)
        cv = np.stack([c[b], cctx], axis=1)
        cT = cv.reshape(16, 128, 2).transpose(1, 0, 2)
        m = dict(shared)
        m["xT"] = np.ascontiguousarray(xT)
        m["cT"] = np.ascontiguousarray(cT)
        m["cckvT"] = np.ascontiguousarray(cckv[b].transpose(0, 2, 1))
        m["ckpe"] = np.ascontiguousarray(ckpe[b])
        m["ckpeT"] = np.ascontiguousarray(ckpe[b].transpose(0, 2, 1))
        maps.append(m)
    return maps


def kernel(**inputs):
    nc = build()
    maps = make_in_maps(inputs)
    res = run_bass_kernel_spmd(nc, maps, core_ids=list(range(8)))
    y_prompt = np.zeros((16, 256, D), np.float32)
    y_sample = np.zeros((8, NS, D), np.float32)
    s_ckv = np.zeros((16, L, 256, 256), np.float32)
    s_kpe = np.zeros((16, L, 256, 32), np.float32)
    for b in range(8):
        r = res.results[b]
        yT = np.asarray(r["yT"])
        y_sample[b] = yT[:, :NS].T
        y_prompt[2 * b] = yT[:, NS:NS + 256].T
        y_prompt[2 * b + 1] = yT[:, NS + 256:].T
        s_ckv[2 * b:2 * b + 2] = np.asarray(r["st_ckv"])
        s_kpe[2 * b:2 * b + 2] = np.asarray(r["st_kpe"])
    return (y_prompt, y_sample, s_ckv, s_kpe)
```

```python
import numpy as np
from contextlib import ExitStack
import concourse.bass as bass
import concourse.mybir as mybir
from concourse.bass_utils import run_bass_kernel_spmd

F32, BF16 = mybir.dt.float32, mybir.dt.bfloat16
AF = mybir.ActivationFunctionType
ALU = mybir.AluOpType
AX = mybir.AxisListType

D = 2048; L = 2; NT = 9; TT = 512; NTOK = 4608; NS = 4096
NIN = 12320; DFF = 5632; H = 16
NKEY = 4864; NKB = 38
EPS = 1e-6
ATTN_SCALE = 96 ** -0.5
OFF_POOL, OFF_SGU, OFF_CONV, OFF_GATE = 1056, 1568, 2592, 4128
NDS = 8
SAME_ENGINE_SYNC = True


class Buf:
    __slots__ = ("lw", "rd", "excl")

    def __init__(self, excl=False):
        self.lw = None
        self.rd = {}
        self.excl = excl


class Op:
    __slots__ = ("eng", "fn", "deps", "inc", "dma", "sem", "val", "epoch")


class Eng:
    pass


class _Stop(Exception):
    pass


class Sched:
    def __init__(self, nc, es):
        self.nc = nc
        self.E = {}
        for name, obj in (("pe", nc.tensor), ("act", nc.scalar), ("dve", nc.vector),
                          ("pool", nc.gpsimd), ("sp", nc.sync)):
            e = Eng()
            e.name = name; e.obj = obj
            e.sem = es.enter_context(nc.semaphore("s_" + name))
            e.count = 0; e.known = {}
            e.dsems = [es.enter_context(nc.semaphore("d_%s%d" % (name, i))) for i in range(NDS)] \
                if name in ("pool", "sp") else []
            e.duse = [0] * NDS; e.dcount = 0
            self.E[name] = e
        self.ops = []
        self.epoch = 0
        self.nid = 0

    def op(self, eng, fn, reads=(), writes=(), dma=False):
        E = self.E[eng]
        X = Op()
        X.eng = E; X.fn = fn; X.inc = False; X.dma = dma; X.sem = None; X.val = 0; X.epoch = self.epoch
        deps = []
        xr = [b for b in reads if b.excl]
        if xr:
            reads = [b for b in reads if not b.excl]
            writes = list(writes) + xr
        for b in reads:
            if b.lw is not None:
                deps.append(b.lw)
        for b in writes:
            if b.lw is not None:
                deps.append(b.lw)
            deps.extend(b.rd.values())
        out = []
        seen = set()
        for P in deps:
            if P.epoch != self.epoch or id(P) in seen or P is X:
                continue
            seen.add(id(P))
            if (not P.dma) and (not dma) and P.eng is E:
                if eng != "dve" or not SAME_ENGINE_SYNC:
                    continue
            if not P.dma:
                P.inc = True
            out.append(P)
        X.deps = out
        if dma:
            self.nid += 1
            key = ("d", self.nid)
        else:
            key = eng
        for b in reads:
            b.rd[key] = X
        for b in writes:
            b.lw = X
            b.rd = {}
        self.ops.append(X)
        return X

    def flush(self):
        last = {}
        for X in self.ops:
            if not X.dma:
                last[X.eng.name] = X
        for X in last.values():
            X.inc = True
        for X in self.ops:
            E = X.eng
            for P in X.deps:
                if E.known.get(P.sem, 0) < P.val:
                    E.obj.wait_ge(P.sem, P.val)
                    E.known[P.sem] = P.val
            if X.dma:
                slot = E.dcount % NDS
                E.dcount += 1
                sem = E.dsems[slot]
                E.duse[slot] += 1
                val = 16 * E.duse[slot]
                if val > 16 and E.known.get(sem, 0) < val - 16:
                    E.obj.wait_ge(sem, val - 16)
                    E.known[sem] = val - 16
                X.fn().then_inc(sem, 16)
                X.sem = sem; X.val = val
            else:
                ins = X.fn()
                if X.inc:
                    E.count += 1
                    ins.then_inc(E.sem, 1)
                    X.sem = E.sem; X.val = E.count
        self.ops = []
        for E in self.E.values():
            for P in self.E.values():
                if P is not E and P.count > 0 and E.known.get(P.sem, 0) < P.count:
                    E.obj.wait_ge(P.sem, P.count)
                    E.known[P.sem] = P.count
                for s in range(NDS if P.dsems else 0):
                    v = 16 * P.duse[s]
                    if v > 0 and E.known.get(P.dsems[s], 0) < v:
                        E.obj.wait_ge(P.dsems[s], v)
                        E.known[P.dsems[s]] = v
        self.epoch += 1


def build(dbg=None):
    nc = bass.Bass("TRN2", target_bir_lowering=False)

    def din(name, shape):
        return nc.dram_tensor(name, list(shape), F32, kind="ExternalInput").ap()

    def dout(name, shape):
        return nc.dram_tensor(name, list(shape), F32, kind="ExternalOutput").ap()

    def dscr(name, shape, dt):
        kind = "ExternalOutput" if (dbg and name in dbg) else "Internal"
        return nc.dram_tensor(name, list(shape), dt, kind=kind).ap()

    xT_in = din("xT", [D, NTOK])
    cT_in = din("cT", [128, 16, 2])
    cckvT_in = din("cckvT", [L, 256, 256])
    ckpe_in = din("ckpe", [L, 256, 32])
    ckpeT_in = din("ckpeT", [L, 32, 256])
    w_mod = din("w_mod", [L, D, 6 * D])
    b_modT = din("b_modT", [L, 128, 96])
    nmixT = din("nmixT", [L, 128, 16])
    nffnT = din("nffnT", [L, 128, 16])
    w_in = din("w_in", [L, D, NIN])
    b_gateT = din("b_gateT", [L, 128, 64])
    qawT = din("qawT", [L, 128, 6])
    kvawT = din("kvawT", [L, 128, 2])
    kvaw_row = din("kvaw_row", [L, 256])
    w_uq = din("w_uq", [L, 768, 1536])
    w_ukv = din("w_ukv", [L, 256, 2048])
    qnw = din("qnw", [L, 96, 1])
    knw = din("knw", [L, 96, 1])
    w_pool = din("w_pool", [L, 4, 128, 128])
    pscaleT = din("pscaleT", [L, 128, 4])
    sguwT = din("sguwT", [L, 128, 4])
    w_spT = din("w_spT", [L, 4, 128, 128])
    b_sp = din("b_sp", [L, 512])
    convwT = din("convwT", [L, 128, 12])
    w_br = [din("w_br_attn", [L, 1024, D]), din("w_br_pool", [L, 512, D]),
            din("w_br_sgu", [L, 512, D]), din("w_br_conv", [L, 512, D])]
    w_out = din("w_out", [L, D, D])
    w_ffn_in = din("w_ffn_in", [L, D, 2 * DFF])
    w_ffn_out = din("w_ffn_out", [L, DFF, D])
    cos_in = din("cosT", [96, NTOK])
    sin_in = din("sinT", [96, NTOK])
    prot_in = din("prot", [96, 96])
    sel_in = din("sel", [65, 64])
    inv_in = din("invtab", [4, 128, 4, 512])

    yT = dout("yT", [D, NTOK])
    st_ckv = dout("st_ckv", [2, L, 256, 256])
    st_kpe = dout("st_kpe", [2, L, 256, 32])

    XT1 = dscr("XT1", [D, NTOK], F32)
    Qsc = dscr("Qsc", [96, H, NTOK], BF16)
    Ksc = dscr("Ksc", [96, H, NKEY], BF16)
    Vsc = dscr("Vsc", [NKB, 128, H * 65], BF16)
    SCsc = dscr("SCsc", [NKB, 128, H], F32)
    ATTNsc = dscr("ATTNsc", [1024, NTOK], BF16)
    POOLsc = dscr("POOLsc", [512, NTOK], F32)
    SGUsc = dscr("SGUsc", [512, NTOK], BF16)
    CONVy = dscr("CONVy", [512, NTOK], F32)
    CONVb = dscr("CONVb", [512, NTOK], F32)
    WG = dscr("WG", [D, 4 * D], BF16)
    WBR = dscr("WBR", [2560, D], BF16)
    WO = dscr("WO", [D, D], BF16)
    WFI = dscr("WFI", [D, 2 * DFF], BF16)
    WFO = dscr("WFO", [DFF, D], BF16)
    BR_OFF = [0, 1024, 1536, 2048]

    stop_after = (dbg or {}).get("stop_after", None)
    nlayers = (dbg or {}).get("nlayers", L)

    with ExitStack() as top:
        S = Sched(nc, top)
        op = S.op

        uid = [0]

        def T(es, name, shape, dt):
            uid[0] += 1
            return es.enter_context(nc.sbuf_tensor("sb_%s_%d" % (name, uid[0]), list(shape), dt))

        def pe(fn, r, w):
            return op("pe", fn, r, w)

        def mm(out, lhsT, rhs, start, stop, r, w):
            return op("pe", lambda: nc.tensor.matmul(out, lhsT=lhsT, rhs=rhs, start=start, stop=stop), r, w)

        def act(fn, r, w):
            return op("act", fn, r, w)

        def dve(fn, r, w):
            return op("dve", fn, r, w)

        def spdma(out, in_, r, w):
            return op("sp", lambda: nc.sync.dma_start(out=out, in_=in_), r, w, dma=True)

        def pldma(out, in_, r, w):
            return op("pool", lambda: nc.gpsimd.dma_start(out=out, in_=in_), r, w, dma=True)

        MV = T(top, "MV", [128, L * 2 * 6, 16], F32)
        MVb = Buf()
        ones_bf = T(top, "ones_bf", [128, 128], BF16)
        prot_bf = T(top, "prot_bf", [96, 96], BF16)
        sel_f = T(top, "sel_f", [65, 64], F32)
        constb = Buf()
        dve(lambda: nc.vector.memset(ones_bf[:], 1.0), [], [constb])
        pldma(prot_bf[:], prot_in, [], [constb])
        spdma(sel_f[:], sel_in, [], [constb])

        def mvec(l, g, i):
            k = (l * 2 + g) * 6 + i
            return MV[:, k, :]

        class PsumPool:
            def __init__(self, es, n=8):
                uid[0] += 1
                self.t = [es.enter_context(nc.psum_tensor("ps%d_%d" % (i, uid[0]), [128, 512], F32)) for i in range(n)]
                self.b = [Buf(excl=True) for _ in range(n)]
                self.i = 0

            def get(self):
                k = self.i % len(self.t)
                self.i += 1
                return self.t[k], self.b[k]

        class WPool:
            def __init__(self, es, n, dt=BF16, cols=512, name="wslot"):
                self.t = [T(es, "%s%d" % (name, i), [128, 16, cols], dt) for i in range(n)]
                self.b = [Buf() for _ in range(n)]
                self.i = 0

            def load(self, src, nk, ncols):
                k = self.i % len(self.t)
                self.i += 1
                t, b = self.t[k], self.b[k]
                pldma(t[:, 0:nk, 0:ncols], src.rearrange("(kc p) n -> p kc n", p=128), [], [b])
                return t, b

        def rstd_from_psum(ps_ap, out_ap, n, psb, outb):
            act(lambda: nc.scalar.activation(out=out_ap, in_=ps_ap, func=AF.Sqrt, scale=1.0 / n, bias=EPS), [psb], [outb])
            dve(lambda: nc.vector.reciprocal(out=out_ap, in_=out_ap), [outb], [outb])

        def phase_mod(l):
            with ExitStack() as es:
                pp = PsumPool(es, 1)
                psm, psmb = pp.get()
                cT = T(es, "cT", [128, 16, 2], F32); cTb = Buf()
                sT = T(es, "sT", [128, 16, 2], F32); sTb = Buf()
                bm = T(es, "bm", [128, 96], F32); bmb = Buf()
                nm = T(es, "nm", [128, 16], F32); nf = T(es, "nf", [128, 16], F32); nb_ = Buf()
                modT = T(es, "modT", [128, 2, 96], F32); modb = Buf()
                wp = WPool(es, 3, dt=F32, cols=256, name="wm")
                spdma(cT[:], cT_in, [], [cTb])
                spdma(bm[:], b_modT[l], [], [bmb])
                spdma(nm[:], nmixT[l], [], [nb_])
                spdma(nf[:], nffnT[l], [], [nb_])
                act(lambda: nc.scalar.activation(out=sT[:], in_=cT[:], func=AF.Silu), [cTb], [sTb])
                for cb in range(48):
                    wt, wb = wp.load(w_mod[l][:, cb * 256:(cb + 1) * 256], 16, 256)
                    for jj in range(2):
                        j = cb * 2 + jj
                        for kc in range(16):
                            mm(psm[:, 2 * j:2 * j + 2], wt[:, kc, jj * 128:(jj + 1) * 128], sT[:, kc, :],
                               kc == 0, kc == 15, [wb, sTb], [psmb])
                pv = psm[:, 0:192].rearrange("p (j g) -> p j g", g=2)
                for g in range(2):
                    dve(lambda g=g: nc.vector.tensor_tensor(out=modT[:, g, :], in0=pv[:, :, g], in1=bm[:], op=ALU.add),
                        [psmb, bmb], [modb])
                for g in range(2):
                    def mg(i, g=g):
                        return modT[:, g, i * 16:(i + 1) * 16]
                    dve(lambda g=g, mg=mg: nc.vector.scalar_tensor_tensor(out=mvec(l, g, 0), in0=mg(1), scalar=1.0, in1=nm[:], op0=ALU.add, op1=ALU.mult), [modb, nb_], [MVb])
                    dve(lambda g=g, mg=mg: nc.vector.tensor_copy(out=mvec(l, g, 1), in_=mg(0)), [modb], [MVb])
                    dve(lambda g=g, mg=mg: nc.vector.tensor_copy(out=mvec(l, g, 2), in_=mg(2)), [modb], [MVb])
                    dve(lambda g=g, mg=mg: nc.vector.scalar_tensor_tensor(out=mvec(l, g, 3), in0=mg(4), scalar=1.0, in1=nf[:], op0=ALU.add, op1=ALU.mult), [modb, nb_], [MVb])
                    dve(lambda g=g, mg=mg: nc.vector.tensor_copy(out=mvec(l, g, 4), in_=mg(3)), [modb], [MVb])
                    dve(lambda g=g, mg=mg: nc.vector.tensor_copy(out=mvec(l, g, 5), in_=mg(5)), [modb], [MVb])
                S.flush()

        def norm_mod(pp, xT, xb, hT, hb, a_ap, b_ap, sq, sqb, rs, rsb, tmpf, tmpb):
            ps, psb = pp.get()
            for kc in range(16):
                k2 = kc % 2
                act(lambda kc=kc, k2=k2: nc.scalar.activation(out=sq[k2][:], in_=xT[:, kc, :], func=AF.Square), [xb], [sqb[k2]])
                mm(ps[:, :], ones_bf[:, :], sq[k2][:], kc == 0, kc == 15, [sqb[k2], constb], [psb])
            rstd_from_psum(ps[:, :], rs[:], D, psb, rsb)
            for kc in range(16):
                k2 = kc % 2
                dve(lambda kc=kc, k2=k2: nc.vector.scalar_tensor_tensor(out=tmpf[k2][:], in0=xT[:, kc, :], scalar=a_ap[:, kc:kc + 1], in1=rs[:], op0=ALU.mult, op1=ALU.mult), [xb, rsb, MVb], [tmpb[k2]])
                act(lambda kc=kc, k2=k2: nc.scalar.activation(out=hT[:, kc, :], in_=tmpf[k2][:], func=AF.Identity, bias=b_ap[:, kc:kc + 1], scale=1.0), [tmpb[k2], MVb], [hb])

        def phase1(l):
            XIN = xT_in if l == 0 else XT1
            with ExitStack() as es:
                pp = PsumPool(es, 8)
                wp = WPool(es, 3)
                U = T(es, "U", [128, 8192], F32); Ub = Buf()
                xT = U[:, :].rearrange("p (k t) -> p k t", t=512)
                zq = U[:, 0:3072].rearrange("p (k t) -> p k t", t=512); zqb = Buf()
                u_t = U[:, 3072:5120].rearrange("p (k t) -> p k t", t=512); ub = Buf()
                cg = U[:, 5120:7168].rearrange("p (k t) -> p k t", t=512); cgb = Buf()
                zkv = U[:, 7168:8192].rearrange("p (k t) -> p k t", t=512); zkvb = Buf()
                hT = T(es, "hT", [128, 16, 512], BF16); hb = Buf()
                wuq = T(es, "wuq", [128, 6, 1536], BF16)
                wukv = T(es, "wukv", [128, 2, 2048], BF16)
                wspT = T(es, "wspT", [128, 4, 128], BF16)
                qaw = T(es, "qaw", [128, 6], F32); kvaw = T(es, "kvaw", [128, 2], F32)
                wq96 = T(es, "wq96", [96, 1], F32); wk96 = T(es, "wk96", [96, 1], F32)
                sguw = T(es, "sguw", [128, 4], F32)
                bspb = T(es, "bspb", [128, 512], F32)
                kvawb = T(es, "kvawb", [128, 256], F32)
                wres = Buf()
                pldma(wuq[:], w_uq[l].rearrange("(kc p) n -> p kc n", p=128), [], [wres])
                pldma(wukv[:], w_ukv[l].rearrange("(kc p) n -> p kc n", p=128), [], [wres])
                pldma(wspT[:], w_spT[l].rearrange("g p q -> p g q"), [], [wres])
                spdma(qaw[:], qawT[l], [], [wres]); spdma(kvaw[:], kvawT[l], [], [wres])
                spdma(wq96[:], qnw[l], [], [wres]); spdma(wk96[:], knw[l], [], [wres])
                spdma(sguw[:], sguwT[l], [], [wres])
                spdma(bspb[:], b_sp[l].partition_broadcast(128), [], [wres])
                spdma(kvawb[:], kvaw_row[l].partition_broadcast(128), [], [wres])

                sq = [T(es, "sq%d" % i, [128, 512], BF16) for i in range(2)]; sqb = [Buf(), Buf()]
                rs = T(es, "rs", [128, 512], F32); rsb = Buf()
                tmpf = [T(es, "tmpf%d" % i, [128, 512], F32) for i in range(2)]; tmpb = [Buf(), Buf()]
                cqT = T(es, "cqT", [128, 6, 512], BF16); cqb = Buf()
                ckvT = T(es, "ckvT", [128, 2, 512], BF16); ckvb = Buf()
                cosT = T(es, "cosT", [96, 512], F32); sinT = T(es, "sinT", [96, 512], F32); csb = Buf()
                kpw = T(es, "kpw", [96, 512], BF16); kpwb = Buf()
                krope = T(es, "krope", [96, 512], BF16); kropeb = Buf()
                t1 = [T(es, "t1%d" % i, [96, 512], F32) for i in range(2)]; t2 = [T(es, "t2%d" % i, [96, 512], F32) for i in range(2)]; t1b = [Buf(), Buf()]; t2b = [Buf(), Buf()]
                sq96 = [T(es, "sq96%d" % i, [96, 512], BF16) for i in range(2)]; sq96b = [Buf(), Buf()]
                qw = [T(es, "qw%d" % i, [96, 512], BF16) for i in range(2)]; qwb = [Buf(), Buf()]
                qn = [T(es, "qn%d" % i, [96, 512], BF16) for i in range(2)]; qnb = [Buf(), Buf()]
                r96 = [T(es, "r96%d" % i, [96, 512], F32) for i in range(2)]; r96b = [Buf(), Buf()]
                Qh = [T(es, "Qh%d" % i, [96, 512], BF16) for i in range(2)]; Qhb = [Buf(), Buf()]
                Kh = [T(es, "Kh%d" % i, [64, 512], BF16) for i in range(2)]; Khb = [Buf(), Buf()]
                Vp = [T(es, "Vp%d" % i, [128, 16, 65], BF16) for i in range(2)]; Vpb = [Buf(), Buf()]
                sqk = T(es, "sqk", [128, 4, 64], F32); sqkb = Buf()
                ssk = T(es, "ssk", [128, 16], F32); sskb = Buf()
                sck = [T(es, "sck%d" % i, [128, 16], F32) for i in range(2)]; sckb = [Buf(), Buf()]
                sspe = T(es, "sspe", [128, 4], F32); sspeb = Buf()
                tm32 = T(es, "tm32", [128, 32], F32); tm32b = Buf()
                tm256 = T(es, "tm256", [128, 256], F32); tm256b = Buf()
                ss1 = T(es, "ss1", [128, 1], F32); ss1b = Buf()
                stc = [T(es, "stc%d" % i, [128, 256], F32) for i in range(2)]; stcb = [Buf(), Buf()]
                stk = [T(es, "stk%d" % i, [128, 32], F32) for i in range(2)]; stkb = [Buf(), Buf()]
                ptmp = [T(es, "ptmp%d" % i, [128, 512], F32) for i in range(2)]; ptmpb = [Buf(), Buf()]
                vg = T(es, "vg", [128, 512], F32); vgb = Buf()
                vh = T(es, "vh", [128, 512], BF16); vhb = Buf()
                mx = T(es, "mx", [128, 128], F32); mxb = Buf()
                sgo = T(es, "sgo", [128, 4, 512], BF16); sgob = Buf()
                cckT = T(es, "cckT", [128, 2, 256], BF16); cckb = Buf()
                ckp96 = T(es, "ckp96", [96, 256], F32); ckp96b = Buf()
                ckpt = T(es, "ckpt", [128, 2, 32], F32); ckptb = Buf()

                cut = (dbg or {}).get("p1_cut", None)

                def stage(k):
                    if cut == k:
                        raise _Stop()
                dve(lambda: nc.vector.memset(kpw[:], 0.0), [], [kpwb])
                dve(lambda: nc.vector.memset(krope[:], 0.0), [], [kropeb])
                for i in range(2):
                    dve(lambda i=i: nc.vector.memset(Vp[i][:], 1.0), [], [Vpb[i]])
                cnt = {"q": 0, "k": 0, "v": 0, "p": 0, "s": 0}

                def gen_kv(ckv_ap, kr_t, kr_b, T_, key0, sspe_ap):
                    nsub = T_ // 128
                    import os
                    GK = os.environ.get("GK", "kvs")
                    for h in (range(H) if "k" in GK else []):
                        ps, psb = pp.get()
                        for kc in range(2):
                            mm(ps[0:64, 0:T_], wukv[:, kc, h * 128:h * 128 + 64], ckv_ap[:, kc, :], kc == 0, kc == 1, [wres, ckvb, cckb], [psb])
                        kb_ = cnt["k"] % 2; cnt["k"] += 1
                        act(lambda ps=ps, kb_=kb_: nc.scalar.activation(out=Kh[kb_][:, 0:T_], in_=ps[0:64, 0:T_], func=AF.Copy, scale=wk96[0:64, 0:1]), [psb, wres], [Khb[kb_]])
                        spdma(Ksc[0:64, h, key0:key0 + T_], Kh[kb_][:, 0:T_], [Khb[kb_]], [])
                        spdma(Ksc[64:96, h, key0:key0 + T_], kr_t[64:96, 0:T_], [kr_b], [])
                    for s in (range(nsub) if "v" in GK else []):
                        vb_ = cnt["v"] % 2; cnt["v"] += 1
                        for c in range(4):
                            ps, psb = pp.get()
                            for kc in range(2):
                                mm(ps[:, :], ckv_ap[:, kc, s * 128:(s + 1) * 128], wukv[:, kc, c * 512:(c + 1) * 512], kc == 0, kc == 1, [wres, ckvb, cckb], [psb])
                            pv = ps[:, :].rearrange("p (h d) -> p h d", d=128)
                            dve(lambda pv=pv, c=c, vb_=vb_: nc.vector.tensor_copy(out=Vp[vb_][:, 4 * c:4 * c + 4, 0:64], in_=pv[:, :, 64:128]), [psb], [Vpb[vb_]])
                            if "s" in GK:
                                act(lambda pv=pv: nc.scalar.activation(out=sqk[:], in_=pv[:, :, 0:64], func=AF.Square), [psb], [sqkb])
                                dve(lambda c=c: nc.vector.reduce_sum(out=ssk[:, 4 * c:4 * c + 4], in_=sqk[:], axis=AX.X), [sqkb], [sskb])
                        sb_ = cnt["s"] % 2; cnt["s"] += 1
                        if "s" not in GK:
                            kb = key0 // 128 + s
                            spdma(Vsc[kb], Vp[vb_][:].rearrange("p h c -> p (h c)"), [Vpb[vb_]], [])
                            continue
                        dve(lambda s=s: nc.vector.tensor_scalar(out=ssk[:], in0=ssk[:], scalar1=sspe_ap[:, s:s + 1], scalar2=None, op0=ALU.add), [sskb, sspeb], [sskb])
                        act(lambda: nc.scalar.activation(out=ssk[:], in_=ssk[:], func=AF.Sqrt, scale=1.0 / 96, bias=EPS), [sskb], [sskb])
                        dve(lambda: nc.vector.reciprocal(out=ssk[:], in_=ssk[:]), [sskb], [sskb])
                        dve(lambda sb_=sb_: nc.vector.tensor_scalar(out=sck[sb_][:], in0=ssk[:], scalar1=ATTN_SCALE, scalar2=None, op0=ALU.mult), [sskb], [sckb[sb_]])
                        kb = key0 // 128 + s
                        spdma(SCsc[kb], sck[sb_][:], [sckb[sb_]], [])
                        spdma(Vsc[kb], Vp[vb_][:].rearrange("p h c -> p (h c)"), [Vpb[vb_]], [])

                if cut != 0:
                    pldma(cckT[:], cckvT_in[l].rearrange("(kc p) t -> p kc t", p=128), [], [cckb])
                    spdma(ckp96[64:96, :], ckpeT_in[l], [], [ckp96b])
                    spdma(ckpt[:], ckpe_in[l].rearrange("(s p) d -> p s d", p=128), [], [ckptb])
                    import os
                    if "act" in os.environ.get("P1A", "act,sq"):
                        act(lambda: nc.scalar.activation(out=krope[64:96, 0:256], in_=ckp96[64:96, :], func=AF.Copy, scale=wk96[64:96, 0:1]), [ckp96b, wres, kropeb], [kropeb])
                    for s in (range(2) if "sq" in os.environ.get("P1A", "act,sq") else []):
                        act(lambda s=s: nc.scalar.activation(out=tm32[:], in_=ckpt[:, s, :], func=AF.Square), [ckptb], [tm32b])
                        dve(lambda s=s: nc.vector.reduce_sum(out=sspe[:, s:s + 1], in_=tm32[:], axis=AX.X), [tm32b], [sspeb])
                try:
                  stage(0)
                  stage(1)
                  gen_kv(cckT, krope, kropeb, 256, 4096, sspe)
                  stage(2)
                  for i in range(NT):
                    g = 0 if i < 8 else 1
                    tok0 = i * TT
                    key0 = tok0 if i < 8 else 4352
                    a1 = mvec(l, g, 0); b1 = mvec(l, g, 1)
                    spdma(xT, XIN[:, tok0:tok0 + TT].rearrange("(kc p) t -> p kc t", p=128), [], [Ub, zqb, ub, cgb, zkvb])
                    spdma(cosT[:], cos_in[:, tok0:tok0 + TT], [], [csb])
                    spdma(sinT[:], sin_in[:, tok0:tok0 + TT], [], [csb])
                    norm_mod(pp, xT, Ub, hT, hb, a1, b1, sq, sqb, rs, rsb, tmpf, tmpb)

                    stage(3)
                    def zgroup(wt, wb, c0, m, ps, psb, n=TT):
                        for kc in range(16):
                            mm(ps[0:m, 0:n], wt[:, kc, c0:c0 + m], hT[:, kc, :], kc == 0, kc == 15, [wb, hb], [psb])

                    wA, wAb = wp.load(w_in[l][:, 0:512], 16, 512)
                    wB, wBb = wp.load(w_in[l][:, 512:768], 16, 256)
                    pss, pssb = pp.get()
                    for j in range(6):
                        wt, wb, c0 = (wA, wAb, j * 128) if j < 4 else (wB, wBb, (j - 4) * 128)
                        ps, psb = pp.get()
                        zgroup(wt, wb, c0, 128, ps, psb)
                        k2 = j % 2
                        act(lambda ps=ps, k2=k2: nc.scalar.activation(out=sq[k2][:], in_=ps[:, :], func=AF.Square), [psb], [sqb[k2]])
                        dve(lambda ps=ps, j=j: nc.vector.tensor_copy(out=zq[:, j, :], in_=ps[:, :]), [psb], [zqb])
                        mm(pss[:, :], ones_bf[:, :], sq[k2][:], j == 0, j == 5, [sqb[k2], constb], [pssb])
                    rstd_from_psum(pss[:, :], rs[:], 768, pssb, rsb)
                    for j in range(6):
                        dve(lambda j=j: nc.vector.scalar_tensor_tensor(out=cqT[:, j, :], in0=zq[:, j, :], scalar=qaw[:, j:j + 1], in1=rs[:], op0=ALU.mult, op1=ALU.mult), [zqb, rsb, wres], [cqb])
                    stage(4)
                    wC, wCb = wp.load(w_in[l][:, 768:1056], 16, 288)
                    pss, pssb = pp.get()
                    for j in range(2):
                        ps, psb = pp.get()
                        zgroup(wC, wCb, j * 128, 128, ps, psb)
                        k2 = j % 2
                        act(lambda ps=ps, k2=k2: nc.scalar.activation(out=sq[k2][:], in_=ps[:, :], func=AF.Square), [psb], [sqb[k2]])
                        dve(lambda ps=ps, j=j: nc.vector.tensor_copy(out=zkv[:, j, :], in_=ps[:, :]), [psb], [zkvb])
                        mm(pss[:, :], ones_bf[:, :], sq[k2][:], j == 0, j == 1, [sqb[k2], constb], [pssb])
                    rstd_from_psum(pss[:, :], rs[:], 256, pssb, rsb)
                    for j in range(2):
                        dve(lambda j=j: nc.vector.scalar_tensor_tensor(out=ckvT[:, j, :], in0=zkv[:, j, :], scalar=kvaw[:, j:j + 1], in1=rs[:], op0=ALU.mult, op1=ALU.mult), [zkvb, rsb, wres], [ckvb])
                    ps, psb = pp.get()
                    zgroup(wC, wCb, 192, 96, ps, psb)
                    dve(lambda ps=ps: nc.vector.tensor_scalar(out=kpw[64:96, :], in0=ps[64:96, :], scalar1=wk96[64:96, 0:1], scalar2=None, op0=ALU.mult), [psb, wres], [kpwb])
                    ps2, ps2b = pp.get()
                    mm(ps2[0:96, :], prot_bf[:, :], kpw[:, :], True, True, [kpwb, constb], [ps2b])
                    dve(lambda: nc.vector.tensor_tensor(out=t1[0][64:96, :], in0=kpw[64:96, :], in1=cosT[64:96, :], op=ALU.mult), [kpwb, csb], [t1b[0]])
                    dve(lambda ps2=ps2: nc.vector.tensor_tensor(out=t2[0][64:96, :], in0=ps2[64:96, :], in1=sinT[64:96, :], op=ALU.mult), [ps2b, csb], [t2b[0]])
                    dve(lambda: nc.vector.tensor_tensor(out=krope[64:96, :], in0=t1[0][64:96, :], in1=t2[0][64:96, :], op=ALU.add), [t1b[0], t2b[0]], [kropeb])
                    stage(5)
                    for s in range(4):
                        ps, psb = pp.get()
                        for kc in range(16):
                            mm(ps[:, 0:288], hT[:, kc, s * 128:(s + 1) * 128], wC[:, kc, 0:288], kc == 0, kc == 15, [wCb, hb], [psb])
                        act(lambda ps=ps: nc.scalar.activation(out=tm32[:], in_=ps[:, 256:288], func=AF.Square), [psb], [tm32b])
                        dve(lambda s=s: nc.vector.reduce_sum(out=sspe[:, s:s + 1], in_=tm32[:], axis=AX.X), [tm32b], [sspeb])
                        if i == 8:
                            p2 = cnt["p"] % 2; cnt["p"] += 1
                            seq, pos0 = s // 2, (s % 2) * 128
                            act(lambda ps=ps: nc.scalar.activation(out=tm256[:], in_=ps[:, 0:256], func=AF.Square), [psb], [tm256b])
                            dve(lambda: nc.vector.reduce_sum(out=ss1[:], in_=tm256[:], axis=AX.X), [tm256b], [ss1b])
                            act(lambda: nc.scalar.activation(out=ss1[:], in_=ss1[:], func=AF.Sqrt, scale=1.0 / 256, bias=EPS), [ss1b], [ss1b])
                            dve(lambda: nc.vector.reciprocal(out=ss1[:], in_=ss1[:]), [ss1b], [ss1b])
                            dve(lambda ps=ps, p2=p2: nc.vector.scalar_tensor_tensor(out=stc[p2][:], in0=ps[:, 0:256], scalar=ss1[:, 0:1], in1=kvawb[:], op0=ALU.mult, op1=ALU.mult), [psb, ss1b, wres], [stcb[p2]])
                            dve(lambda ps=ps, p2=p2: nc.vector.tensor_copy(out=stk[p2][:], in_=ps[:, 256:288]), [psb], [stkb[p2]])
                            spdma(st_ckv[seq, l, pos0:pos0 + 128, :], stc[p2][:], [stcb[p2]], [])
                            spdma(st_kpe[seq, l, pos0:pos0 + 128, :], stk[p2][:], [stkb[p2]], [])
                    stage(6)
                    qps = {}

                    def qA(h):
                        q2 = h % 2
                        ps, psb = pp.get()
                        for kc in range(6):
                            mm(ps[0:96, :], wuq[:, kc, h * 96:(h + 1) * 96], cqT[:, kc, :], kc == 0, kc == 5, [wres, cqb], [psb])
                        act(lambda: nc.scalar.activation(out=sq96[q2][:], in_=ps[0:96, :], func=AF.Square), [psb], [sq96b[q2]])
                        act(lambda: nc.scalar.activation(out=qw[q2][:], in_=ps[0:96, :], func=AF.Copy, scale=wq96[:, 0:1]), [psb, wres], [qwb[q2]])

                    def qB_pe(h):
                        q2 = h % 2
                        pq, pqb = pp.get()
                        mm(pq[0:96, :], ones_bf[0:96, 0:96], sq96[q2][:], True, True, [sq96b[q2], constb], [pqb])
                        act(lambda: nc.scalar.activation(out=r96[q2][:], in_=pq[0:96, :], func=AF.Sqrt, scale=1.0 / 96, bias=EPS), [pqb], [r96b[q2]])

                    def qB_dve(h):
                        q2 = h % 2
                        dve(lambda: nc.vector.reciprocal(out=r96[q2][:], in_=r96[q2][:]), [r96b[q2]], [r96b[q2]])

                    def qC(h):
                        q2 = h % 2
                        dve(lambda: nc.vector.tensor_tensor(out=qn[q2][:], in0=qw[q2][:], in1=r96[q2][:], op=ALU.mult), [qwb[q2], r96b[q2]], [qnb[q2]])
                        pr, prb = pp.get()
                        mm(pr[0:96, :], prot_bf[:, :], qn[q2][:], True, True, [qnb[q2], constb], [prb])
                        qps[h] = (pr, prb)

                    def qD1(h):
                        q2 = h % 2
                        pr, prb = qps[h]
                        dve(lambda: nc.vector.tensor_tensor(out=t1[q2][:], in0=qn[q2][:], in1=cosT[:], op=ALU.mult), [qnb[q2], csb], [t1b[q2]])
                        dve(lambda: nc.vector.tensor_tensor(out=t2[q2][:], in0=pr[0:96, :], in1=sinT[:], op=ALU.mult), [prb, csb], [t2b[q2]])

                    def qD2(h, tok0=tok0):
                        q2 = h % 2
                        dve(lambda: nc.vector.tensor_tensor(out=Qh[q2][:], in0=t1[q2][:], in1=t2[q2][:], op=ALU.add), [t1b[q2], t2b[q2]], [Qhb[q2]])
                        spdma(Qsc[:, h, tok0:tok0 + TT], Qh[q2][:], [Qhb[q2]], [])
                    for st in range(H + 3):
                        if 0 <= st - 1 < H:
                            qB_pe(st - 1)
                        if 0 <= st - 2 < H:
                            qC(st - 2)
                        if 0 <= st - 3 < H:
                            qD1(st - 3)
                        if st < H:
                            qA(st)
                        if 0 <= st - 1 < H:
                            qB_dve(st - 1)
                        if 0 <= st - 3 < H:
                            qD2(st - 3)
                    stage(7)
                    gen_kv(ckvT, krope, kropeb, TT, key0, sspe)
                    stage(8)
                    wD, wDb = wp.load(w_in[l][:, OFF_POOL:OFF_POOL + 512], 16, 512)
                    for j in range(4):
                        ps, psb = pp.get()
                        zgroup(wD, wDb, j * 128, 128, ps, psb)
                        p2 = j % 2
                        act(lambda ps=ps, p2=p2: nc.scalar.copy(out=ptmp[p2][:], in_=ps[:, :]), [psb], [ptmpb[p2]])
                        spdma(POOLsc[j * 128:(j + 1) * 128, tok0:tok0 + TT], ptmp[p2][:], [ptmpb[p2]], [])
                    stage(9)
                    wE, wEb = wp.load(w_in[l][:, OFF_SGU:OFF_SGU + 512], 16, 512)
                    for j in range(4):
                        ps, psb = pp.get()
                        zgroup(wE, wEb, j * 128, 128, ps, psb)
                        act(lambda ps=ps, j=j: nc.scalar.activation(out=u_t[:, j, :], in_=ps[:, :], func=AF.Gelu_apprx_tanh), [psb], [ub])
                    wF, wFb = wp.load(w_in[l][:, OFF_SGU + 512:OFF_SGU + 1024], 16, 512)
                    for s in range(4):
                        ps, psb = pp.get()
                        for kc in range(16):
                            mm(ps[:, :], hT[:, kc, s * 128:(s + 1) * 128], wF[:, kc, :], kc == 0, kc == 15, [wFb, hb], [psb])
                        act(lambda ps=ps: nc.scalar.activation(out=vg[:], in_=ps[:, :], func=AF.Gelu_apprx_tanh), [psb], [vgb])
                        dve(lambda: nc.vector.tensor_tensor(out=tmpf[0][:], in0=vg[:], in1=vg[:], op=ALU.mult), [vgb], [tmpb[0]])
                        dve(lambda: nc.vector.reduce_sum(out=ss1[:], in_=tmpf[0][:], axis=AX.X), [tmpb[0]], [ss1b])
                        act(lambda: nc.scalar.activation(out=ss1[:], in_=ss1[:], func=AF.Sqrt, scale=1.0 / 512, bias=EPS), [ss1b], [ss1b])
                        dve(lambda: nc.vector.reciprocal(out=ss1[:], in_=ss1[:]), [ss1b], [ss1b])
                        dve(lambda: nc.vector.tensor_scalar(out=vh[:], in0=vg[:], scalar1=ss1[:, 0:1], scalar2=None, op0=ALU.mult), [vgb, ss1b], [vhb])
                        pm, pmb = pp.get()
                        for gg in range(4):
                            mm(pm[:, gg * 128:(gg + 1) * 128], vh[:, gg * 128:(gg + 1) * 128], wspT[:, gg, :], True, True, [vhb, wres], [pmb])
                        for gg in range(4):
                            dve(lambda pm=pm, gg=gg: nc.vector.scalar_tensor_tensor(out=mx[:], in0=pm[:, gg * 128:(gg + 1) * 128], scalar=sguw[:, gg:gg + 1], in1=bspb[:, gg * 128:(gg + 1) * 128], op0=ALU.mult, op1=ALU.add), [pmb, wres], [mxb])
                            dve(lambda gg=gg, s=s: nc.vector.tensor_tensor(out=sgo[:, gg, s * 128:(s + 1) * 128], in0=mx[:], in1=u_t[:, gg, s * 128:(s + 1) * 128], op=ALU.mult), [mxb, ub], [sgob])
                    spdma(SGUsc[:, tok0:tok0 + TT].rearrange("(g p) t -> p g t", p=128), sgo[:], [sgob], [])
                    stage(10)
                    wG, wGb = wp.load(w_in[l][:, OFF_CONV:OFF_CONV + 512], 16, 512)
                    for j in range(4):
                        ps, psb = pp.get()
                        zgroup(wG, wGb, j * 128, 128, ps, psb)
                        p2 = j % 2
                        act(lambda ps=ps, p2=p2: nc.scalar.copy(out=ptmp[p2][:], in_=ps[:, :]), [psb], [ptmpb[p2]])
                        spdma(CONVb[j * 128:(j + 1) * 128, tok0:tok0 + TT], ptmp[p2][:], [ptmpb[p2]], [])
                    wH, wHb = wp.load(w_in[l][:, OFF_CONV + 512:OFF_CONV + 1024], 16, 512)
                    for j in range(4):
                        ps, psb = pp.get()
                        zgroup(wH, wHb, j * 128, 128, ps, psb)
                        act(lambda ps=ps, j=j: nc.scalar.copy(out=cg[:, j, :], in_=ps[:, :]), [psb], [cgb])
                    wI, wIb = wp.load(w_in[l][:, OFF_CONV + 1024:OFF_CONV + 1536], 16, 512)
                    for j in range(4):
                        ps, psb = pp.get()
                        zgroup(wI, wIb, j * 128, 128, ps, psb)
                        p2 = j % 2
                        dve(lambda ps=ps, p2=p2, j=j: nc.vector.tensor_tensor(out=ptmp[p2][:], in0=ps[:, :], in1=cg[:, j, :], op=ALU.mult), [psb, cgb], [ptmpb[p2]])
                        spdma(CONVy[j * 128:(j + 1) * 128, tok0:tok0 + TT], ptmp[p2][:], [ptmpb[p2]], [])
                    stage(11)
                except _Stop:
                  pass
                S.flush()

        brKk = [8, 4, 4, 4]

        def phase2(l):
            oc = 0
            with ExitStack() as es:
                ppS = PsumPool(es, 5)
                ppO = PsumPool(es, 2)
                ppD = PsumPool(es, 1)
                Vall = T(es, "Vall", [128, NKB, H * 65], BF16); Vb = Buf()
                SCall = T(es, "SCall", [128, NKB, H], F32); SCb = Buf()
                Kt = [T(es, "Kt%d" % i, [96, NKEY], BF16) for i in range(2)]; Ktb = [Buf(), Buf()]
                Qt = [T(es, "Qt%d" % i, [96, NTOK], BF16) for i in range(2)]; Qtb = [Buf(), Buf()]
                pT = [T(es, "pT%d" % i, [128, 512], BF16) for i in range(4)]; pTb = [Buf() for _ in range(4)]
                osb = [T(es, "osb%d" % i, [65, 512], F32) for i in range(2)]; osbb = [Buf(), Buf()]
                rden = T(es, "rden", [64, 512], F32); rdenb = Buf()
                on = [T(es, "on%d" % i, [64, 512], BF16) for i in range(2)]; onb = [Buf(), Buf()]
                for c in range(2):
                    spdma(Vall[:, c * 19:(c + 1) * 19, :], Vsc[c * 19:(c + 1) * 19].rearrange("kb p c -> p kb c"), [], [Vb])
                spdma(SCall[:], SCsc.rearrange("kb p h -> p kb h"), [], [SCb])
                spdma(Kt[0][:], Ksc[:, 0, :], [], [Ktb[0]])
                spdma(Qt[0][:], Qsc[:, 0, :], [], [Qtb[0]])
                for r0 in range(0, D, 128):
                    pldma(WG[r0:r0 + 128, :], w_in[l][r0:r0 + 128, OFF_GATE:OFF_GATE + 4 * D], [Vb, SCb, Ktb[0], Qtb[0]] if r0 == 0 else [], [])
                for br in range(4):
                    for r0 in range(0, brKk[br] * 128, 128):
                        pldma(WBR[BR_OFF[br] + r0:BR_OFF[br] + r0 + 128, :], w_br[br][l][r0:r0 + 128, :], [], [])
                for r0 in range(0, D, 128):
                    pldma(WO[r0:r0 + 128, :], w_out[l][r0:r0 + 128, :], [], [])
                for r0 in range(0, D, 128):
                    pldma(WFI[r0:r0 + 128, :], w_ffn_in[l][r0:r0 + 128, :], [], [])
                for r0 in range(0, DFF, 128):
                    pldma(WFO[r0:r0 + 128, :], w_ffn_out[l][r0:r0 + 128, :], [], [])
                jobs = [(qt * 512, 512, list(range(34))) for qt in range(8)]
                jobs += [(4096 + 256 * s, 256, [34 + 2 * s, 35 + 2 * s]) for s in range(2)]
                pc = 0
                LA = 4
                pending = [None]

                def run_pending():
                    if pending[0] is not None:
                        pending[0]()
                        pending[0] = None
                for h in range(H):
                    hb2 = h % 2
                    for ji, (q0, nq, kbs) in enumerate(jobs):
                        if ji == 1 and h + 1 < H:
                            spdma(Kt[1 - hb2][:], Ksc[:, h + 1, :], [], [Ktb[1 - hb2]])
                            spdma(Qt[1 - hb2][:], Qsc[:, h + 1, :], [], [Qtb[1 - hb2]])
                        po, pob = ppO.get()
                        sps = {}

                        def smm(idx, kbs=kbs, q0=q0, nq=nq, hb2=hb2, sps=sps):
                            ps, psb = ppS.get()
                            kb = kbs[idx]
                            mm(ps[:, 0:nq], Kt[hb2][:, kb * 128:(kb + 1) * 128], Qt[hb2][:, q0:q0 + nq], True, True, [Ktb[hb2], Qtb[hb2]], [psb])
                            sps[idx] = (ps, psb)
                        for k0 in range(min(LA, len(kbs))):
                            smm(k0)
                        run_pending()
                        for idx, kb in enumerate(kbs):
                            ps, psb = sps.pop(idx)
                            r = pc % 4; pc += 1
                            act(lambda ps=ps, r=r, kb=kb, h=h, nq=nq: nc.scalar.activation(out=pT[r][:, 0:nq], in_=ps[:, 0:nq], func=AF.Exp, scale=SCall[:, kb, h:h + 1]), [psb, SCb], [pTb[r]])
                            mm(po[0:65, 0:nq], Vall[:, kb, h * 65:(h + 1) * 65], pT[r][:, 0:nq], idx == 0, idx == len(kbs) - 1, [Vb, pTb[r]], [pob])
                            if idx + LA < len(kbs):
                                smm(idx + LA)

                        def norm(po=po, pob=pob, nq=nq, q0=q0, h=h):
                            nonlocal oc
                            o2 = oc % 2; oc += 1
                            dve(lambda: nc.vector.tensor_copy(out=osb[o2][:, 0:nq], in_=po[0:65, 0:nq]), [pob], [osbb[o2]])
                            pd, pdb = ppD.get()
                            mm(pd[0:64, 0:nq], sel_f[:, :], osb[o2][:, 0:nq], True, True, [osbb[o2], constb], [pdb])
                            dve(lambda: nc.vector.reciprocal(out=rden[:, 0:nq], in_=pd[0:64, 0:nq]), [pdb], [rdenb])
                            dve(lambda: nc.vector.tensor_tensor(out=on[o2][:, 0:nq], in0=osb[o2][0:64, 0:nq], in1=rden[:, 0:nq], op=ALU.mult), [osbb[o2], rdenb], [onb[o2]])
                            spdma(ATTNsc[h * 64:(h + 1) * 64, q0:q0 + nq], on[o2][:, 0:nq], [onb[o2]], [])
                        pending[0] = norm
                run_pending()
                S.flush()

        def phase3(l):
            XIN = xT_in if l == 0 else XT1
            XOUT = XT1 if l == 0 else yT
            with ExitStack() as es:
                pp = PsumPool(es, 8)
                wp = WPool(es, 3)
                xT = T(es, "xT3", [128, 16, 512], F32); xb = Buf()
                hT = T(es, "hT3", [128, 16, 512], BF16); hb = Buf()
                sq = [T(es, "sq3%d" % i, [128, 512], BF16) for i in range(2)]; sqb = [Buf(), Buf()]
                rs = T(es, "rs3", [128, 512], F32); rsb = Buf()
                tmpf = [T(es, "tmpf3%d" % i, [128, 512], F32) for i in range(2)]; tmpb = [Buf(), Buf()]
                R = T(es, "R3", [128, 22800], BF16)
                actT = R[:, 0:22528].rearrange("p (k t) -> p k t", t=512); actb = Buf()
                Rf = R[:, :].bitcast(F32)
                zp = Rf[:, 0:2112].rearrange("p (g t) -> p g t", t=528); zpb = Buf()
                pa = [Rf[:, 2112 + k * 528:2112 + (k + 1) * 528] for k in range(4)]; pab = Buf()
                cy = Rf[:, 4224:6280].rearrange("p (g t) -> p g t", t=514); cyb = Buf()
                cb_ = Rf[:, 6280:8328].rearrange("p (g t) -> p g t", t=512); cbb = Buf()
                ctmp = Rf[:, 8328:8840]; ctb = Buf()
                ptm = Rf[:, 8840:9352]; ptmb = Buf()
                invt = Rf[:, 9352:11400].rearrange("p (g t) -> p g t", t=512); invb = Buf()
                brT_attn = T(es, "brA", [128, 8, 512], BF16)
                brT_pool = T(es, "brP", [128, 4, 512], BF16)
                brT_sgu = T(es, "brS", [128, 4, 512], BF16)
                brT_conv = T(es, "brC", [128, 4, 512], BF16)
                brb = [Buf(), Buf(), Buf(), Buf()]
                pl = T(es, "pl", [128, 512], BF16); plb = Buf()
                mT = T(es, "mT", [128, 16, 512], BF16); mb_ = Buf()
                macc = T(es, "macc", [128, 512], F32); maccb = Buf()
                maccs = [T(es, "maccs%d" % i, [128, 512], F32) for i in range(4)]; maccsb = [Buf() for _ in range(4)]
                gt = [T(es, "gt%d" % i, [128, 512], F32) for i in range(2)]; gtb = [Buf(), Buf()]
                wpl = T(es, "wpl", [128, 4, 128], BF16)
                psc = T(es, "psc", [128, 4], F32); cw = T(es, "cw", [128, 12], F32); bg = T(es, "bg", [128, 64], F32)
                wres = Buf()
                pldma(wpl[:], w_pool[l].rearrange("g i o -> i g o"), [], [wres])
                spdma(psc[:], pscaleT[l], [], [wres]); spdma(cw[:], convwT[l], [], [wres]); spdma(bg[:], b_gateT[l], [], [wres])
                brT = [brT_attn, brT_pool, brT_sgu, brT_conv]
                brK = [8, 4, 4, 4]
                gc = 0
                for i in range(NT):
                    g = 0 if i < 8 else 1
                    tok0 = i * TT
                    a1 = mvec(l, g, 0); b1 = mvec(l, g, 1); ga = mvec(l, g, 2)
                    a2 = mvec(l, g, 3); b2 = mvec(l, g, 4); gf = mvec(l, g, 5)
                    spdma(xT[:], XIN[:, tok0:tok0 + TT].rearrange("(kc p) t -> p kc t", p=128), [], [xb])
                    spdma(brT_attn[:], ATTNsc[:, tok0:tok0 + TT].rearrange("(kc p) t -> p kc t", p=128), [], [brb[0]])
                    spdma(brT_sgu[:], SGUsc[:, tok0:tok0 + TT].rearrange("(kc p) t -> p kc t", p=128), [], [brb[2]])
                    norm_mod(pp, xT, xb, hT, hb, a1, b1, sq, sqb, rs, rsb, tmpf, tmpb)
                    segs = [(tok0, 512, i > 0, i < 7)] if i < 8 else [(tok0, 256, False, False), (tok0 + 256, 256, False, False)]
                    kind = (0 if i == 0 else (2 if i == 7 else 1)) if i < 8 else 3
                    spdma(invt, inv_in[kind], [], [invb, actb])
                    for (c0, W, lh, rh) in segs:
                        so = c0 - tok0
                        dve(lambda: nc.vector.memset(zp, 0.0), [], [zpb, actb])
                        dve(lambda: nc.vector.memset(cy, 0.0), [], [cyb, actb])
                        lo = c0 - (8 if lh else 0); hi = c0 + W + (8 if rh else 0)
                        spdma(zp[:, :, 8 - (c0 - lo):8 + (hi - c0)], POOLsc[:, lo:hi].rearrange("(g p) t -> p g t", p=128), [], [zpb])
                        lo = c0 - (1 if lh else 0); hi = c0 + W + (1 if rh else 0)
                        spdma(cy[:, :, 1 - (c0 - lo):1 + (hi - c0)], CONVy[:, lo:hi].rearrange("(g p) t -> p g t", p=128), [], [cyb])
                        spdma(cb_[:, :, 0:W], CONVb[:, c0:c0 + W].rearrange("(g p) t -> p g t", p=128), [], [cbb, actb])
                        n = W + 16
                        for gg in range(4):
                            X = zp[:, gg, :]
                            dve(lambda X=X, n=n: nc.vector.tensor_tensor(out=pa[0][:, 1:n], in0=X[:, 1:n], in1=X[:, 0:n - 1], op=ALU.add), [zpb], [pab, actb])
                            E = pa[0][:, 8:8 + W]
                            if gg >= 1:
                                dve(lambda n=n: nc.vector.tensor_tensor(out=pa[1][:, 3:n], in0=pa[0][:, 3:n], in1=pa[0][:, 1:n - 2], op=ALU.add), [pab], [pab])
                                E = pa[1][:, 9:9 + W]
                            if gg >= 2:
                                dve(lambda n=n: nc.vector.tensor_tensor(out=pa[2][:, 7:n], in0=pa[1][:, 7:n], in1=pa[1][:, 3:n - 4], op=ALU.add), [pab], [pab])
                                E = pa[2][:, 11:11 + W]
                            if gg >= 3:
                                dve(lambda n=n: nc.vector.tensor_tensor(out=pa[3][:, 15:n], in0=pa[2][:, 15:n], in1=pa[2][:, 7:n - 8], op=ALU.add), [pab], [pab])
                                E = pa[3][:, 15:15 + W]
                            dve(lambda E=E, gg=gg, W=W, so=so: nc.vector.tensor_tensor(out=ptm[:, 0:W], in0=E, in1=invt[:, gg, so:so + W], op=ALU.mult), [pab, invb], [ptmb, actb])
                            dve(lambda X=X, W=W: nc.vector.tensor_tensor(out=pl[:, 0:W], in0=ptm[:, 0:W], in1=X[:, 8:8 + W], op=ALU.subtract), [ptmb, zpb], [plb])
                            ps, psb = pp.get()
                            mm(ps[:, 0:W], wpl[:, gg, :], pl[:, 0:W], True, True, [plb, wres], [psb])
                            act(lambda ps=ps, gg=gg, W=W, so=so: nc.scalar.activation(out=brT_pool[:, gg, so:so + W], in_=ps[:, 0:W], func=AF.Copy, scale=psc[:, gg:gg + 1]), [psb, wres], [brb[1]])
                        for j in range(4):
                            Y = cy[:, j, :]
                            dve(lambda Y=Y, j=j, W=W: nc.vector.tensor_scalar(out=ctmp[:, 0:W], in0=Y[:, 0:W], scalar1=cw[:, j:j + 1], scalar2=None, op0=ALU.mult), [cyb, wres], [ctb, actb])
                            dve(lambda Y=Y, j=j, W=W: nc.vector.scalar_tensor_tensor(out=ctmp[:, 0:W], in0=Y[:, 1:1 + W], scalar=cw[:, 4 + j:5 + j], in1=ctmp[:, 0:W], op0=ALU.mult, op1=ALU.add), [cyb, ctb, wres], [ctb])
                            dve(lambda Y=Y, j=j, W=W: nc.vector.scalar_tensor_tensor(out=ctmp[:, 0:W], in0=Y[:, 2:2 + W], scalar=cw[:, 8 + j:9 + j], in1=ctmp[:, 0:W], op0=ALU.mult, op1=ALU.add), [cyb, ctb, wres], [ctb])
                            dve(lambda j=j, W=W, so=so: nc.vector.tensor_tensor(out=brT_conv[:, j, so:so + W], in0=ctmp[:, 0:W], in1=cb_[:, j, 0:W], op=ALU.mult), [ctb, cbb], [brb[3]])
                    for jb in range(4):
                        for br in range(4):
                            c0 = br * D + jb * 512
                            wg, wgb = wp.load(WG[:, c0:c0 + 512], 16, 512)
                            wb_, wbb = wp.load(WBR[BR_OFF[br]:BR_OFF[br] + brK[br] * 128, jb * 512:(jb + 1) * 512], brK[br], 512)
                            for jj in range(4):
                                j = jb * 4 + jj
                                pg, pgb = pp.get()
                                for kc in range(16):
                                    mm(pg[:, :], wg[:, kc, jj * 128:(jj + 1) * 128], hT[:, kc, :], kc == 0, kc == 15, [wgb, hb], [pgb])
                                pb, pbb = pp.get()
                                for kc in range(brK[br]):
                                    mm(pb[:, :], wb_[:, kc, jj * 128:(jj + 1) * 128], brT[br][:, kc, :], kc == 0, kc == brK[br] - 1, [wbb, brb[br]], [pbb])
                                g2 = gc % 2; gc += 1
                                act(lambda pg=pg, g2=g2, br=br, j=j: nc.scalar.activation(out=gt[g2][:], in_=pg[:, :], func=AF.Sigmoid, bias=bg[:, br * 16 + j:br * 16 + j + 1], scale=1.0), [pgb, wres], [gtb[g2]])
                                if br == 0:
                                    dve(lambda pb=pb, g2=g2, jj=jj: nc.vector.tensor_tensor(out=maccs[jj][:], in0=gt[g2][:], in1=pb[:, :], op=ALU.mult), [gtb[g2], pbb], [maccsb[jj]])
                                else:
                                    dve(lambda pb=pb, g2=g2: nc.vector.tensor_tensor(out=macc[:], in0=gt[g2][:], in1=pb[:, :], op=ALU.mult), [gtb[g2], pbb], [maccb])
                                    if br < 3:
                                        dve(lambda jj=jj: nc.vector.tensor_tensor(out=maccs[jj][:], in0=maccs[jj][:], in1=macc[:], op=ALU.add), [maccb, maccsb[jj]], [maccsb[jj]])
                                    else:
                                        dve(lambda jj=jj, j=j: nc.vector.tensor_tensor(out=mT[:, j, :], in0=maccs[jj][:], in1=macc[:], op=ALU.add), [maccb, maccsb[jj]], [mb_])
                    for jb in range(4):
                        wo, wob = wp.load(WO[:, jb * 512:(jb + 1) * 512], 16, 512)
                        for jj in range(4):
                            j = jb * 4 + jj
                            ps, psb = pp.get()
                            for kc in range(16):
                                mm(ps[:, :], wo[:, kc, jj * 128:(jj + 1) * 128], mT[:, kc, :], kc == 0, kc == 15, [wob, mb_], [psb])
                            dve(lambda ps=ps, j=j, ga=ga: nc.vector.scalar_tensor_tensor(out=xT[:, j, :], in0=ps[:, :], scalar=ga[:, j:j + 1], in1=xT[:, j, :], op0=ALU.mult, op1=ALU.add), [psb, xb, MVb], [xb])
                    norm_mod(pp, xT, xb, hT, hb, a2, b2, sq, sqb, rs, rsb, tmpf, tmpb)
                    for jb in range(11):
                        wa, wab = wp.load(WFI[:, jb * 512:(jb + 1) * 512], 16, 512)
                        wl, wlb = wp.load(WFI[:, DFF + jb * 512:DFF + (jb + 1) * 512], 16, 512)
                        for jj in range(4):
                            j = jb * 4 + jj
                            pa_, pab_ = pp.get()
                            for kc in range(16):
                                mm(pa_[:, :], wa[:, kc, jj * 128:(jj + 1) * 128], hT[:, kc, :], kc == 0, kc == 15, [wab, hb], [pab_])
                            pl_, plb_ = pp.get()
                            for kc in range(16):
                                mm(pl_[:, :], wl[:, kc, jj * 128:(jj + 1) * 128], hT[:, kc, :], kc == 0, kc == 15, [wlb, hb], [plb_])
                            g2 = gc % 2; gc += 1
                            act(lambda pa_=pa_, g2=g2: nc.scalar.activation(out=gt[g2][:], in_=pa_[:, :], func=AF.Silu), [pab_], [gtb[g2]])
                            dve(lambda pl_=pl_, g2=g2, j=j: nc.vector.tensor_tensor(out=actT[:, j, :], in0=gt[g2][:], in1=pl_[:, :], op=ALU.mult), [gtb[g2], plb_], [actb, zpb, pab, cyb, cbb, ctb, ptmb, invb])
                    for jb in range(4):
                        accs = [pp.get() for _ in range(4)]
                        for kb0, nk in ((0, 16), (16, 16), (32, 12)):
                            wf, wfb = wp.load(WFO[kb0 * 128:(kb0 + nk) * 128, jb * 512:(jb + 1) * 512], nk, 512)
                            for jj in range(4):
                                ps, psb = accs[jj]
                                for kc in range(nk):
                                    kg = kb0 + kc
                                    mm(ps[:, :], wf[:, kc, jj * 128:(jj + 1) * 128], actT[:, kg, :], kg == 0, kg == 43, [wfb, actb], [psb])
                        for jj in range(4):
                            j = jb * 4 + jj
                            ps, psb = accs[jj]
                            dve(lambda ps=ps, j=j, gf=gf: nc.vector.scalar_tensor_tensor(out=xT[:, j, :], in0=ps[:, :], scalar=gf[:, j:j + 1], in1=xT[:, j, :], op0=ALU.mult, op1=ALU.add), [psb, xb, MVb], [xb])
                    spdma(XOUT[:, tok0:tok0 + TT].rearrange("(kc p) t -> p kc t", p=128), xT[:], [xb], [])
                S.flush()

        S.flush()
        for l in range(nlayers):
            phase_mod(l)
        if dbg and "dbg_MV" in dbg:
            dmv = nc.dram_tensor("dbg_MV", [128, L * 12, 16], F32, kind="ExternalOutput").ap()
            spdma(dmv, MV[:], [MVb], [])
            S.flush()
        if stop_after == "p0":
            nlayers = 0
        for l in range(nlayers):
            phase1(l)
            if stop_after == ("p1", l):
                break
            phase2(l)
            if stop_after == ("p2", l):
                break
            phase3(l)
            if stop_after == ("p3", l):
                break
    return nc


def _rope_tables():
    T_ = NS
    rows = T_ // 64
    row = np.repeat(np.arange(rows), 64).astype(np.float32)
    col = np.tile(np.arange(64), rows).astype(np.float32)
    inv = (10000.0 ** (-np.arange(8, dtype=np.float32) / 8)).astype(np.float32)
    ang_r = row[:, None] * inv
    ang_c = col[:, None] * inv
    ang = np.stack([ang_r, ang_r, ang_c, ang_c], axis=1).reshape(T_, 32)
    cos = np.ones((96, NTOK), np.float32)
    sin = np.zeros((96, NTOK), np.float32)
    cos[64:, :NS] = np.cos(ang).T
    sin[64:, :NS] = np.sin(ang).T
    return cos, sin


def _consts():
    cos, sin = _rope_tables()
    prot = np.zeros((96, 96), np.float32)
    for a in range(2):
        for f in range(8):
            i0 = 64 + a * 16 + f
            i1 = 64 + a * 16 + 8 + f
            prot[i1, i0] = -1.0
            prot[i0, i1] = 1.0
    sel = np.zeros((65, 64), np.float32)
    sel[64, :] = 1.0

    def inv_cnt(Tseq):
        t = np.arange(Tseq)
        out = np.zeros((4, Tseq), np.float32)
        for gi, w in enumerate((2, 4, 8, 16)):
            lo = np.clip(t - w // 2, 0, Tseq - 1)
            hi = np.clip(t + w // 2 - 1, 0, Tseq - 1)
            out[gi] = 1.0 / (hi - lo + 1).astype(np.float32)
        return out
    ic = inv_cnt(NS)
    ip = inv_cnt(256)
    tab = np.zeros((4, 4, 512), np.float32)
    tab[0] = ic[:, 0:512]
    tab[1] = ic[:, 512:1024]
    tab[2] = ic[:, NS - 512:NS]
    tab[3] = np.concatenate([ip, ip], axis=1)
    invtab = np.ascontiguousarray(np.broadcast_to(tab[:, None], (4, 128, 4, 512)))
    return cos, sin, prot, sel, invtab


def _vecT(v, n):
    return np.ascontiguousarray(v.reshape(v.shape[0], n, 128).transpose(0, 2, 1))


def make_in_maps(inp):
    f = lambda a: np.ascontiguousarray(np.asarray(a, dtype=np.float32))
    cos, sin, prot, sel, invtab = _consts()
    shared = {
        "w_mod": f(inp["w_mod"]), "b_modT": _vecT(f(inp["b_mod"]), 96),
        "nmixT": _vecT(f(inp["norm_mix_w"]), 16), "nffnT": _vecT(f(inp["norm_ffn_w"]), 16),
        "w_in": f(inp["w_in"]), "b_gateT": _vecT(f(inp["b_gate"]), 64),
        "qawT": _vecT(f(inp["q_a_norm_w"]), 6), "kvawT": _vecT(f(inp["kv_a_norm_w"]), 2),
        "kvaw_row": f(inp["kv_a_norm_w"]),
        "w_uq": f(inp["w_uq"]), "w_ukv": f(inp["w_ukv"]),
        "qnw": f(inp["q_norm_w"]).reshape(L, 96, 1), "knw": f(inp["k_norm_w"]).reshape(L, 96, 1),
        "w_pool": f(inp["w_pool"]), "pscaleT": _vecT(f(inp["pool_scale"]), 4),
        "sguwT": _vecT(f(inp["sgu_norm_w"]), 4),
        "w_spT": np.ascontiguousarray(f(inp["w_spatial"]).transpose(0, 1, 3, 2)),
        "b_sp": f(inp["b_spatial"]).reshape(L, 512),
        "convwT": np.ascontiguousarray(f(inp["conv_w"]).reshape(L, 3, 4, 128).transpose(0, 3, 1, 2).reshape(L, 128, 12)),
        "w_br_attn": f(inp["w_br_attn"]), "w_br_pool": f(inp["w_br_pool"]),
        "w_br_sgu": f(inp["w_br_sgu"]), "w_br_conv": f(inp["w_br_conv"]),
        "w_out": f(inp["w_out"]), "w_ffn_in": f(inp["w_ffn_in"]), "w_ffn_out": f(inp["w_ffn_out"]),
        "cosT": cos, "sinT": sin, "prot": prot, "sel": sel, "invtab": invtab,
    }
    xs = f(inp["x_sample"]); xp = f(inp["x_prompt"])
    cckv = f(inp["cache_ckv"]); ckpe = f(inp["cache_kpe"])
    c = f(inp["c"]); cctx = f(inp["c_ctx"])
    maps = []
    for b in range(8):
        xT = np.concatenate([xs[b].T, xp[2 * b].T, xp[2 * b + 1].T], axis=1)
        cv = np.stack([c[b], cctx], axis=1)
        cT = cv.reshape(16, 128, 2).transpose(1, 0, 2)
        m = dict(shared)
        m["xT"] = np.ascontiguousarray(xT)
        m["cT"] = np.ascontiguousarray(cT)
        m["cckvT"] = np.ascontiguousarray(cckv[b].transpose(0, 2, 1))
        m["ckpe"] = np.ascontiguousarray(ckpe[b])
        m["ckpeT"] = np.ascontiguousarray(ckpe[b].transpose(0, 2, 1))
        maps.append(m)
    return maps


def kernel(**inputs):
    nc = build()
    maps = make_in_maps(inputs)
    res = run_bass_kernel_spmd(nc, maps, core_ids=list(range(8)))
    y_prompt = np.zeros((16, 256, D), np.float32)
    y_sample = np.zeros((8, NS, D), np.float32)
    s_ckv = np.zeros((16, L, 256, 256), np.float32)
    s_kpe = np.zeros((16, L, 256, 32), np.float32)
    for b in range(8):
        r = res.results[b]
        yT = np.asarray(r["yT"])
        y_sample[b] = yT[:, :NS].T
        y_prompt[2 * b] = yT[:, NS:NS + 256].T
        y_prompt[2 * b + 1] = yT[:, NS + 256:].T
        s_ckv[2 * b:2 * b + 2] = np.asarray(r["st_ckv"])
        s_kpe[2 * b:2 * b + 2] = np.asarray(r["st_kpe"])
    return (y_prompt, y_sample, s_ckv, s_kpe)
```

```python
import numpy as np
from contextlib import ExitStack
import concourse.bass as bass
import concourse.mybir as mybir
from concourse.bass_utils import run_bass_kernel_spmd

F32, BF16 = mybir.dt.float32, mybir.dt.bfloat16
AF = mybir.ActivationFunctionType
ALU = mybir.AluOpType
AX = mybir.AxisListType

D = 2048; L = 2; NT = 9; TT = 512; NTOK = 4608; NS = 4096
NIN = 12320; DFF = 5632; H = 16
NKEY = 4864; NKB = 38
EPS = 1e-6
ATTN_SCALE = 96 ** -0.5
OFF_POOL, OFF_SGU, OFF_CONV, OFF_GATE = 1056, 1568, 2592, 4128
NDS = 8
SAME_ENGINE_SYNC = True


class Buf:
    __slots__ = ("lw", "rd", "excl")

    def __init__(self, excl=False):
        self.lw = None
        self.rd = {}
        self.excl = excl


class Op:
    __slots__ = ("eng", "fn", "deps", "inc", "dma", "sem", "val", "epoch")


class Eng:
    pass


class _Stop(Exception):
    pass


class Sched:
    def __init__(self, nc, es):
        self.nc = nc
        self.E = {}
        for name, obj in (("pe", nc.tensor), ("act", nc.scalar), ("dve", nc.vector),
                          ("pool", nc.gpsimd), ("sp", nc.sync)):
            e = Eng()
            e.name = name; e.obj = obj
            e.sem = es.enter_context(nc.semaphore("s_" + name))
            e.count = 0; e.known = {}
            e.dsems = [es.enter_context(nc.semaphore("d_%s%d" % (name, i))) for i in range(NDS)] \
                if name in ("pool", "sp") else []
            e.duse = [0] * NDS; e.dcount = 0
            self.E[name] = e
        self.ops = []
        self.epoch = 0
        self.nid = 0

    def op(self, eng, fn, reads=(), writes=(), dma=False):
        E = self.E[eng]
        X = Op()
        X.eng = E; X.fn = fn; X.inc = False; X.dma = dma; X.sem = None; X.val = 0; X.epoch = self.epoch
        deps = []
        xr = [b for b in reads if b.excl]
        if xr:
            reads = [b for b in reads if not b.excl]
            writes = list(writes) + xr
        for b in reads:
            if b.lw is not None:
                deps.append(b.lw)
        for b in writes:
            if b.lw is not None:
                deps.append(b.lw)
            deps.extend(b.rd.values())
        out = []
        seen = set()
        for P in deps:
            if P.epoch != self.epoch or id(P) in seen or P is X:
                continue
            seen.add(id(P))
            if (not P.dma) and (not dma) and P.eng is E:
                if eng != "dve" or not SAME_ENGINE_SYNC:
                    continue
            if not P.dma:
                P.inc = True
            out.append(P)
        X.deps = out
        if dma:
            self.nid += 1
            key = ("d", self.nid)
        else:
            key = eng
        for b in reads:
            b.rd[key] = X
        for b in writes:
            b.lw = X
            b.rd = {}
        self.ops.append(X)
        return X

    def flush(self):
        last = {}
        for X in self.ops:
            if not X.dma:
                last[X.eng.name] = X
        for X in last.values():
            X.inc = True
        for X in self.ops:
            E = X.eng
            for P in X.deps:
                if E.known.get(P.sem, 0) < P.val:
                    E.obj.wait_ge(P.sem, P.val)
                    E.known[P.sem] = P.val
            if X.dma:
                slot = E.dcount % NDS
                E.dcount += 1
                sem = E.dsems[slot]
                E.duse[slot] += 1
                val = 16 * E.duse[slot]
                if val > 16 and E.known.get(sem, 0) < val - 16:
                    E.obj.wait_ge(sem, val - 16)
                    E.known[sem] = val - 16
                X.fn().then_inc(sem, 16)
                X.sem = sem; X.val = val
            else:
                ins = X.fn()
                if X.inc:
                    E.count += 1
                    ins.then_inc(E.sem, 1)
                    X.sem = E.sem; X.val = E.count
        self.ops = []
        for E in self.E.values():
            for P in self.E.values():
                if P is not E and P.count > 0 and E.known.get(P.sem, 0) < P.count:
                    E.obj.wait_ge(P.sem, P.count)
                    E.known[P.sem] = P.count
                for s in range(NDS if P.dsems else 0):
                    v = 16 * P.duse[s]
                    if v > 0 and E.known.get(P.dsems[s], 0) < v:
                        E.obj.wait_ge(P.dsems[s], v)
                        E.known[P.dsems[s]] = v
        self.epoch += 1


def build(dbg=None):
    nc = bass.Bass("TRN2", target_bir_lowering=False)

    def din(name, shape):
        return nc.dram_tensor(name, list(shape), F32, kind="ExternalInput").ap()

    def dout(name, shape):
        return nc.dram_tensor(name, list(shape), F32, kind="ExternalOutput").ap()

    def dscr(name, shape, dt):
        kind = "ExternalOutput" if (dbg and name in dbg) else "Internal"
        return nc.dram_tensor(name, list(shape), dt, kind=kind).ap()

    xT_in = din("xT", [D, NTOK])
    cT_in = din("cT", [128, 16, 2])
    cckvT_in = din("cckvT", [L, 256, 256])
    ckpe_in = din("ckpe", [L, 256, 32])
    ckpeT_in = din("ckpeT", [L, 32, 256])
    w_mod = din("w_mod", [L, D, 6 * D])
    b_modT = din("b_modT", [L, 128, 96])
    nmixT = din("nmixT", [L, 128, 16])
    nffnT = din("nffnT", [L, 128, 16])
    w_in = din("w_in", [L, D, NIN])
    b_gateT = din("b_gateT", [L, 128, 64])
    qawT = din("qawT", [L, 128, 6])
    kvawT = din("kvawT", [L, 128, 2])
    kvaw_row = din("kvaw_row", [L, 256])
    w_uq = din("w_uq", [L, 768, 1536])
    w_ukv = din("w_ukv", [L, 256, 2048])
    qnw = din("qnw", [L, 96, 1])
    knw = din("knw", [L, 96, 1])
    w_pool = din("w_pool", [L, 4, 128, 128])
    pscaleT = din("pscaleT", [L, 128, 4])
    sguwT = din("sguwT", [L, 128, 4])
    w_spT = din("w_spT", [L, 4, 128, 128])
    b_sp = din("b_sp", [L, 512])
    convwT = din("convwT", [L, 128, 12])
    w_br = [din("w_br_attn", [L, 1024, D]), din("w_br_pool", [L, 512, D]),
            din("w_br_sgu", [L, 512, D]), din("w_br_conv", [L, 512, D])]
    w_out = din("w_out", [L, D, D])
    w_ffn_in = din("w_ffn_in", [L, D, 2 * DFF])
    w_ffn_out = din("w_ffn_out", [L, DFF, D])
    cos_in = din("cosT", [96, NTOK])
    sin_in = din("sinT", [96, NTOK])
    prot_in = din("prot", [96, 96])
    sel_in = din("sel", [65, 64])
    inv_in = din("invtab", [4, 128, 4, 512])

    yT = dout("yT", [D, NTOK])
    st_ckv = dout("st_ckv", [2, L, 256, 256])
    st_kpe = dout("st_kpe", [2, L, 256, 32])

    XT1 = dscr("XT1", [D, NTOK], F32)
    Qsc = dscr("Qsc", [96, H, NTOK], BF16)
    Ksc = dscr("Ksc", [96, H, NKEY], BF16)
    Vsc = dscr("Vsc", [NKB, 128, H * 65], BF16)
    SCsc = dscr("SCsc", [NKB, 128, H], F32)
    ATTNsc = dscr("ATTNsc", [1024, NTOK], BF16)
    POOLsc = dscr("POOLsc", [512, NTOK], F32)
    SGUsc = dscr("SGUsc", [512, NTOK], BF16)
    CONVy = dscr("CONVy", [512, NTOK], F32)
    CONVb = dscr("CONVb", [512, NTOK], F32)
    WG = dscr("WG", [D, 4 * D], BF16)
    WBR = dscr("WBR", [2560, D], BF16)
    WO = dscr("WO", [D, D], BF16)
    WFI = dscr("WFI", [D, 2 * DFF], BF16)
    WFO = dscr("WFO", [DFF, D], BF16)
    BR_OFF = [0, 1024, 1536, 2048]

    stop_after = (dbg or {}).get("stop_after", None)
    nlayers = (dbg or {}).get("nlayers", L)

    with ExitStack() as top:
        S = Sched(nc, top)
        op = S.op

        uid = [0]

        def T(es, name, shape, dt):
            uid[0] += 1
            return es.enter_context(nc.sbuf_tensor("sb_%s_%d" % (name, uid[0]), list(shape), dt))

        def pe(fn, r, w):
            return op("pe", fn, r, w)

        def mm(out, lhsT, rhs, start, stop, r, w):
            return op("pe", lambda: nc.tensor.matmul(out, lhsT=lhsT, rhs=rhs, start=start, stop=stop), r, w)

        def act(fn, r, w):
            return op("act", fn, r, w)

        def dve(fn, r, w):
            return op("dve", fn, r, w)

        def spdma(out, in_, r, w):
            return op("sp", lambda: nc.sync.dma_start(out=out, in_=in_), r, w, dma=True)

        def pldma(out, in_, r, w):
            return op("pool", lambda: nc.gpsimd.dma_start(out=out, in_=in_), r, w, dma=True)

        MV = T(top, "MV", [128, L * 2 * 6, 16], F32)
        MVb = Buf()
        ones_bf = T(top, "ones_bf", [128, 128], BF16)
        prot_bf = T(top, "prot_bf", [96, 96], BF16)
        sel_f = T(top, "sel_f", [65, 64], F32)
        constb = Buf()
        dve(lambda: nc.vector.memset(ones_bf[:], 1.0), [], [constb])
        pldma(prot_bf[:], prot_in, [], [constb])
        spdma(sel_f[:], sel_in, [], [constb])

        def mvec(l, g, i):
            k = (l * 2 + g) * 6 + i
            return MV[:, k, :]

        class PsumPool:
            def __init__(self, es, n=8):
                uid[0] += 1
                self.t = [es.enter_context(nc.psum_tensor("ps%d_%d" % (i, uid[0]), [128, 512], F32)) for i in range(n)]
                self.b = [Buf(excl=True) for _ in range(n)]
                self.i = 0

            def get(self):
                k = self.i % len(self.t)
                self.i += 1
                return self.t[k], self.b[k]

        class WPool:
            def __init__(self, es, n, dt=BF16, cols=512, name="wslot"):
                self.t = [T(es, "%s%d" % (name, i), [128, 16, cols], dt) for i in range(n)]
                self.b = [Buf() for _ in range(n)]
                self.i = 0

            def load(self, src, nk, ncols):
                k = self.i % len(self.t)
                self.i += 1
                t, b = self.t[k], self.b[k]
                pldma(t[:, 0:nk, 0:ncols], src.rearrange("(kc p) n -> p kc n", p=128), [], [b])
                return t, b

        def rstd_from_psum(ps_ap, out_ap, n, psb, outb):
            act(lambda: nc.scalar.activation(out=out_ap, in_=ps_ap, func=AF.Sqrt, scale=1.0 / n, bias=EPS), [psb], [outb])
            dve(lambda: nc.vector.reciprocal(out=out_ap, in_=out_ap), [outb], [outb])

        def phase_mod(l):
            with ExitStack() as es:
                pp = PsumPool(es, 1)
                psm, psmb = pp.get()
                cT = T(es, "cT", [128, 16, 2], F32); cTb = Buf()
                sT = T(es, "sT", [128, 16, 2], F32); sTb = Buf()
                bm = T(es, "bm", [128, 96], F32); bmb = Buf()
                nm = T(es, "nm", [128, 16], F32); nf = T(es, "nf", [128, 16], F32); nb_ = Buf()
                modT = T(es, "modT", [128, 2, 96], F32); modb = Buf()
                wp = WPool(es, 3, dt=F32, cols=256, name="wm")
                spdma(cT[:], cT_in, [], [cTb])
                spdma(bm[:], b_modT[l], [], [bmb])
                spdma(nm[:], nmixT[l], [], [nb_])
                spdma(nf[:], nffnT[l], [], [nb_])
                act(lambda: nc.scalar.activation(out=sT[:], in_=cT[:], func=AF.Silu), [cTb], [sTb])
                for cb in range(48):
                    wt, wb = wp.load(w_mod[l][:, cb * 256:(cb + 1) * 256], 16, 256)
                    for jj in range(2):
                        j = cb * 2 + jj
                        for kc in range(16):
                            mm(psm[:, 2 * j:2 * j + 2], wt[:, kc, jj * 128:(jj + 1) * 128], sT[:, kc, :],
                               kc == 0, kc == 15, [wb, sTb], [psmb])
                pv = psm[:, 0:192].rearrange("p (j g) -> p j g", g=2)
                for g in range(2):
                    dve(lambda g=g: nc.vector.tensor_tensor(out=modT[:, g, :], in0=pv[:, :, g], in1=bm[:], op=ALU.add),
                        [psmb, bmb], [modb])
                for g in range(2):
                    def mg(i, g=g):
                        return modT[:, g, i * 16:(i + 1) * 16]
                    dve(lambda g=g, mg=mg: nc.vector.scalar_tensor_tensor(out=mvec(l, g, 0), in0=mg(1), scalar=1.0, in1=nm[:], op0=ALU.add, op1=ALU.mult), [modb, nb_], [MVb])
                    dve(lambda g=g, mg=mg: nc.vector.tensor_copy(out=mvec(l, g, 1), in_=mg(0)), [modb], [MVb])
                    dve(lambda g=g, mg=mg: nc.vector.tensor_copy(out=mvec(l, g, 2), in_=mg(2)), [modb], [MVb])
                    dve(lambda g=g, mg=mg: nc.vector.scalar_tensor_tensor(out=mvec(l, g, 3), in0=mg(4), scalar=1.0, in1=nf[:], op0=ALU.add, op1=ALU.mult), [modb, nb_], [MVb])
                    dve(lambda g=g, mg=mg: nc.vector.tensor_copy(out=mvec(l, g, 4), in_=mg(3)), [modb], [MVb])
                    dve(lambda g=g, mg=mg: nc.vector.tensor_copy(out=mvec(l, g, 5), in_=mg(5)), [modb], [MVb])
                S.flush()

        def norm_mod(pp, xT, xb, hT, hb, a_ap, b_ap, sq, sqb, rs, rsb, tmpf, tmpb):
            ps, psb = pp.get()
            xbl = xb if isinstance(xb, list) else [xb] * 16
            for kc in range(16):
                k2 = kc % 2
                act(lambda kc=kc, k2=k2: nc.scalar.activation(out=sq[k2][:], in_=xT[:, kc, :], func=AF.Square), [xbl[kc]], [sqb[k2]])
                mm(ps[:, :], ones_bf[:, :], sq[k2][:], kc == 0, kc == 15, [sqb[k2], constb], [psb])
            rstd_from_psum(ps[:, :], rs[:], D, psb, rsb)
            for kc in range(16):
                k2 = kc % 2
                dve(lambda kc=kc, k2=k2: nc.vector.scalar_tensor_tensor(out=tmpf[k2][:], in0=xT[:, kc, :], scalar=a_ap[:, kc:kc + 1], in1=rs[:], op0=ALU.mult, op1=ALU.mult), [xbl[kc], rsb, MVb], [tmpb[k2]])
                act(lambda kc=kc, k2=k2: nc.scalar.activation(out=hT[:, kc, :], in_=tmpf[k2][:], func=AF.Identity, bias=b_ap[:, kc:kc + 1], scale=1.0), [tmpb[k2], MVb], [hb])

        def phase1(l):
            XIN = xT_in if l == 0 else XT1
            with ExitStack() as es:
                pp = PsumPool(es, 8)
                wp = WPool(es, 3)
                U = T(es, "U", [128, 8192], F32); Ub = Buf(); xpb = [Buf() for _ in range(4)]
                xT = U[:, :].rearrange("p (k t) -> p k t", t=512)
                zq = U[:, 0:3072].rearrange("p (k t) -> p k t", t=512); zqb = Buf()
                u_t = U[:, 3072:5120].rearrange("p (k t) -> p k t", t=512); ub = Buf()
                cg = U[:, 5120:7168].rearrange("p (k t) -> p k t", t=512); cgb = Buf()
                zkv = U[:, 7168:8192].rearrange("p (k t) -> p k t", t=512); zkvb = Buf()
                hT = T(es, "hT", [128, 16, 512], BF16); hb = Buf()
                wuq = T(es, "wuq", [128, 6, 1536], BF16)
                wukv = T(es, "wukv", [128, 2, 2048], BF16)
                wspT = T(es, "wspT", [128, 4, 128], BF16)
                qaw = T(es, "qaw", [128, 6], F32); kvaw = T(es, "kvaw", [128, 2], F32)
                wq96 = T(es, "wq96", [96, 1], F32); wk96 = T(es, "wk96", [96, 1], F32)
                sguw = T(es, "sguw", [128, 4], F32)
                bspb = T(es, "bspb", [128, 512], F32)
                kvawb = T(es, "kvawb", [128, 256], F32)
                wres = Buf()
                pldma(wuq[:], w_uq[l].rearrange("(kc p) n -> p kc n", p=128), [], [wres])
                pldma(wukv[:], w_ukv[l].rearrange("(kc p) n -> p kc n", p=128), [], [wres])
                pldma(wspT[:], w_spT[l].rearrange("g p q -> p g q"), [], [wres])
                spdma(qaw[:], qawT[l], [], [wres]); spdma(kvaw[:], kvawT[l], [], [wres])
                spdma(wq96[:], qnw[l], [], [wres]); spdma(wk96[:], knw[l], [], [wres])
                spdma(sguw[:], sguwT[l], [], [wres])
                spdma(bspb[:], b_sp[l].partition_broadcast(128), [], [wres])
                spdma(kvawb[:], kvaw_row[l].partition_broadcast(128), [], [wres])

                sq = [T(es, "sq%d" % i, [128, 512], BF16) for i in range(2)]; sqb = [Buf(), Buf()]
                rs = T(es, "rs", [128, 512], F32); rsb = Buf()
                tmpf = [T(es, "tmpf%d" % i, [128, 512], F32) for i in range(2)]; tmpb = [Buf(), Buf()]
                cqT = T(es, "cqT", [128, 6, 512], BF16); cqb = Buf()
                ckvT = T(es, "ckvT", [128, 2, 512], BF16); ckvb = Buf()
                cosT = T(es, "cosT", [96, 512], F32); sinT = T(es, "sinT", [96, 512], F32); csb = Buf()
                kpw = T(es, "kpw", [96, 512], BF16); kpwb = Buf()
                krope = T(es, "krope", [96, 512], BF16); kropeb = Buf()
                t1 = [T(es, "t1%d" % i, [96, 512], F32) for i in range(2)]; t2 = [T(es, "t2%d" % i, [96, 512], F32) for i in range(2)]; t1b = [Buf(), Buf()]; t2b = [Buf(), Buf()]
                sq96 = [T(es, "sq96%d" % i, [96, 512], BF16) for i in range(2)]; sq96b = [Buf(), Buf()]
                qw = [T(es, "qw%d" % i, [96, 512], BF16) for i in range(2)]; qwb = [Buf(), Buf()]
                qn = [T(es, "qn%d" % i, [96, 512], BF16) for i in range(2)]; qnb = [Buf(), Buf()]
                r96 = [T(es, "r96%d" % i, [96, 512], F32) for i in range(2)]; r96b = [Buf(), Buf()]
                Qh = [T(es, "Qh%d" % i, [96, 512], BF16) for i in range(2)]; Qhb = [Buf(), Buf()]
                Kh = [T(es, "Kh%d" % i, [64, 512], BF16) for i in range(2)]; Khb = [Buf(), Buf()]
                Vp = [T(es, "Vp%d" % i, [128, 16, 65], BF16) for i in range(2)]; Vpb = [Buf(), Buf()]
                sqk = T(es, "sqk", [128, 4, 64], F32); sqkb = Buf()
                ssk = T(es, "ssk", [128, 16], F32); sskb = Buf()
                sck = [T(es, "sck%d" % i, [128, 16], F32) for i in range(2)]; sckb = [Buf(), Buf()]
                sspe = T(es, "sspe", [128, 4], F32); sspeb = Buf()
                tm32 = T(es, "tm32", [128, 32], F32); tm32b = Buf()
                tm256 = T(es, "tm256", [128, 256], F32); tm256b = Buf()
                ss1 = T(es, "ss1", [128, 1], F32); ss1b = Buf()
                stc = [T(es, "stc%d" % i, [128, 256], F32) for i in range(2)]; stcb = [Buf(), Buf()]
                stk = [T(es, "stk%d" % i, [128, 32], F32) for i in range(2)]; stkb = [Buf(), Buf()]
                ptmp = [T(es, "ptmp%d" % i, [128, 512], F32) for i in range(2)]; ptmpb = [Buf(), Buf()]
                vg = [T(es, "vg%d" % i, [128, 512], F32) for i in range(2)]; vgb = [Buf(), Buf()]
                vh = [T(es, "vh%d" % i, [128, 512], BF16) for i in range(2)]; vhb = [Buf(), Buf()]
                ssv = T(es, "ssv", [128, 4], F32); ssvb = [Buf() for _ in range(4)]
                mx = [T(es, "mx%d" % i, [128, 128], F32) for i in range(2)]; mxb = [Buf(), Buf()]
                sgo = T(es, "sgo", [128, 4, 512], BF16); sgob = Buf()
                cckT = T(es, "cckT", [128, 2, 256], BF16); cckb = Buf()
                ckp96 = T(es, "ckp96", [96, 256], F32); ckp96b = Buf()
                ckpt = T(es, "ckpt", [128, 2, 32], F32); ckptb = Buf()

                cut = (dbg or {}).get("p1_cut", None)

                def stage(k):
                    if cut == k:
                        raise _Stop()
                dve(lambda: nc.vector.memset(kpw[:], 0.0), [], [kpwb])
                dve(lambda: nc.vector.memset(krope[:], 0.0), [], [kropeb])
                for i in range(2):
                    dve(lambda i=i: nc.vector.memset(Vp[i][:], 1.0), [], [Vpb[i]])
                cnt = {"q": 0, "k": 0, "v": 0, "p": 0, "s": 0}

                def gen_kv(ckv_ap, kr_t, kr_b, T_, key0, sspe_ap):
                    nsub = T_ // 128
                    import os
                    GK = os.environ.get("GK", "kvs")
                    for h in (range(H) if "k" in GK else []):
                        ps, psb = pp.get()
                        for kc in range(2):
                            mm(ps[0:64, 0:T_], wukv[:, kc, h * 128:h * 128 + 64], ckv_ap[:, kc, :], kc == 0, kc == 1, [wres, ckvb, cckb], [psb])
                        kb_ = cnt["k"] % 2; cnt["k"] += 1
                        act(lambda ps=ps, kb_=kb_: nc.scalar.activation(out=Kh[kb_][:, 0:T_], in_=ps[0:64, 0:T_], func=AF.Copy, scale=wk96[0:64, 0:1]), [psb, wres], [Khb[kb_]])
                        spdma(Ksc[0:64, h, key0:key0 + T_], Kh[kb_][:, 0:T_], [Khb[kb_]], [])
                        spdma(Ksc[64:96, h, key0:key0 + T_], kr_t[64:96, 0:T_], [kr_b], [])
                    for s in (range(nsub) if "v" in GK else []):
                        vb_ = cnt["v"] % 2; cnt["v"] += 1
                        for c in range(4):
                            ps, psb = pp.get()
                            for kc in range(2):
                                mm(ps[:, :], ckv_ap[:, kc, s * 128:(s + 1) * 128], wukv[:, kc, c * 512:(c + 1) * 512], kc == 0, kc == 1, [wres, ckvb, cckb], [psb])
                            pv = ps[:, :].rearrange("p (h d) -> p h d", d=128)
                            dve(lambda pv=pv, c=c, vb_=vb_: nc.vector.tensor_copy(out=Vp[vb_][:, 4 * c:4 * c + 4, 0:64], in_=pv[:, :, 64:128]), [psb], [Vpb[vb_]])
                            if "s" in GK:
                                act(lambda pv=pv: nc.scalar.activation(out=sqk[:], in_=pv[:, :, 0:64], func=AF.Square), [psb], [sqkb])
                                dve(lambda c=c: nc.vector.reduce_sum(out=ssk[:, 4 * c:4 * c + 4], in_=sqk[:], axis=AX.X), [sqkb], [sskb])
                        sb_ = cnt["s"] % 2; cnt["s"] += 1
                        if "s" not in GK:
                            kb = key0 // 128 + s
                            spdma(Vsc[kb], Vp[vb_][:].rearrange("p h c -> p (h c)"), [Vpb[vb_]], [])
                            continue
                        dve(lambda s=s: nc.vector.tensor_scalar(out=ssk[:], in0=ssk[:], scalar1=sspe_ap[:, s:s + 1], scalar2=None, op0=ALU.add), [sskb, sspeb], [sskb])
                        act(lambda: nc.scalar.activation(out=ssk[:], in_=ssk[:], func=AF.Sqrt, scale=1.0 / 96, bias=EPS), [sskb], [sskb])
                        dve(lambda: nc.vector.reciprocal(out=ssk[:], in_=ssk[:]), [sskb], [sskb])
                        dve(lambda sb_=sb_: nc.vector.tensor_scalar(out=sck[sb_][:], in0=ssk[:], scalar1=ATTN_SCALE, scalar2=None, op0=ALU.mult), [sskb], [sckb[sb_]])
                        kb = key0 // 128 + s
                        spdma(SCsc[kb], sck[sb_][:], [sckb[sb_]], [])
                        spdma(Vsc[kb], Vp[vb_][:].rearrange("p h c -> p (h c)"), [Vpb[vb_]], [])

                if cut != 0:
                    pldma(cckT[:], cckvT_in[l].rearrange("(kc p) t -> p kc t", p=128), [], [cckb])
                    spdma(ckp96[64:96, :], ckpeT_in[l], [], [ckp96b])
                    spdma(ckpt[:], ckpe_in[l].rearrange("(s p) d -> p s d", p=128), [], [ckptb])
                    import os
                    if "act" in os.environ.get("P1A", "act,sq"):
                        act(lambda: nc.scalar.activation(out=krope[64:96, 0:256], in_=ckp96[64:96, :], func=AF.Copy, scale=wk96[64:96, 0:1]), [ckp96b, wres, kropeb], [kropeb])
                    for s in (range(2) if "sq" in os.environ.get("P1A", "act,sq") else []):
                        act(lambda s=s: nc.scalar.activation(out=tm32[:], in_=ckpt[:, s, :], func=AF.Square), [ckptb], [tm32b])
                        dve(lambda s=s: nc.vector.reduce_sum(out=sspe[:, s:s + 1], in_=tm32[:], axis=AX.X), [tm32b], [sspeb])
                try:
                  stage(0)
                  stage(1)
                  gen_kv(cckT, krope, kropeb, 256, 4096, sspe)
                  stage(2)
                  pieces = ((0, 6, zqb), (6, 10, ub), (10, 14, cgb), (14, 16, zkvb))

                  def xload(ti, pi):
                      k0, k1, tb = pieces[pi]
                      t0_ = ti * TT
                      spdma(xT[:, k0:k1, :], XIN[k0 * 128:k1 * 128, t0_:t0_ + TT].rearrange("(kc p) t -> p kc t", p=128), [], [xpb[pi], tb])
                  for i in range(NT):
                    g = 0 if i < 8 else 1
                    tok0 = i * TT
                    key0 = tok0 if i < 8 else 4352
                    a1 = mvec(l, g, 0); b1 = mvec(l, g, 1)
                    if i == 0:
                        for pi in range(4):
                            xload(0, pi)
                    spdma(cosT[:], cos_in[:, tok0:tok0 + TT], [], [csb])
                    spdma(sinT[:], sin_in[:, tok0:tok0 + TT], [], [csb])
                    norm_mod(pp, xT, [xpb[0]] * 6 + [xpb[1]] * 4 + [xpb[2]] * 4 + [xpb[3]] * 2, hT, hb, a1, b1, sq, sqb, rs, rsb, tmpf, tmpb)

                    stage(3)
                    def zgroup(wt, wb, c0, m, ps, psb, n=TT):
                        for kc in range(16):
                            mm(ps[0:m, 0:n], wt[:, kc, c0:c0 + m], hT[:, kc, :], kc == 0, kc == 15, [wb, hb], [psb])

                    wA, wAb = wp.load(w_in[l][:, 0:512], 16, 512)
                    wB, wBb = wp.load(w_in[l][:, 512:768], 16, 256)
                    pss, pssb = pp.get()
                    for j in range(6):
                        wt, wb, c0 = (wA, wAb, j * 128) if j < 4 else (wB, wBb, (j - 4) * 128)
                        ps, psb = pp.get()
                        zgroup(wt, wb, c0, 128, ps, psb)
                        k2 = j % 2
                        act(lambda ps=ps, k2=k2: nc.scalar.activation(out=sq[k2][:], in_=ps[:, :], func=AF.Square), [psb], [sqb[k2]])
                        dve(lambda ps=ps, j=j: nc.vector.tensor_copy(out=zq[:, j, :], in_=ps[:, :]), [psb], [zqb])
                        mm(pss[:, :], ones_bf[:, :], sq[k2][:], j == 0, j == 5, [sqb[k2], constb], [pssb])
                    rstd_from_psum(pss[:, :], rs[:], 768, pssb, rsb)
                    for j in range(6):
                        dve(lambda j=j: nc.vector.scalar_tensor_tensor(out=cqT[:, j, :], in0=zq[:, j, :], scalar=qaw[:, j:j + 1], in1=rs[:], op0=ALU.mult, op1=ALU.mult), [zqb, rsb, wres], [cqb])
                    stage(4)
                    wC, wCb = wp.load(w_in[l][:, 768:1056], 16, 288)
                    pss, pssb = pp.get()
                    for j in range(2):
                        ps, psb = pp.get()
                        zgroup(wC, wCb, j * 128, 128, ps, psb)
                        k2 = j % 2
                        act(lambda ps=ps, k2=k2: nc.scalar.activation(out=sq[k2][:], in_=ps[:, :], func=AF.Square), [psb], [sqb[k2]])
                        dve(lambda ps=ps, j=j: nc.vector.tensor_copy(out=zkv[:, j, :], in_=ps[:, :]), [psb], [zkvb])
                        mm(pss[:, :], ones_bf[:, :], sq[k2][:], j == 0, j == 1, [sqb[k2], constb], [pssb])
                    rstd_from_psum(pss[:, :], rs[:], 256, pssb, rsb)
                    for j in range(2):
                        dve(lambda j=j: nc.vector.scalar_tensor_tensor(out=ckvT[:, j, :], in0=zkv[:, j, :], scalar=kvaw[:, j:j + 1], in1=rs[:], op0=ALU.mult, op1=ALU.mult), [zkvb, rsb, wres], [ckvb])
                    if i + 1 < NT:
                        xload(i + 1, 0)
                        xload(i + 1, 3)
                    ps, psb = pp.get()
                    zgroup(wC, wCb, 192, 96, ps, psb)
                    dve(lambda ps=ps: nc.vector.tensor_scalar(out=kpw[64:96, :], in0=ps[64:96, :], scalar1=wk96[64:96, 0:1], scalar2=None, op0=ALU.mult), [psb, wres], [kpwb])
                    ps2, ps2b = pp.get()
                    mm(ps2[0:96, :], prot_bf[:, :], kpw[:, :], True, True, [kpwb, constb], [ps2b])
                    dve(lambda: nc.vector.tensor_tensor(out=t1[0][64:96, :], in0=kpw[64:96, :], in1=cosT[64:96, :], op=ALU.mult), [kpwb, csb], [t1b[0]])
                    dve(lambda ps2=ps2: nc.vector.tensor_tensor(out=t2[0][64:96, :], in0=ps2[64:96, :], in1=sinT[64:96, :], op=ALU.mult), [ps2b, csb], [t2b[0]])
                    dve(lambda: nc.vector.tensor_tensor(out=krope[64:96, :], in0=t1[0][64:96, :], in1=t2[0][64:96, :], op=ALU.add), [t1b[0], t2b[0]], [kropeb])
                    stage(5)
                    for s in range(4):
                        ps, psb = pp.get()
                        for kc in range(16):
                            mm(ps[:, 0:288], hT[:, kc, s * 128:(s + 1) * 128], wC[:, kc, 0:288], kc == 0, kc == 15, [wCb, hb], [psb])
                        act(lambda ps=ps: nc.scalar.activation(out=tm32[:], in_=ps[:, 256:288], func=AF.Square), [psb], [tm32b])
                        dve(lambda s=s: nc.vector.reduce_sum(out=sspe[:, s:s + 1], in_=tm32[:], axis=AX.X), [tm32b], [sspeb])
                        if i == 8:
                            p2 = cnt["p"] % 2; cnt["p"] += 1
                            seq, pos0 = s // 2, (s % 2) * 128
                            act(lambda ps=ps: nc.scalar.activation(out=tm256[:], in_=ps[:, 0:256], func=AF.Square), [psb], [tm256b])
                            dve(lambda: nc.vector.reduce_sum(out=ss1[:], in_=tm256[:], axis=AX.X), [tm256b], [ss1b])
                            act(lambda: nc.scalar.activation(out=ss1[:], in_=ss1[:], func=AF.Sqrt, scale=1.0 / 256, bias=EPS), [ss1b], [ss1b])
                            dve(lambda: nc.vector.reciprocal(out=ss1[:], in_=ss1[:]), [ss1b], [ss1b])
                            dve(lambda ps=ps, p2=p2: nc.vector.scalar_tensor_tensor(out=stc[p2][:], in0=ps[:, 0:256], scalar=ss1[:, 0:1], in1=kvawb[:], op0=ALU.mult, op1=ALU.mult), [psb, ss1b, wres], [stcb[p2]])
                            dve(lambda ps=ps, p2=p2: nc.vector.tensor_copy(out=stk[p2][:], in_=ps[:, 256:288]), [psb], [stkb[p2]])
                            spdma(st_ckv[seq, l, pos0:pos0 + 128, :], stc[p2][:], [stcb[p2]], [])
                            spdma(st_kpe[seq, l, pos0:pos0 + 128, :], stk[p2][:], [stkb[p2]], [])
                    stage(6)
                    qps = {}

                    def qA(h):
                        q2 = h % 2
                        ps, psb = pp.get()
                        for kc in range(6):
                            mm(ps[0:96, :], wuq[:, kc, h * 96:(h + 1) * 96], cqT[:, kc, :], kc == 0, kc == 5, [wres, cqb], [psb])
                        act(lambda: nc.scalar.activation(out=sq96[q2][:], in_=ps[0:96, :], func=AF.Square), [psb], [sq96b[q2]])
                        act(lambda: nc.scalar.activation(out=qw[q2][:], in_=ps[0:96, :], func=AF.Copy, scale=wq96[:, 0:1]), [psb, wres], [qwb[q2]])

                    def qB_pe(h):
                        q2 = h % 2
                        pq, pqb = pp.get()
                        mm(pq[0:96, :], ones_bf[0:96, 0:96], sq96[q2][:], True, True, [sq96b[q2], constb], [pqb])
                        act(lambda: nc.scalar.activation(out=r96[q2][:], in_=pq[0:96, :], func=AF.Sqrt, scale=1.0 / 96, bias=EPS), [pqb], [r96b[q2]])

                    def qB_dve(h):
                        q2 = h % 2
                        dve(lambda: nc.vector.reciprocal(out=r96[q2][:], in_=r96[q2][:]), [r96b[q2]], [r96b[q2]])

                    def qC(h):
                        q2 = h % 2
                        dve(lambda: nc.vector.tensor_tensor(out=qn[q2][:], in0=qw[q2][:], in1=r96[q2][:], op=ALU.mult), [qwb[q2], r96b[q2]], [qnb[q2]])
                        pr, prb = pp.get()
                        mm(pr[0:96, :], prot_bf[:, :], qn[q2][:], True, True, [qnb[q2], constb], [prb])
                        qps[h] = (pr, prb)

                    def qD1(h):
                        q2 = h % 2
                        pr, prb = qps[h]
                        dve(lambda: nc.vector.tensor_tensor(out=t1[q2][:], in0=qn[q2][:], in1=cosT[:], op=ALU.mult), [qnb[q2], csb], [t1b[q2]])
                        dve(lambda: nc.vector.tensor_tensor(out=t2[q2][:], in0=pr[0:96, :], in1=sinT[:], op=ALU.mult), [prb, csb], [t2b[q2]])

                    def qD2(h, tok0=tok0):
                        q2 = h % 2
                        dve(lambda: nc.vector.tensor_tensor(out=Qh[q2][:], in0=t1[q2][:], in1=t2[q2][:], op=ALU.add), [t1b[q2], t2b[q2]], [Qhb[q2]])
                        spdma(Qsc[:, h, tok0:tok0 + TT], Qh[q2][:], [Qhb[q2]], [])
                    for st in range(H + 3):
                        if 0 <= st - 1 < H:
                            qB_pe(st - 1)
                        if 0 <= st - 2 < H:
                            qC(st - 2)
                        if 0 <= st - 3 < H:
                            qD1(st - 3)
                        if st < H:
                            qA(st)
                        if 0 <= st - 1 < H:
                            qB_dve(st - 1)
                        if 0 <= st - 3 < H:
                            qD2(st - 3)
                    stage(7)
                    gen_kv(ckvT, krope, kropeb, TT, key0, sspe)
                    stage(8)
                    wD, wDb = wp.load(w_in[l][:, OFF_POOL:OFF_POOL + 512], 16, 512)
                    for j in range(4):
                        ps, psb = pp.get()
                        zgroup(wD, wDb, j * 128, 128, ps, psb)
                        p2 = j % 2
                        act(lambda ps=ps, p2=p2: nc.scalar.copy(out=ptmp[p2][:], in_=ps[:, :]), [psb], [ptmpb[p2]])
                        spdma(POOLsc[j * 128:(j + 1) * 128, tok0:tok0 + TT], ptmp[p2][:], [ptmpb[p2]], [])
                    stage(9)
                    wE, wEb = wp.load(w_in[l][:, OFF_SGU:OFF_SGU + 512], 16, 512)
                    for j in range(4):
                        ps, psb = pp.get()
                        zgroup(wE, wEb, j * 128, 128, ps, psb)
                        act(lambda ps=ps, j=j: nc.scalar.activation(out=u_t[:, j, :], in_=ps[:, :], func=AF.Gelu_apprx_tanh), [psb], [ub])
                    wF, wFb = wp.load(w_in[l][:, OFF_SGU + 512:OFF_SGU + 1024], 16, 512)
                    def sA(s_):
                        v2 = s_ % 2
                        ps, psb = pp.get()
                        for kc in range(16):
                            mm(ps[:, :], hT[:, kc, s_ * 128:(s_ + 1) * 128], wF[:, kc, :], kc == 0, kc == 15, [wFb, hb], [psb])
                        act(lambda: nc.scalar.activation(out=vg[v2][:], in_=ps[:, :], func=AF.Gelu_apprx_tanh), [psb], [vgb[v2]])
                        dve(lambda: nc.vector.tensor_tensor(out=tmpf[v2][:], in0=vg[v2][:], in1=vg[v2][:], op=ALU.mult), [vgb[v2]], [tmpb[v2]])
                        dve(lambda: nc.vector.reduce_sum(out=ssv[:, s_:s_ + 1], in_=tmpf[v2][:], axis=AX.X), [tmpb[v2]], [ssvb[s_]])
                        act(lambda: nc.scalar.activation(out=ssv[:, s_:s_ + 1], in_=ssv[:, s_:s_ + 1], func=AF.Sqrt, scale=1.0 / 512, bias=EPS), [ssvb[s_]], [ssvb[s_]])

                    def sA2(s_):
                        v2 = s_ % 2
                        dve(lambda: nc.vector.reciprocal(out=ssv[:, s_:s_ + 1], in_=ssv[:, s_:s_ + 1]), [ssvb[s_]], [ssvb[s_]])
                        dve(lambda: nc.vector.tensor_scalar(out=vh[v2][:], in0=vg[v2][:], scalar1=ssv[:, s_:s_ + 1], scalar2=None, op0=ALU.mult), [vgb[v2], ssvb[s_]], [vhb[v2]])

                    def sB(s_):
                        v2 = s_ % 2
                        pm, pmb = pp.get()
                        for gg in range(4):
                            mm(pm[:, gg * 128:(gg + 1) * 128], vh[v2][:, gg * 128:(gg + 1) * 128], wspT[:, gg, :], True, True, [vhb[v2], wres], [pmb])
                        for gg in range(4):
                            dve(lambda gg=gg: nc.vector.scalar_tensor_tensor(out=mx[gg % 2][:], in0=pm[:, gg * 128:(gg + 1) * 128], scalar=sguw[:, gg:gg + 1], in1=bspb[:, gg * 128:(gg + 1) * 128], op0=ALU.mult, op1=ALU.add), [pmb, wres], [mxb[gg % 2]])
                            dve(lambda gg=gg: nc.vector.tensor_tensor(out=sgo[:, gg, s_ * 128:(s_ + 1) * 128], in0=mx[gg % 2][:], in1=u_t[:, gg, s_ * 128:(s_ + 1) * 128], op=ALU.mult), [mxb[gg % 2], ub], [sgob])
                    sA(0); sA(1); sA2(0); sB(0); sA(2); sA2(1); sB(1); sA(3); sA2(2); sB(2); sA2(3); sB(3)
                    spdma(SGUsc[:, tok0:tok0 + TT].rearrange("(g p) t -> p g t", p=128), sgo[:], [sgob], [])
                    if i + 1 < NT:
                        xload(i + 1, 1)
                    stage(10)
                    wG, wGb = wp.load(w_in[l][:, OFF_CONV:OFF_CONV + 512], 16, 512)
                    for j in range(4):
                        ps, psb = pp.get()
                        zgroup(wG, wGb, j * 128, 128, ps, psb)
                        p2 = j % 2
                        act(lambda ps=ps, p2=p2: nc.scalar.copy(out=ptmp[p2][:], in_=ps[:, :]), [psb], [ptmpb[p2]])
                        spdma(CONVb[j * 128:(j + 1) * 128, tok0:tok0 + TT], ptmp[p2][:], [ptmpb[p2]], [])
                    wH, wHb = wp.load(w_in[l][:, OFF_CONV + 512:OFF_CONV + 1024], 16, 512)
                    for j in range(4):
                        ps, psb = pp.get()
                        zgroup(wH, wHb, j * 128, 128, ps, psb)
                        act(lambda ps=ps, j=j: nc.scalar.copy(out=cg[:, j, :], in_=ps[:, :]), [psb], [cgb])
                    wI, wIb = wp.load(w_in[l][:, OFF_CONV + 1024:OFF_CONV + 1536], 16, 512)
                    for j in range(4):
                        ps, psb = pp.get()
                        zgroup(wI, wIb, j * 128, 128, ps, psb)
                        p2 = j % 2
                        dve(lambda ps=ps, p2=p2, j=j: nc.vector.tensor_tensor(out=ptmp[p2][:], in0=ps[:, :], in1=cg[:, j, :], op=ALU.mult), [psb, cgb], [ptmpb[p2]])
                        spdma(CONVy[j * 128:(j + 1) * 128, tok0:tok0 + TT], ptmp[p2][:], [ptmpb[p2]], [])
                    if i + 1 < NT:
                        xload(i + 1, 2)
                    stage(11)
                except _Stop:
                  pass
                S.flush()

        brKk = [8, 4, 4, 4]

        def phase2(l):
            oc = 0
            with ExitStack() as es:
                ppS = PsumPool(es, 5)
                ppO = PsumPool(es, 2)
                ppD = PsumPool(es, 1)
                Vall = T(es, "Vall", [128, NKB, H * 65], BF16); Vb = Buf()
                SCall = T(es, "SCall", [128, NKB, H], F32); SCb = Buf()
                Kt = [T(es, "Kt%d" % i, [96, NKEY], BF16) for i in range(2)]; Ktb = [Buf(), Buf()]
                Qt = [T(es, "Qt%d" % i, [96, NTOK], BF16) for i in range(2)]; Qtb = [Buf(), Buf()]
                pT = [T(es, "pT%d" % i, [128, 512], BF16) for i in range(4)]; pTb = [Buf() for _ in range(4)]
                osb = [T(es, "osb%d" % i, [65, 512], F32) for i in range(2)]; osbb = [Buf(), Buf()]
                rden = T(es, "rden", [64, 512], F32); rdenb = Buf()
                on = [T(es, "on%d" % i, [64, 512], BF16) for i in range(2)]; onb = [Buf(), Buf()]
                for c in range(2):
                    spdma(Vall[:, c * 19:(c + 1) * 19, :], Vsc[c * 19:(c + 1) * 19].rearrange("kb p c -> p kb c"), [], [Vb])
                spdma(SCall[:], SCsc.rearrange("kb p h -> p kb h"), [], [SCb])
                spdma(Kt[0][:], Ksc[:, 0, :], [], [Ktb[0]])
                spdma(Qt[0][:], Qsc[:, 0, :], [], [Qtb[0]])
                for r0 in range(0, D, 128):
                    pldma(WG[r0:r0 + 128, :], w_in[l][r0:r0 + 128, OFF_GATE:OFF_GATE + 4 * D], [Vb, SCb, Ktb[0], Qtb[0]] if r0 == 0 else [], [])
                for br in range(4):
                    for r0 in range(0, brKk[br] * 128, 128):
                        pldma(WBR[BR_OFF[br] + r0:BR_OFF[br] + r0 + 128, :], w_br[br][l][r0:r0 + 128, :], [], [])
                for r0 in range(0, D, 128):
                    pldma(WO[r0:r0 + 128, :], w_out[l][r0:r0 + 128, :], [], [])
                for r0 in range(0, D, 128):
                    pldma(WFI[r0:r0 + 128, :], w_ffn_in[l][r0:r0 + 128, :], [], [])
                for r0 in range(0, DFF, 128):
                    pldma(WFO[r0:r0 + 128, :], w_ffn_out[l][r0:r0 + 128, :], [], [])
                jobs = [(qt * 512, 512, list(range(34))) for qt in range(8)]
                jobs += [(4096 + 256 * s, 256, [34 + 2 * s, 35 + 2 * s]) for s in range(2)]
                pc = 0
                LA = 4
                pending = [None]

                def run_pending():
                    if pending[0] is not None:
                        pending[0]()
                        pending[0] = None
                for h in range(H):
                    hb2 = h % 2
                    for ji, (q0, nq, kbs) in enumerate(jobs):
                        if ji == 1 and h + 1 < H:
                            spdma(Kt[1 - hb2][:], Ksc[:, h + 1, :], [], [Ktb[1 - hb2]])
                            spdma(Qt[1 - hb2][:], Qsc[:, h + 1, :], [], [Qtb[1 - hb2]])
                        po, pob = ppO.get()
                        sps = {}

                        def smm(idx, kbs=kbs, q0=q0, nq=nq, hb2=hb2, sps=sps):
                            ps, psb = ppS.get()
                            kb = kbs[idx]
                            mm(ps[:, 0:nq], Kt[hb2][:, kb * 128:(kb + 1) * 128], Qt[hb2][:, q0:q0 + nq], True, True, [Ktb[hb2], Qtb[hb2]], [psb])
                            sps[idx] = (ps, psb)
                        for k0 in range(min(LA, len(kbs))):
                            smm(k0)
                        run_pending()
                        for idx, kb in enumerate(kbs):
                            ps, psb = sps.pop(idx)
                            r = pc % 4; pc += 1
                            act(lambda ps=ps, r=r, kb=kb, h=h, nq=nq: nc.scalar.activation(out=pT[r][:, 0:nq], in_=ps[:, 0:nq], func=AF.Exp, scale=SCall[:, kb, h:h + 1]), [psb, SCb], [pTb[r]])
                            mm(po[0:65, 0:nq], Vall[:, kb, h * 65:(h + 1) * 65], pT[r][:, 0:nq], idx == 0, idx == len(kbs) - 1, [Vb, pTb[r]], [pob])
                            if idx + LA < len(kbs):
                                smm(idx + LA)

                        def norm(po=po, pob=pob, nq=nq, q0=q0, h=h):
                            nonlocal oc
                            o2 = oc % 2; oc += 1
                            dve(lambda: nc.vector.tensor_copy(out=osb[o2][:, 0:nq], in_=po[0:65, 0:nq]), [pob], [osbb[o2]])
                            pd, pdb = ppD.get()
                            mm(pd[0:64, 0:nq], sel_f[:, :], osb[o2][:, 0:nq], True, True, [osbb[o2], constb], [pdb])
                            dve(lambda: nc.vector.reciprocal(out=rden[:, 0:nq], in_=pd[0:64, 0:nq]), [pdb], [rdenb])
                            dve(lambda: nc.vector.tensor_tensor(out=on[o2][:, 0:nq], in0=osb[o2][0:64, 0:nq], in1=rden[:, 0:nq], op=ALU.mult), [osbb[o2], rdenb], [onb[o2]])
                            spdma(ATTNsc[h * 64:(h + 1) * 64, q0:q0 + nq], on[o2][:, 0:nq], [onb[o2]], [])
                        pending[0] = norm
                run_pending()
                S.flush()

        def phase3(l):
            XIN = xT_in if l == 0 else XT1
            XOUT = XT1 if l == 0 else yT
            with ExitStack() as es:
                pp = PsumPool(es, 8)
                wp = WPool(es, 3)
                xT = T(es, "xT3", [128, 16, 512], F32); xb = Buf()
                hT = T(es, "hT3", [128, 16, 512], BF16); hb = Buf()
                sq = [T(es, "sq3%d" % i, [128, 512], BF16) for i in range(2)]; sqb = [Buf(), Buf()]
                rs = T(es, "rs3", [128, 512], F32); rsb = Buf()
                tmpf = [T(es, "tmpf3%d" % i, [128, 512], F32) for i in range(2)]; tmpb = [Buf(), Buf()]
                R = T(es, "R3", [128, 22800], BF16)
                actT = R[:, 0:22528].rearrange("p (k t) -> p k t", t=512); actb = Buf()
                Rf = R[:, :].bitcast(F32)
                zp = Rf[:, 0:2112].rearrange("p (g t) -> p g t", t=528); zpb = Buf()
                pa = [Rf[:, 2112 + k * 528:2112 + (k + 1) * 528] for k in range(4)]; pab = Buf()
                cy = Rf[:, 4224:6280].rearrange("p (g t) -> p g t", t=514); cyb = Buf()
                cb_ = Rf[:, 6280:8328].rearrange("p (g t) -> p g t", t=512); cbb = Buf()
                ctmp = Rf[:, 8328:8840]; ctb = Buf()
                ptm = Rf[:, 8840:9352]; ptmb = Buf()
                invt = Rf[:, 9352:11400].rearrange("p (g t) -> p g t", t=512); invb = Buf()
                brT_attn = T(es, "brA", [128, 8, 512], BF16)
                brT_pool = T(es, "brP", [128, 4, 512], BF16)
                brT_sgu = T(es, "brS", [128, 4, 512], BF16)
                brT_conv = T(es, "brC", [128, 4, 512], BF16)
                brb = [Buf(), Buf(), Buf(), Buf()]
                pl = T(es, "pl", [128, 512], BF16); plb = Buf()
                mT = T(es, "mT", [128, 16, 512], BF16); mb_ = Buf()
                macc = T(es, "macc", [128, 512], F32); maccb = Buf()
                maccs = [T(es, "maccs%d" % i, [128, 512], F32) for i in range(4)]; maccsb = [Buf() for _ in range(4)]
                gt = [T(es, "gt%d" % i, [128, 512], F32) for i in range(2)]; gtb = [Buf(), Buf()]
                wpl = T(es, "wpl", [128, 4, 128], BF16)
                psc = T(es, "psc", [128, 4], F32); cw = T(es, "cw", [128, 12], F32); bg = T(es, "bg", [128, 64], F32)
                wres = Buf()
                pldma(wpl[:], w_pool[l].rearrange("g i o -> i g o"), [], [wres])
                spdma(psc[:], pscaleT[l], [], [wres]); spdma(cw[:], convwT[l], [], [wres]); spdma(bg[:], b_gateT[l], [], [wres])
                brT = [brT_attn, brT_pool, brT_sgu, brT_conv]
                brK = [8, 4, 4, 4]
                gc = 0
                for i in range(NT):
                    g = 0 if i < 8 else 1
                    tok0 = i * TT
                    a1 = mvec(l, g, 0); b1 = mvec(l, g, 1); ga = mvec(l, g, 2)
                    a2 = mvec(l, g, 3); b2 = mvec(l, g, 4); gf = mvec(l, g, 5)
                    spdma(xT[:], XIN[:, tok0:tok0 + TT].rearrange("(kc p) t -> p kc t", p=128), [], [xb])
                    spdma(brT_attn[:], ATTNsc[:, tok0:tok0 + TT].rearrange("(kc p) t -> p kc t", p=128), [], [brb[0]])
                    spdma(brT_sgu[:], SGUsc[:, tok0:tok0 + TT].rearrange("(kc p) t -> p kc t", p=128), [], [brb[2]])
                    norm_mod(pp, xT, xb, hT, hb, a1, b1, sq, sqb, rs, rsb, tmpf, tmpb)
                    segs = [(tok0, 512, i > 0, i < 7)] if i < 8 else [(tok0, 256, False, False), (tok0 + 256, 256, False, False)]
                    kind = (0 if i == 0 else (2 if i == 7 else 1)) if i < 8 else 3
                    spdma(invt, inv_in[kind], [], [invb, actb])
                    for (c0, W, lh, rh) in segs:
                        so = c0 - tok0
                        dve(lambda: nc.vector.memset(zp, 0.0), [], [zpb, actb])
                        dve(lambda: nc.vector.memset(cy, 0.0), [], [cyb, actb])
                        lo = c0 - (8 if lh else 0); hi = c0 + W + (8 if rh else 0)
                        spdma(zp[:, :, 8 - (c0 - lo):8 + (hi - c0)], POOLsc[:, lo:hi].rearrange("(g p) t -> p g t", p=128), [], [zpb])
                        lo = c0 - (1 if lh else 0); hi = c0 + W + (1 if rh else 0)
                        spdma(cy[:, :, 1 - (c0 - lo):1 + (hi - c0)], CONVy[:, lo:hi].rearrange("(g p) t -> p g t", p=128), [], [cyb])
                        spdma(cb_[:, :, 0:W], CONVb[:, c0:c0 + W].rearrange("(g p) t -> p g t", p=128), [], [cbb, actb])
                        n = W + 16
                        for gg in range(4):
                            X = zp[:, gg, :]
                            dve(lambda X=X, n=n: nc.vector.tensor_tensor(out=pa[0][:, 1:n], in0=X[:, 1:n], in1=X[:, 0:n - 1], op=ALU.add), [zpb], [pab, actb])
                            E = pa[0][:, 8:8 + W]
                            if gg >= 1:
                                dve(lambda n=n: nc.vector.tensor_tensor(out=pa[1][:, 3:n], in0=pa[0][:, 3:n], in1=pa[0][:, 1:n - 2], op=ALU.add), [pab], [pab])
                                E = pa[1][:, 9:9 + W]
                            if gg >= 2:
                                dve(lambda n=n: nc.vector.tensor_tensor(out=pa[2][:, 7:n], in0=pa[1][:, 7:n], in1=pa[1][:, 3:n - 4], op=ALU.add), [pab], [pab])
                                E = pa[2][:, 11:11 + W]
                            if gg >= 3:
                                dve(lambda n=n: nc.vector.tensor_tensor(out=pa[3][:, 15:n], in0=pa[2][:, 15:n], in1=pa[2][:, 7:n - 8], op=ALU.add), [pab], [pab])
                                E = pa[3][:, 15:15 + W]
                            dve(lambda E=E, gg=gg, W=W, so=so: nc.vector.tensor_tensor(out=ptm[:, 0:W], in0=E, in1=invt[:, gg, so:so + W], op=ALU.mult), [pab, invb], [ptmb, actb])
                            dve(lambda X=X, W=W: nc.vector.tensor_tensor(out=pl[:, 0:W], in0=ptm[:, 0:W], in1=X[:, 8:8 + W], op=ALU.subtract), [ptmb, zpb], [plb])
                            ps, psb = pp.get()
                            mm(ps[:, 0:W], wpl[:, gg, :], pl[:, 0:W], True, True, [plb, wres], [psb])
                            act(lambda ps=ps, gg=gg, W=W, so=so: nc.scalar.activation(out=brT_pool[:, gg, so:so + W], in_=ps[:, 0:W], func=AF.Copy, scale=psc[:, gg:gg + 1]), [psb, wres], [brb[1]])
                        for j in range(4):
                            Y = cy[:, j, :]
                            dve(lambda Y=Y, j=j, W=W: nc.vector.tensor_scalar(out=ctmp[:, 0:W], in0=Y[:, 0:W], scalar1=cw[:, j:j + 1], scalar2=None, op0=ALU.mult), [cyb, wres], [ctb, actb])
                            dve(lambda Y=Y, j=j, W=W: nc.vector.scalar_tensor_tensor(out=ctmp[:, 0:W], in0=Y[:, 1:1 + W], scalar=cw[:, 4 + j:5 + j], in1=ctmp[:, 0:W], op0=ALU.mult, op1=ALU.add), [cyb, ctb, wres], [ctb])
                            dve(lambda Y=Y, j=j, W=W: nc.vector.scalar_tensor_tensor(out=ctmp[:, 0:W], in0=Y[:, 2:2 + W], scalar=cw[:, 8 + j:9 + j], in1=ctmp[:, 0:W], op0=ALU.mult, op1=ALU.add), [cyb, ctb, wres], [ctb])
                            dve(lambda j=j, W=W, so=so: nc.vector.tensor_tensor(out=brT_conv[:, j, so:so + W], in0=ctmp[:, 0:W], in1=cb_[:, j, 0:W], op=ALU.mult), [ctb, cbb], [brb[3]])
                    for jb in range(4):
                        for br in range(4):
                            c0 = br * D + jb * 512
                            wg, wgb = wp.load(WG[:, c0:c0 + 512], 16, 512)
                            wb_, wbb = wp.load(WBR[BR_OFF[br]:BR_OFF[br] + brK[br] * 128, jb * 512:(jb + 1) * 512], brK[br], 512)
                            for jj in range(4):
                                j = jb * 4 + jj
                                pg, pgb = pp.get()
                                for kc in range(16):
                                    mm(pg[:, :], wg[:, kc, jj * 128:(jj + 1) * 128], hT[:, kc, :], kc == 0, kc == 15, [wgb, hb], [pgb])
                                pb, pbb = pp.get()
                                for kc in range(brK[br]):
                                    mm(pb[:, :], wb_[:, kc, jj * 128:(jj + 1) * 128], brT[br][:, kc, :], kc == 0, kc == brK[br] - 1, [wbb, brb[br]], [pbb])
                                g2 = gc % 2; gc += 1
                                act(lambda pg=pg, g2=g2, br=br, j=j: nc.scalar.activation(out=gt[g2][:], in_=pg[:, :], func=AF.Sigmoid, bias=bg[:, br * 16 + j:br * 16 + j + 1], scale=1.0), [pgb, wres], [gtb[g2]])
                                if br == 0:
                                    dve(lambda pb=pb, g2=g2, jj=jj: nc.vector.tensor_tensor(out=maccs[jj][:], in0=gt[g2][:], in1=pb[:, :], op=ALU.mult), [gtb[g2], pbb], [maccsb[jj]])
                                else:
                                    dve(lambda pb=pb, g2=g2: nc.vector.tensor_tensor(out=macc[:], in0=gt[g2][:], in1=pb[:, :], op=ALU.mult), [gtb[g2], pbb], [maccb])
                                    if br < 3:
                                        dve(lambda jj=jj: nc.vector.tensor_tensor(out=maccs[jj][:], in0=maccs[jj][:], in1=macc[:], op=ALU.add), [maccb, maccsb[jj]], [maccsb[jj]])
                                    else:
                                        dve(lambda jj=jj, j=j: nc.vector.tensor_tensor(out=mT[:, j, :], in0=maccs[jj][:], in1=macc[:], op=ALU.add), [maccb, maccsb[jj]], [mb_])
                    for jb in range(4):
                        wo, wob = wp.load(WO[:, jb * 512:(jb + 1) * 512], 16, 512)
                        for jj in range(4):
                            j = jb * 4 + jj
                            ps, psb = pp.get()
                            for kc in range(16):
                                mm(ps[:, :], wo[:, kc, jj * 128:(jj + 1) * 128], mT[:, kc, :], kc == 0, kc == 15, [wob, mb_], [psb])
                            dve(lambda ps=ps, j=j, ga=ga: nc.vector.scalar_tensor_tensor(out=xT[:, j, :], in0=ps[:, :], scalar=ga[:, j:j + 1], in1=xT[:, j, :], op0=ALU.mult, op1=ALU.add), [psb, xb, MVb], [xb])
                    norm_mod(pp, xT, xb, hT, hb, a2, b2, sq, sqb, rs, rsb, tmpf, tmpb)
                    for jb in range(11):
                        wa, wab = wp.load(WFI[:, jb * 512:(jb + 1) * 512], 16, 512)
                        wl, wlb = wp.load(WFI[:, DFF + jb * 512:DFF + (jb + 1) * 512], 16, 512)
                        for jj in range(4):
                            j = jb * 4 + jj
                            pa_, pab_ = pp.get()
                            for kc in range(16):
                                mm(pa_[:, :], wa[:, kc, jj * 128:(jj + 1) * 128], hT[:, kc, :], kc == 0, kc == 15, [wab, hb], [pab_])
                            pl_, plb_ = pp.get()
                            for kc in range(16):
                                mm(pl_[:, :], wl[:, kc, jj * 128:(jj + 1) * 128], hT[:, kc, :], kc == 0, kc == 15, [wlb, hb], [plb_])
                            g2 = gc % 2; gc += 1
                            act(lambda pa_=pa_, g2=g2: nc.scalar.activation(out=gt[g2][:], in_=pa_[:, :], func=AF.Silu), [pab_], [gtb[g2]])
                            dve(lambda pl_=pl_, g2=g2, j=j: nc.vector.tensor_tensor(out=actT[:, j, :], in0=gt[g2][:], in1=pl_[:, :], op=ALU.mult), [gtb[g2], plb_], [actb, zpb, pab, cyb, cbb, ctb, ptmb, invb])
                    for jb in range(4):
                        accs = [pp.get() for _ in range(4)]
                        for kb0, nk in ((0, 16), (16, 16), (32, 12)):
                            wf, wfb = wp.load(WFO[kb0 * 128:(kb0 + nk) * 128, jb * 512:(jb + 1) * 512], nk, 512)
                            for jj in range(4):
                                ps, psb = accs[jj]
                                for kc in range(nk):
                                    kg = kb0 + kc
                                    mm(ps[:, :], wf[:, kc, jj * 128:(jj + 1) * 128], actT[:, kg, :], kg == 0, kg == 43, [wfb, actb], [psb])
                        for jj in range(4):
                            j = jb * 4 + jj
                            ps, psb = accs[jj]
                            dve(lambda ps=ps, j=j, gf=gf: nc.vector.scalar_tensor_tensor(out=xT[:, j, :], in0=ps[:, :], scalar=gf[:, j:j + 1], in1=xT[:, j, :], op0=ALU.mult, op1=ALU.add), [psb, xb, MVb], [xb])
                    spdma(XOUT[:, tok0:tok0 + TT].rearrange("(kc p) t -> p kc t", p=128), xT[:], [xb], [])
                S.flush()

        S.flush()
        for l in range(nlayers):
            phase_mod(l)
        if dbg and "dbg_MV" in dbg:
            dmv = nc.dram_tensor("dbg_MV", [128, L * 12, 16], F32, kind="ExternalOutput").ap()
            spdma(dmv, MV[:], [MVb], [])
            S.flush()
        if stop_after == "p0":
            nlayers = 0
        for l in range(nlayers):
            phase1(l)
            if stop_after == ("p1", l):
                break
            phase2(l)
            if stop_after == ("p2", l):
                break
            phase3(l)
            if stop_after == ("p3", l):
                break
    return nc


def _rope_tables():
    T_ = NS
    rows = T_ // 64
    row = np.repeat(np.arange(rows), 64).astype(np.float32)
    col = np.tile(np.arange(64), rows).astype(np.float32)
    inv = (10000.0 ** (-np.arange(8, dtype=np.float32) / 8)).astype(np.float32)
    ang_r = row[:, None] * inv
    ang_c = col[:, None] * inv
    ang = np.stack([ang_r, ang_r, ang_c, ang_c], axis=1).reshape(T_, 32)
    cos = np.ones((96, NTOK), np.float32)
    sin = np.zeros((96, NTOK), np.float32)
    cos[64:, :NS] = np.cos(ang).T
    sin[64:, :NS] = np.sin(ang).T
    return cos, sin


def _consts():
    cos, sin = _rope_tables()
    prot = np.zeros((96, 96), np.float32)
    for a in range(2):
        for f in range(8):
            i0 = 64 + a * 16 + f
            i1 = 64 + a * 16 + 8 + f
            prot[i1, i0] = -1.0
            prot[i0, i1] = 1.0
    sel = np.zeros((65, 64), np.float32)
    sel[64, :] = 1.0

    def inv_cnt(Tseq):
        t = np.arange(Tseq)
        out = np.zeros((4, Tseq), np.float32)
        for gi, w in enumerate((2, 4, 8, 16)):
            lo = np.clip(t - w // 2, 0, Tseq - 1)
            hi = np.clip(t + w // 2 - 1, 0, Tseq - 1)
            out[gi] = 1.0 / (hi - lo + 1).astype(np.float32)
        return out
    ic = inv_cnt(NS)
    ip = inv_cnt(256)
    tab = np.zeros((4, 4, 512), np.float32)
    tab[0] = ic[:, 0:512]
    tab[1] = ic[:, 512:1024]
    tab[2] = ic[:, NS - 512:NS]
    tab[3] = np.concatenate([ip, ip], axis=1)
    invtab = np.ascontiguousarray(np.broadcast_to(tab[:, None], (4, 128, 4, 512)))
    return cos, sin, prot, sel, invtab


def _vecT(v, n):
    return np.ascontiguousarray(v.reshape(v.shape[0], n, 128).transpose(0, 2, 1))


def make_in_maps(inp):
    f = lambda a: np.ascontiguousarray(np.asarray(a, dtype=np.float32))
    cos, sin, prot, sel, invtab = _consts()
    shared = {
        "w_mod": f(inp["w_mod"]), "b_modT": _vecT(f(inp["b_mod"]), 96),
        "nmixT": _vecT(f(inp["norm_mix_w"]), 16), "nffnT": _vecT(f(inp["norm_ffn_w"]), 16),
        "w_in": f(inp["w_in"]), "b_gateT": _vecT(f(inp["b_gate"]), 64),
        "qawT": _vecT(f(inp["q_a_norm_w"]), 6), "kvawT": _vecT(f(inp["kv_a_norm_w"]), 2),
        "kvaw_row": f(inp["kv_a_norm_w"]),
        "w_uq": f(inp["w_uq"]), "w_ukv": f(inp["w_ukv"]),
        "qnw": f(inp["q_norm_w"]).reshape(L, 96, 1), "knw": f(inp["k_norm_w"]).reshape(L, 96, 1),
        "w_pool": f(inp["w_pool"]), "pscaleT": _vecT(f(inp["pool_scale"]), 4),
        "sguwT": _vecT(f(inp["sgu_norm_w"]), 4),
        "w_spT": np.ascontiguousarray(f(inp["w_spatial"]).transpose(0, 1, 3, 2)),
        "b_sp": f(inp["b_spatial"]).reshape(L, 512),
        "convwT": np.ascontiguousarray(f(inp["conv_w"]).reshape(L, 3, 4, 128).transpose(0, 3, 1, 2).reshape(L, 128, 12)),
        "w_br_attn": f(inp["w_br_attn"]), "w_br_pool": f(inp["w_br_pool"]),
        "w_br_sgu": f(inp["w_br_sgu"]), "w_br_conv": f(inp["w_br_conv"]),
        "w_out": f(inp["w_out"]), "w_ffn_in": f(inp["w_ffn_in"]), "w_ffn_out": f(inp["w_ffn_out"]),
        "cosT": cos, "sinT": sin, "prot": prot, "sel": sel, "invtab": invtab,
    }
    xs = f(inp["x_sample"]); xp = f(inp["x_prompt"])
    cckv = f(inp["cache_ckv"]); ckpe = f(inp["cache_kpe"])
    c = f(inp["c"]); cctx = f(inp["c_ctx"])
    maps = []
    for b in range(8):
        xT = np.concatenate([xs[b].T, xp[2 * b].T, xp[2 * b + 1].T], axis=1)
        cv = np.stack([c[b], cctx], axis=1)
        cT = cv.reshape(16, 128, 2).transpose(1, 0, 2)
        m = dict(shared)
        m["xT"] = np.ascontiguousarray(xT)
        m["cT"] = np.ascontiguousarray(cT)
        m["cckvT"] = np.ascontiguousarray(cckv[b].transpose(0, 2, 1))
        m["ckpe"] = np.ascontiguousarray(ckpe[b])
        m["ckpeT"] = np.ascontiguousarray(ckpe[b].transpose(0, 2, 1))
        maps.append(m)
    return maps


def kernel(**inputs):
    nc = build()
    maps = make_in_maps(inputs)
    res = run_bass_kernel_spmd(nc, maps, core_ids=list(range(8)))
    y_prompt = np.zeros((16, 256, D), np.float32)
    y_sample = np.zeros((8, NS, D), np.float32)
    s_ckv = np.zeros((16, L, 256, 256), np.float32)
    s_kpe = np.zeros((16, L, 256, 32), np.float32)
    for b in range(8):
        r = res.results[b]
        yT = np.asarray(r["yT"])
        y_sample[b] = yT[:, :NS].T
        y_prompt[2 * b] = yT[:, NS:NS + 256].T
        y_prompt[2 * b + 1] = yT[:, NS + 256:].T
        s_ckv[2 * b:2 * b + 2] = np.asarray(r["st_ckv"])
        s_kpe[2 * b:2 * b + 2] = np.asarray(r["st_kpe"])
    return (y_prompt, y_sample, s_ckv, s_kpe)
```

```python
import numpy as np
from contextlib import ExitStack
import concourse.bass as bass
import concourse.mybir as mybir
from concourse.bass_utils import run_bass_kernel_spmd

F32, BF16 = mybir.dt.float32, mybir.dt.bfloat16
AF = mybir.ActivationFunctionType
ALU = mybir.AluOpType
AX = mybir.AxisListType

D = 2048; L = 2; NT = 9; TT = 512; NTOK = 4608; NS = 4096
NIN = 12320; DFF = 5632; H = 16
NKEY = 4864; NKB = 38
EPS = 1e-6
ATTN_SCALE = 96 ** -0.5
OFF_POOL, OFF_SGU, OFF_CONV, OFF_GATE = 1056, 1568, 2592, 4128
NDS = 8
SAME_ENGINE_SYNC = True


class Buf:
    __slots__ = ("lw", "rd", "excl")

    def __init__(self, excl=False):
        self.lw = None
        self.rd = {}
        self.excl = excl


class Op:
    __slots__ = ("eng", "fn", "deps", "inc", "dma", "sem", "val", "epoch")


class Eng:
    pass


class _Stop(Exception):
    pass


class Sched:
    def __init__(self, nc, es):
        self.nc = nc
        self.E = {}
        for name, obj in (("pe", nc.tensor), ("act", nc.scalar), ("dve", nc.vector),
                          ("pool", nc.gpsimd), ("sp", nc.sync)):
            e = Eng()
            e.name = name; e.obj = obj
            e.sem = es.enter_context(nc.semaphore("s_" + name))
            e.count = 0; e.known = {}
            e.dsems = [es.enter_context(nc.semaphore("d_%s%d" % (name, i))) for i in range(NDS)] \
                if name in ("pool", "sp") else []
            e.duse = [0] * NDS; e.dcount = 0
            self.E[name] = e
        self.ops = []
        self.epoch = 0
        self.nid = 0

    def op(self, eng, fn, reads=(), writes=(), dma=False):
        E = self.E[eng]
        X = Op()
        X.eng = E; X.fn = fn; X.inc = False; X.dma = dma; X.sem = None; X.val = 0; X.epoch = self.epoch
        deps = []
        xr = [b for b in reads if b.excl]
        if xr:
            reads = [b for b in reads if not b.excl]
            writes = list(writes) + xr
        for b in reads:
            if b.lw is not None:
                deps.append(b.lw)
        for b in writes:
            if b.lw is not None:
                deps.append(b.lw)
            deps.extend(b.rd.values())
        out = []
        seen = set()
        for P in deps:
            if P.epoch != self.epoch or id(P) in seen or P is X:
                continue
            seen.add(id(P))
            if (not P.dma) and (not dma) and P.eng is E:
                if eng != "dve" or not SAME_ENGINE_SYNC:
                    continue
            if not P.dma:
                P.inc = True
            out.append(P)
        X.deps = out
        if dma:
            self.nid += 1
            key = ("d", self.nid)
        else:
            key = eng
        for b in reads:
            b.rd[key] = X
        for b in writes:
            b.lw = X
            b.rd = {}
        self.ops.append(X)
        return X

    def flush(self):
        last = {}
        for X in self.ops:
            if not X.dma:
                last[X.eng.name] = X
        for X in last.values():
            X.inc = True
        for X in self.ops:
            E = X.eng
            for P in X.deps:
                if E.known.get(P.sem, 0) < P.val:
                    E.obj.wait_ge(P.sem, P.val)
                    E.known[P.sem] = P.val
            if X.dma:
                slot = E.dcount % NDS
                E.dcount += 1
                sem = E.dsems[slot]
                E.duse[slot] += 1
                val = 16 * E.duse[slot]
                if val > 16 and E.known.get(sem, 0) < val - 16:
                    E.obj.wait_ge(sem, val - 16)
                    E.known[sem] = val - 16
                X.fn().then_inc(sem, 16)
                X.sem = sem; X.val = val
            else:
                ins = X.fn()
                if X.inc:
                    E.count += 1
                    ins.then_inc(E.sem, 1)
                    X.sem = E.sem; X.val = E.count
        self.ops = []
        for E in self.E.values():
            for P in self.E.values():
                if P is not E and P.count > 0 and E.known.get(P.sem, 0) < P.count:
                    E.obj.wait_ge(P.sem, P.count)
                    E.known[P.sem] = P.count
                for s in range(NDS if P.dsems else 0):
                    v = 16 * P.duse[s]
                    if v > 0 and E.known.get(P.dsems[s], 0) < v:
                        E.obj.wait_ge(P.dsems[s], v)
                        E.known[P.dsems[s]] = v
        self.epoch += 1


def build(dbg=None):
    nc = bass.Bass("TRN2", target_bir_lowering=False)

    def din(name, shape):
        return nc.dram_tensor(name, list(shape), F32, kind="ExternalInput").ap()

    def dout(name, shape):
        return nc.dram_tensor(name, list(shape), F32, kind="ExternalOutput").ap()

    def dscr(name, shape, dt):
        kind = "ExternalOutput" if (dbg and name in dbg) else "Internal"
        return nc.dram_tensor(name, list(shape), dt, kind=kind).ap()

    xT_in = din("xT", [D, NTOK])
    cT_in = din("cT", [128, 16, 2])
    cckvT_in = din("cckvT", [L, 256, 256])
    ckpe_in = din("ckpe", [L, 256, 32])
    ckpeT_in = din("ckpeT", [L, 32, 256])
    w_mod = din("w_mod", [L, D, 6 * D])
    b_modT = din("b_modT", [L, 128, 96])
    nmixT = din("nmixT", [L, 128, 16])
    nffnT = din("nffnT", [L, 128, 16])
    w_in = din("w_in", [L, D, NIN])
    b_gateT = din("b_gateT", [L, 128, 64])
    qawT = din("qawT", [L, 128, 6])
    kvawT = din("kvawT", [L, 128, 2])
    kvaw_row = din("kvaw_row", [L, 256])
    w_uq = din("w_uq", [L, 768, 1536])
    w_ukv = din("w_ukv", [L, 256, 2048])
    qnw = din("qnw", [L, 96, 1])
    knw = din("knw", [L, 96, 1])
    w_pool = din("w_pool", [L, 4, 128, 128])
    pscaleT = din("pscaleT", [L, 128, 4])
    sguwT = din("sguwT", [L, 128, 4])
    w_spT = din("w_spT", [L, 4, 128, 128])
    b_sp = din("b_sp", [L, 512])
    convwT = din("convwT", [L, 128, 12])
    w_br = [din("w_br_attn", [L, 1024, D]), din("w_br_pool", [L, 512, D]),
            din("w_br_sgu", [L, 512, D]), din("w_br_conv", [L, 512, D])]
    w_out = din("w_out", [L, D, D])
    w_ffn_in = din("w_ffn_in", [L, D, 2 * DFF])
    w_ffn_out = din("w_ffn_out", [L, DFF, D])
    cos_in = din("cosT", [96, NTOK])
    sin_in = din("sinT", [96, NTOK])
    prot_in = din("prot", [96, 96])
    sel_in = din("sel", [65, 64])
    inv_in = din("invtab", [4, 128, 4, 512])

    yT = dout("yT", [D, NTOK])
    st_ckv = dout("st_ckv", [2, L, 256, 256])
    st_kpe = dout("st_kpe", [2, L, 256, 32])

    XT1 = dscr("XT1", [D, NTOK], F32)
    Qsc = dscr("Qsc", [96, H, NTOK], BF16)
    Ksc = dscr("Ksc", [96, H, NKEY], BF16)
    Vsc = dscr("Vsc", [NKB, 128, H * 65], BF16)
    SCsc = dscr("SCsc", [NKB, 128, H], F32)
    ATTNsc = dscr("ATTNsc", [1024, NTOK], BF16)
    POOLsc = dscr("POOLsc", [512, NTOK], F32)
    SGUsc = dscr("SGUsc", [512, NTOK], BF16)
    CONVy = dscr("CONVy", [512, NTOK], F32)
    CONVb = dscr("CONVb", [512, NTOK], F32)
    WG = dscr("WG", [D, 4 * D], BF16)
    WBR = dscr("WBR", [2560, D], BF16)
    WO = dscr("WO", [D, D], BF16)
    WFI = dscr("WFI", [D, 2 * DFF], BF16)
    WFO = dscr("WFO", [DFF, D], BF16)
    BR_OFF = [0, 1024, 1536, 2048]

    stop_after = (dbg or {}).get("stop_after", None)
    nlayers = (dbg or {}).get("nlayers", L)

    with ExitStack() as top:
        S = Sched(nc, top)
        op = S.op

        uid = [0]

        def T(es, name, shape, dt):
            uid[0] += 1
            return es.enter_context(nc.sbuf_tensor("sb_%s_%d" % (name, uid[0]), list(shape), dt))

        def pe(fn, r, w):
            return op("pe", fn, r, w)

        def mm(out, lhsT, rhs, start, stop, r, w):
            return op("pe", lambda: nc.tensor.matmul(out, lhsT=lhsT, rhs=rhs, start=start, stop=stop), r, w)

        def act(fn, r, w):
            return op("act", fn, r, w)

        def dve(fn, r, w):
            return op("dve", fn, r, w)

        def spdma(out, in_, r, w):
            return op("sp", lambda: nc.sync.dma_start(out=out, in_=in_), r, w, dma=True)

        def pldma(out, in_, r, w):
            return op("pool", lambda: nc.gpsimd.dma_start(out=out, in_=in_), r, w, dma=True)

        MV = T(top, "MV", [128, L * 2 * 6, 16], F32)
        MVb = Buf()
        ones_bf = T(top, "ones_bf", [128, 128], BF16)
        prot_bf = T(top, "prot_bf", [96, 96], BF16)
        sel_f = T(top, "sel_f", [65, 64], F32)
        constb = Buf()
        dve(lambda: nc.vector.memset(ones_bf[:], 1.0), [], [constb])
        pldma(prot_bf[:], prot_in, [], [constb])
        spdma(sel_f[:], sel_in, [], [constb])

        def mvec(l, g, i):
            k = (l * 2 + g) * 6 + i
            return MV[:, k, :]

        class PsumPool:
            def __init__(self, es, n=8):
                uid[0] += 1
                self.t = [es.enter_context(nc.psum_tensor("ps%d_%d" % (i, uid[0]), [128, 512], F32)) for i in range(n)]
                self.b = [Buf(excl=True) for _ in range(n)]
                self.i = 0

            def get(self):
                k = self.i % len(self.t)
                self.i += 1
                return self.t[k], self.b[k]

        class WPool:
            def __init__(self, es, n, dt=BF16, cols=512, name="wslot"):
                self.t = [T(es, "%s%d" % (name, i), [128, 16, cols], dt) for i in range(n)]
                self.b = [Buf() for _ in range(n)]
                self.i = 0

            def load(self, src, nk, ncols):
                k = self.i % len(self.t)
                self.i += 1
                t, b = self.t[k], self.b[k]
                pldma(t[:, 0:nk, 0:ncols], src.rearrange("(kc p) n -> p kc n", p=128), [], [b])
                return t, b

        def rstd_from_psum(ps_ap, out_ap, n, psb, outb):
            act(lambda: nc.scalar.activation(out=out_ap, in_=ps_ap, func=AF.Sqrt, scale=1.0 / n, bias=EPS), [psb], [outb])
            dve(lambda: nc.vector.reciprocal(out=out_ap, in_=out_ap), [outb], [outb])

        def phase_mod(l):
            with ExitStack() as es:
                pp = PsumPool(es, 1)
                psm, psmb = pp.get()
                cT = T(es, "cT", [128, 16, 2], F32); cTb = Buf()
                sT = T(es, "sT", [128, 16, 2], BF16); sTb = Buf()
                bm = T(es, "bm", [128, 96], F32); bmb = Buf()
                nm = T(es, "nm", [128, 16], F32); nf = T(es, "nf", [128, 16], F32); nb_ = Buf()
                modT = T(es, "modT", [128, 2, 96], F32); modb = Buf()
                wp = WPool(es, 3, dt=BF16, cols=512, name="wm")
                spdma(cT[:], cT_in, [], [cTb])
                spdma(bm[:], b_modT[l], [], [bmb])
                spdma(nm[:], nmixT[l], [], [nb_])
                spdma(nf[:], nffnT[l], [], [nb_])
                act(lambda: nc.scalar.activation(out=sT[:], in_=cT[:], func=AF.Silu), [cTb], [sTb])
                for cb in range(24):
                    wt, wb = wp.load(w_mod[l][:, cb * 512:(cb + 1) * 512], 16, 512)
                    for jj in range(4):
                        j = cb * 4 + jj
                        for kc in range(16):
                            mm(psm[:, 2 * j:2 * j + 2], wt[:, kc, jj * 128:(jj + 1) * 128], sT[:, kc, :],
                               kc == 0, kc == 15, [wb, sTb], [psmb])
                pv = psm[:, 0:192].rearrange("p (j g) -> p j g", g=2)
                for g in range(2):
                    dve(lambda g=g: nc.vector.tensor_tensor(out=modT[:, g, :], in0=pv[:, :, g], in1=bm[:], op=ALU.add),
                        [psmb, bmb], [modb])
                for g in range(2):
                    def mg(i, g=g):
                        return modT[:, g, i * 16:(i + 1) * 16]
                    dve(lambda g=g, mg=mg: nc.vector.scalar_tensor_tensor(out=mvec(l, g, 0), in0=mg(1), scalar=1.0, in1=nm[:], op0=ALU.add, op1=ALU.mult), [modb, nb_], [MVb])
                    dve(lambda g=g, mg=mg: nc.vector.tensor_copy(out=mvec(l, g, 1), in_=mg(0)), [modb], [MVb])
                    dve(lambda g=g, mg=mg: nc.vector.tensor_copy(out=mvec(l, g, 2), in_=mg(2)), [modb], [MVb])
                    dve(lambda g=g, mg=mg: nc.vector.scalar_tensor_tensor(out=mvec(l, g, 3), in0=mg(4), scalar=1.0, in1=nf[:], op0=ALU.add, op1=ALU.mult), [modb, nb_], [MVb])
                    dve(lambda g=g, mg=mg: nc.vector.tensor_copy(out=mvec(l, g, 4), in_=mg(3)), [modb], [MVb])
                    dve(lambda g=g, mg=mg: nc.vector.tensor_copy(out=mvec(l, g, 5), in_=mg(5)), [modb], [MVb])
                S.flush()

        def norm_ss_chunk(ps, psb, xT, xbl, sq, sqb, kc):
            k2 = kc % 2
            act(lambda: nc.scalar.activation(out=sq[k2][:], in_=xT[:, kc, :], func=AF.Square), [xbl[kc]], [sqb[k2]])
            mm(ps[:, :], ones_bf[:, :], sq[k2][:], kc == 0, kc == 15, [sqb[k2], constb], [psb])

        def norm_apply(ps, psb, xT, xbl, hT, hb, a_ap, b_ap, rs, rsb, tmpf, tmpb):
            rstd_from_psum(ps[:, :], rs[:], D, psb, rsb)
            for kc in range(16):
                k2 = kc % 2
                dve(lambda kc=kc, k2=k2: nc.vector.scalar_tensor_tensor(out=tmpf[k2][:], in0=xT[:, kc, :], scalar=a_ap[:, kc:kc + 1], in1=rs[:], op0=ALU.mult, op1=ALU.mult), [xbl[kc], rsb, MVb], [tmpb[k2]])
                act(lambda kc=kc, k2=k2: nc.scalar.activation(out=hT[:, kc, :], in_=tmpf[k2][:], func=AF.Identity, bias=b_ap[:, kc:kc + 1], scale=1.0), [tmpb[k2], MVb], [hb[kc]])

        def norm_mod(pp, xT, xb, hT, hb, a_ap, b_ap, sq, sqb, rs, rsb, tmpf, tmpb, psx=None):
            ps, psb = psx if psx is not None else pp.get()
            xbl = xb if isinstance(xb, list) else [xb] * 16
            for kc in range(16):
                norm_ss_chunk(ps, psb, xT, xbl, sq, sqb, kc)
            norm_apply(ps, psb, xT, xbl, hT, hb, a_ap, b_ap, rs, rsb, tmpf, tmpb)

        def phase1(l):
            XIN = xT_in if l == 0 else XT1
            with ExitStack() as es:
                pp = PsumPool(es, 8)
                wp = WPool(es, 3)
                U = T(es, "U", [128, 8192], F32); Ub = Buf(); xpb = [Buf() for _ in range(4)]
                xT = U[:, :].rearrange("p (k t) -> p k t", t=512)
                zq = U[:, 0:3072].rearrange("p (k t) -> p k t", t=512); zqb = Buf()
                u_t = U[:, 3072:5120].rearrange("p (k t) -> p k t", t=512); ub = Buf()
                cg = U[:, 5120:7168].rearrange("p (k t) -> p k t", t=512); cgb = Buf()
                zkv = U[:, 7168:8192].rearrange("p (k t) -> p k t", t=512); zkvb = Buf()
                hT = T(es, "hT", [128, 16, 512], BF16); hb = [Buf() for _ in range(16)]
                wuq = T(es, "wuq", [128, 6, 1536], BF16)
                wukv = T(es, "wukv", [128, 2, 2048], BF16)
                wspT = T(es, "wspT", [128, 4, 128], BF16)
                qaw = T(es, "qaw", [128, 6], F32); kvaw = T(es, "kvaw", [128, 2], F32)
                wq96 = T(es, "wq96", [96, 1], F32); wk96 = T(es, "wk96", [96, 1], F32)
                sguw = T(es, "sguw", [128, 4], F32)
                bspb = T(es, "bspb", [128, 512], F32)
                kvawb = T(es, "kvawb", [128, 256], F32)
                wres = Buf()
                pldma(wuq[:], w_uq[l].rearrange("(kc p) n -> p kc n", p=128), [], [wres])
                pldma(wukv[:], w_ukv[l].rearrange("(kc p) n -> p kc n", p=128), [], [wres])
                pldma(wspT[:], w_spT[l].rearrange("g p q -> p g q"), [], [wres])
                spdma(qaw[:], qawT[l], [], [wres]); spdma(kvaw[:], kvawT[l], [], [wres])
                spdma(wq96[:], qnw[l], [], [wres]); spdma(wk96[:], knw[l], [], [wres])
                spdma(sguw[:], sguwT[l], [], [wres])
                spdma(bspb[:], b_sp[l].partition_broadcast(128), [], [wres])
                spdma(kvawb[:], kvaw_row[l].partition_broadcast(128), [], [wres])

                sq = [T(es, "sq%d" % i, [128, 512], BF16) for i in range(2)]; sqb = [Buf(), Buf()]
                rs = T(es, "rs", [128, 512], F32); rsb = Buf()
                tmpf = [T(es, "tmpf%d" % i, [128, 512], F32) for i in range(2)]; tmpb = [Buf(), Buf()]
                cqT = T(es, "cqT", [128, 6, 512], BF16); cqb = Buf()
                ckvT = T(es, "ckvT", [128, 2, 512], BF16); ckvb = Buf()
                cosT = T(es, "cosT", [96, 512], F32); sinT = T(es, "sinT", [96, 512], F32); csb = Buf()
                kpw = T(es, "kpw", [96, 512], BF16); kpwb = Buf()
                krope = T(es, "krope", [96, 512], BF16); kropeb = Buf()
                t1 = [T(es, "t1%d" % i, [96, 512], F32) for i in range(2)]; t2 = [T(es, "t2%d" % i, [96, 512], F32) for i in range(2)]; t1b = [Buf(), Buf()]; t2b = [Buf(), Buf()]
                sq96 = [T(es, "sq96%d" % i, [96, 512], BF16) for i in range(2)]; sq96b = [Buf(), Buf()]
                qw = [T(es, "qw%d" % i, [96, 512], BF16) for i in range(2)]; qwb = [Buf(), Buf()]
                qn = [T(es, "qn%d" % i, [96, 512], BF16) for i in range(2)]; qnb = [Buf(), Buf()]
                r96 = [T(es, "r96%d" % i, [96, 512], F32) for i in range(2)]; r96b = [Buf(), Buf()]
                Qh = [T(es, "Qh%d" % i, [96, 512], BF16) for i in range(2)]; Qhb = [Buf(), Buf()]
                Kh = [T(es, "Kh%d" % i, [64, 512], BF16) for i in range(2)]; Khb = [Buf(), Buf()]
                Vp = [T(es, "Vp%d" % i, [128, 16, 65], BF16) for i in range(2)]; Vpb = [Buf(), Buf()]
                sqk = T(es, "sqk", [128, 4, 64], F32); sqkb = Buf()
                ssk = T(es, "ssk", [128, 16], F32); sskb = Buf()
                sck = [T(es, "sck%d" % i, [128, 16], F32) for i in range(2)]; sckb = [Buf(), Buf()]
                sspe = T(es, "sspe", [128, 4], F32); sspeb = Buf()
                tm32 = T(es, "tm32", [128, 32], F32); tm32b = Buf()
                tm256 = T(es, "tm256", [128, 256], F32); tm256b = Buf()
                ss1 = T(es, "ss1", [128, 1], F32); ss1b = Buf()
                stc = [T(es, "stc%d" % i, [128, 256], F32) for i in range(2)]; stcb = [Buf(), Buf()]
                stk = [T(es, "stk%d" % i, [128, 32], F32) for i in range(2)]; stkb = [Buf(), Buf()]
                ptmp = [T(es, "ptmp%d" % i, [128, 512], F32) for i in range(2)]; ptmpb = [Buf(), Buf()]
                vg = [T(es, "vg%d" % i, [128, 512], F32) for i in range(2)]; vgb = [Buf(), Buf()]
                vh = [T(es, "vh%d" % i, [128, 512], BF16) for i in range(2)]; vhb = [Buf(), Buf()]
                ssv = T(es, "ssv", [128, 4], F32); ssvb = [Buf() for _ in range(4)]
                mx = [T(es, "mx%d" % i, [128, 128], F32) for i in range(2)]; mxb = [Buf(), Buf()]
                sgo = T(es, "sgo", [128, 4, 512], BF16); sgob = Buf()
                cckT = T(es, "cckT", [128, 2, 256], BF16); cckb = Buf()
                ckp96 = T(es, "ckp96", [96, 256], F32); ckp96b = Buf()
                ckpt = T(es, "ckpt", [128, 2, 32], F32); ckptb = Buf()

                cut = (dbg or {}).get("p1_cut", None)

                def stage(k):
                    if cut == k:
                        raise _Stop()
                dve(lambda: nc.vector.memset(kpw[:], 0.0), [], [kpwb])
                dve(lambda: nc.vector.memset(krope[:], 0.0), [], [kropeb])
                for i in range(2):
                    dve(lambda i=i: nc.vector.memset(Vp[i][:], 1.0), [], [Vpb[i]])
                cnt = {"q": 0, "k": 0, "v": 0, "p": 0, "s": 0}

                def gen_kv(ckv_ap, kr_t, kr_b, T_, key0, sspe_ap):
                    nsub = T_ // 128
                    import os
                    GK = os.environ.get("GK", "kvs")
                    for h in (range(H) if "k" in GK else []):
                        ps, psb = pp.get()
                        for kc in range(2):
                            mm(ps[0:64, 0:T_], wukv[:, kc, h * 128:h * 128 + 64], ckv_ap[:, kc, :], kc == 0, kc == 1, [wres, ckvb, cckb], [psb])
                        kb_ = cnt["k"] % 2; cnt["k"] += 1
                        act(lambda ps=ps, kb_=kb_: nc.scalar.activation(out=Kh[kb_][:, 0:T_], in_=ps[0:64, 0:T_], func=AF.Copy, scale=wk96[0:64, 0:1]), [psb, wres], [Khb[kb_]])
                        spdma(Ksc[0:64, h, key0:key0 + T_], Kh[kb_][:, 0:T_], [Khb[kb_]], [])
                        spdma(Ksc[64:96, h, key0:key0 + T_], kr_t[64:96, 0:T_], [kr_b], [])
                    for s in (range(nsub) if "v" in GK else []):
                        vb_ = cnt["v"] % 2; cnt["v"] += 1
                        for c in range(4):
                            ps, psb = pp.get()
                            for kc in range(2):
                                mm(ps[:, :], ckv_ap[:, kc, s * 128:(s + 1) * 128], wukv[:, kc, c * 512:(c + 1) * 512], kc == 0, kc == 1, [wres, ckvb, cckb], [psb])
                            pv = ps[:, :].rearrange("p (h d) -> p h d", d=128)
                            dve(lambda pv=pv, c=c, vb_=vb_: nc.vector.tensor_copy(out=Vp[vb_][:, 4 * c:4 * c + 4, 0:64], in_=pv[:, :, 64:128]), [psb], [Vpb[vb_]])
                            if "s" in GK:
                                act(lambda pv=pv: nc.scalar.activation(out=sqk[:], in_=pv[:, :, 0:64], func=AF.Square), [psb], [sqkb])
                                dve(lambda c=c: nc.vector.reduce_sum(out=ssk[:, 4 * c:4 * c + 4], in_=sqk[:], axis=AX.X), [sqkb], [sskb])
                        sb_ = cnt["s"] % 2; cnt["s"] += 1
                        if "s" not in GK:
                            kb = key0 // 128 + s
                            spdma(Vsc[kb], Vp[vb_][:].rearrange("p h c -> p (h c)"), [Vpb[vb_]], [])
                            continue
                        dve(lambda s=s: nc.vector.tensor_scalar(out=ssk[:], in0=ssk[:], scalar1=sspe_ap[:, s:s + 1], scalar2=None, op0=ALU.add), [sskb, sspeb], [sskb])
                        act(lambda: nc.scalar.activation(out=ssk[:], in_=ssk[:], func=AF.Sqrt, scale=1.0 / 96, bias=EPS), [sskb], [sskb])
                        dve(lambda: nc.vector.reciprocal(out=ssk[:], in_=ssk[:]), [sskb], [sskb])
                        dve(lambda sb_=sb_: nc.vector.tensor_scalar(out=sck[sb_][:], in0=ssk[:], scalar1=ATTN_SCALE, scalar2=None, op0=ALU.mult), [sskb], [sckb[sb_]])
                        kb = key0 // 128 + s
                        spdma(SCsc[kb], sck[sb_][:], [sckb[sb_]], [])
                        spdma(Vsc[kb], Vp[vb_][:].rearrange("p h c -> p (h c)"), [Vpb[vb_]], [])

                if cut != 0:
                    pldma(cckT[:], cckvT_in[l].rearrange("(kc p) t -> p kc t", p=128), [], [cckb])
                    spdma(ckp96[64:96, :], ckpeT_in[l], [], [ckp96b])
                    spdma(ckpt[:], ckpe_in[l].rearrange("(s p) d -> p s d", p=128), [], [ckptb])
                    import os
                    if "act" in os.environ.get("P1A", "act,sq"):
                        act(lambda: nc.scalar.activation(out=krope[64:96, 0:256], in_=ckp96[64:96, :], func=AF.Copy, scale=wk96[64:96, 0:1]), [ckp96b, wres, kropeb], [kropeb])
                    for s in (range(2) if "sq" in os.environ.get("P1A", "act,sq") else []):
                        act(lambda s=s: nc.scalar.activation(out=tm32[:], in_=ckpt[:, s, :], func=AF.Square), [ckptb], [tm32b])
                        dve(lambda s=s: nc.vector.reduce_sum(out=sspe[:, s:s + 1], in_=tm32[:], axis=AX.X), [tm32b], [sspeb])
                try:
                  stage(0)
                  stage(1)
                  gen_kv(cckT, krope, kropeb, 256, 4096, sspe)
                  stage(2)
                  pieces = ((0, 6, zqb), (6, 10, ub), (10, 14, cgb), (14, 16, zkvb))

                  def xload(ti, pi):
                      k0, k1, tb = pieces[pi]
                      t0_ = ti * TT
                      spdma(xT[:, k0:k1, :], XIN[k0 * 128:k1 * 128, t0_:t0_ + TT].rearrange("(kc p) t -> p kc t", p=128), [], [xpb[pi], tb])
                  for i in range(NT):
                    g = 0 if i < 8 else 1
                    tok0 = i * TT
                    key0 = tok0 if i < 8 else 4352
                    a1 = mvec(l, g, 0); b1 = mvec(l, g, 1)
                    if i == 0:
                        for pi in range(4):
                            xload(0, pi)
                    spdma(cosT[:], cos_in[:, tok0:tok0 + TT], [], [csb])
                    spdma(sinT[:], sin_in[:, tok0:tok0 + TT], [], [csb])
                    norm_mod(pp, xT, [xpb[0]] * 6 + [xpb[1]] * 4 + [xpb[2]] * 4 + [xpb[3]] * 2, hT, hb, a1, b1, sq, sqb, rs, rsb, tmpf, tmpb)

                    stage(3)
                    def zgroup(wt, wb, c0, m, ps, psb, n=TT):
                        for kc in range(16):
                            mm(ps[0:m, 0:n], wt[:, kc, c0:c0 + m], hT[:, kc, :], kc == 0, kc == 15, [wb, hb[kc]], [psb])

                    wA, wAb = wp.load(w_in[l][:, 0:512], 16, 512)
                    wB, wBb = wp.load(w_in[l][:, 512:768], 16, 256)
                    pss, pssb = pp.get()
                    for j in range(6):
                        wt, wb, c0 = (wA, wAb, j * 128) if j < 4 else (wB, wBb, (j - 4) * 128)
                        ps, psb = pp.get()
                        zgroup(wt, wb, c0, 128, ps, psb)
                        k2 = j % 2
                        act(lambda ps=ps, k2=k2: nc.scalar.activation(out=sq[k2][:], in_=ps[:, :], func=AF.Square), [psb], [sqb[k2]])
                        dve(lambda ps=ps, j=j: nc.vector.tensor_copy(out=zq[:, j, :], in_=ps[:, :]), [psb], [zqb])
                        mm(pss[:, :], ones_bf[:, :], sq[k2][:], j == 0, j == 5, [sqb[k2], constb], [pssb])
                    rstd_from_psum(pss[:, :], rs[:], 768, pssb, rsb)
                    for j in range(6):
                        dve(lambda j=j: nc.vector.scalar_tensor_tensor(out=cqT[:, j, :], in0=zq[:, j, :], scalar=qaw[:, j:j + 1], in1=rs[:], op0=ALU.mult, op1=ALU.mult), [zqb, rsb, wres], [cqb])
                    stage(4)
                    wC, wCb = wp.load(w_in[l][:, 768:1056], 16, 288)
                    pss, pssb = pp.get()
                    for j in range(2):
                        ps, psb = pp.get()
                        zgroup(wC, wCb, j * 128, 128, ps, psb)
                        k2 = j % 2
                        act(lambda ps=ps, k2=k2: nc.scalar.activation(out=sq[k2][:], in_=ps[:, :], func=AF.Square), [psb], [sqb[k2]])
                        dve(lambda ps=ps, j=j: nc.vector.tensor_copy(out=zkv[:, j, :], in_=ps[:, :]), [psb], [zkvb])
                        mm(pss[:, :], ones_bf[:, :], sq[k2][:], j == 0, j == 1, [sqb[k2], constb], [pssb])
                    rstd_from_psum(pss[:, :], rs[:], 256, pssb, rsb)
                    for j in range(2):
                        dve(lambda j=j: nc.vector.scalar_tensor_tensor(out=ckvT[:, j, :], in0=zkv[:, j, :], scalar=kvaw[:, j:j + 1], in1=rs[:], op0=ALU.mult, op1=ALU.mult), [zkvb, rsb, wres], [ckvb])
                    if i + 1 < NT:
                        xload(i + 1, 0)
                        xload(i + 1, 3)
                    ps, psb = pp.get()
                    zgroup(wC, wCb, 192, 96, ps, psb)
                    dve(lambda ps=ps: nc.vector.tensor_scalar(out=kpw[64:96, :], in0=ps[64:96, :], scalar1=wk96[64:96, 0:1], scalar2=None, op0=ALU.mult), [psb, wres], [kpwb])
                    ps2, ps2b = pp.get()
                    mm(ps2[0:96, :], prot_bf[:, :], kpw[:, :], True, True, [kpwb, constb], [ps2b])
                    dve(lambda: nc.vector.tensor_tensor(out=t1[0][64:96, :], in0=kpw[64:96, :], in1=cosT[64:96, :], op=ALU.mult), [kpwb, csb], [t1b[0]])
                    dve(lambda ps2=ps2: nc.vector.tensor_tensor(out=t2[0][64:96, :], in0=ps2[64:96, :], in1=sinT[64:96, :], op=ALU.mult), [ps2b, csb], [t2b[0]])
                    dve(lambda: nc.vector.tensor_tensor(out=krope[64:96, :], in0=t1[0][64:96, :], in1=t2[0][64:96, :], op=ALU.add), [t1b[0], t2b[0]], [kropeb])
                    stage(5)
                    for s in range(4):
                        ps, psb = pp.get()
                        for kc in range(16):
                            mm(ps[:, 0:288], hT[:, kc, s * 128:(s + 1) * 128], wC[:, kc, 0:288], kc == 0, kc == 15, [wCb, hb[kc]], [psb])
                        act(lambda ps=ps: nc.scalar.activation(out=tm32[:], in_=ps[:, 256:288], func=AF.Square), [psb], [tm32b])
                        dve(lambda s=s: nc.vector.reduce_sum(out=sspe[:, s:s + 1], in_=tm32[:], axis=AX.X), [tm32b], [sspeb])
                        if i == 8:
                            p2 = cnt["p"] % 2; cnt["p"] += 1
                            seq, pos0 = s // 2, (s % 2) * 128
                            act(lambda ps=ps: nc.scalar.activation(out=tm256[:], in_=ps[:, 0:256], func=AF.Square), [psb], [tm256b])
                            dve(lambda: nc.vector.reduce_sum(out=ss1[:], in_=tm256[:], axis=AX.X), [tm256b], [ss1b])
                            act(lambda: nc.scalar.activation(out=ss1[:], in_=ss1[:], func=AF.Sqrt, scale=1.0 / 256, bias=EPS), [ss1b], [ss1b])
                            dve(lambda: nc.vector.reciprocal(out=ss1[:], in_=ss1[:]), [ss1b], [ss1b])
                            dve(lambda ps=ps, p2=p2: nc.vector.scalar_tensor_tensor(out=stc[p2][:], in0=ps[:, 0:256], scalar=ss1[:, 0:1], in1=kvawb[:], op0=ALU.mult, op1=ALU.mult), [psb, ss1b, wres], [stcb[p2]])
                            dve(lambda ps=ps, p2=p2: nc.vector.tensor_copy(out=stk[p2][:], in_=ps[:, 256:288]), [psb], [stkb[p2]])
                            spdma(st_ckv[seq, l, pos0:pos0 + 128, :], stc[p2][:], [stcb[p2]], [])
                            spdma(st_kpe[seq, l, pos0:pos0 + 128, :], stk[p2][:], [stkb[p2]], [])
                    stage(6)
                    qps = {}

                    def qA(h):
                        q2 = h % 2
                        ps, psb = pp.get()
                        for kc in range(6):
                            mm(ps[0:96, :], wuq[:, kc, h * 96:(h + 1) * 96], cqT[:, kc, :], kc == 0, kc == 5, [wres, cqb], [psb])
                        act(lambda: nc.scalar.activation(out=sq96[q2][:], in_=ps[0:96, :], func=AF.Square), [psb], [sq96b[q2]])
                        act(lambda: nc.scalar.activation(out=qw[q2][:], in_=ps[0:96, :], func=AF.Copy, scale=wq96[:, 0:1]), [psb, wres], [qwb[q2]])

                    def qB_pe(h):
                        q2 = h % 2
                        pq, pqb = pp.get()
                        mm(pq[0:96, :], ones_bf[0:96, 0:96], sq96[q2][:], True, True, [sq96b[q2], constb], [pqb])
                        act(lambda: nc.scalar.activation(out=r96[q2][:], in_=pq[0:96, :], func=AF.Sqrt, scale=1.0 / 96, bias=EPS), [pqb], [r96b[q2]])

                    def qB_dve(h):
                        q2 = h % 2
                        dve(lambda: nc.vector.reciprocal(out=r96[q2][:], in_=r96[q2][:]), [r96b[q2]], [r96b[q2]])

                    def qC(h):
                        q2 = h % 2
                        dve(lambda: nc.vector.tensor_tensor(out=qn[q2][:], in0=qw[q2][:], in1=r96[q2][:], op=ALU.mult), [qwb[q2], r96b[q2]], [qnb[q2]])
                        pr, prb = pp.get()
                        mm(pr[0:96, :], prot_bf[:, :], qn[q2][:], True, True, [qnb[q2], constb], [prb])
                        qps[h] = (pr, prb)

                    def qD1(h):
                        q2 = h % 2
                        pr, prb = qps[h]
                        dve(lambda: nc.vector.tensor_tensor(out=t1[q2][:], in0=qn[q2][:], in1=cosT[:], op=ALU.mult), [qnb[q2], csb], [t1b[q2]])
                        dve(lambda: nc.vector.tensor_tensor(out=t2[q2][:], in0=pr[0:96, :], in1=sinT[:], op=ALU.mult), [prb, csb], [t2b[q2]])

                    def qD2(h, tok0=tok0):
                        q2 = h % 2
                        dve(lambda: nc.vector.tensor_tensor(out=Qh[q2][:], in0=t1[q2][:], in1=t2[q2][:], op=ALU.add), [t1b[q2], t2b[q2]], [Qhb[q2]])
                        spdma(Qsc[:, h, tok0:tok0 + TT], Qh[q2][:], [Qhb[q2]], [])
                    for st in range(H + 3):
                        if 0 <= st - 1 < H:
                            qB_pe(st - 1)
                        if 0 <= st - 2 < H:
                            qC(st - 2)
                        if 0 <= st - 3 < H:
                            qD1(st - 3)
                        if st < H:
                            qA(st)
                        if 0 <= st - 1 < H:
                            qB_dve(st - 1)
                        if 0 <= st - 3 < H:
                            qD2(st - 3)
                    stage(7)
                    gen_kv(ckvT, krope, kropeb, TT, key0, sspe)
                    stage(8)
                    wD, wDb = wp.load(w_in[l][:, OFF_POOL:OFF_POOL + 512], 16, 512)
                    for j in range(4):
                        ps, psb = pp.get()
                        zgroup(wD, wDb, j * 128, 128, ps, psb)
                        p2 = j % 2
                        act(lambda ps=ps, p2=p2: nc.scalar.copy(out=ptmp[p2][:], in_=ps[:, :]), [psb], [ptmpb[p2]])
                        spdma(POOLsc[j * 128:(j + 1) * 128, tok0:tok0 + TT], ptmp[p2][:], [ptmpb[p2]], [])
                    stage(9)
                    wE, wEb = wp.load(w_in[l][:, OFF_SGU:OFF_SGU + 512], 16, 512)
                    for j in range(4):
                        ps, psb = pp.get()
                        zgroup(wE, wEb, j * 128, 128, ps, psb)
                        act(lambda ps=ps, j=j: nc.scalar.activation(out=u_t[:, j, :], in_=ps[:, :], func=AF.Gelu_apprx_tanh), [psb], [ub])
                    wF, wFb = wp.load(w_in[l][:, OFF_SGU + 512:OFF_SGU + 1024], 16, 512)
                    def sA(s_):
                        v2 = s_ % 2
                        ps, psb = pp.get()
                        for kc in range(16):
                            mm(ps[:, :], hT[:, kc, s_ * 128:(s_ + 1) * 128], wF[:, kc, :], kc == 0, kc == 15, [wFb, hb[kc]], [psb])
                        act(lambda: nc.scalar.activation(out=vg[v2][:], in_=ps[:, :], func=AF.Gelu_apprx_tanh), [psb], [vgb[v2]])
                        dve(lambda: nc.vector.tensor_tensor(out=tmpf[v2][:], in0=vg[v2][:], in1=vg[v2][:], op=ALU.mult), [vgb[v2]], [tmpb[v2]])
                        dve(lambda: nc.vector.reduce_sum(out=ssv[:, s_:s_ + 1], in_=tmpf[v2][:], axis=AX.X), [tmpb[v2]], [ssvb[s_]])
                        act(lambda: nc.scalar.activation(out=ssv[:, s_:s_ + 1], in_=ssv[:, s_:s_ + 1], func=AF.Sqrt, scale=1.0 / 512, bias=EPS), [ssvb[s_]], [ssvb[s_]])

                    def sA2(s_):
                        v2 = s_ % 2
                        dve(lambda: nc.vector.reciprocal(out=ssv[:, s_:s_ + 1], in_=ssv[:, s_:s_ + 1]), [ssvb[s_]], [ssvb[s_]])
                        dve(lambda: nc.vector.tensor_scalar(out=vh[v2][:], in0=vg[v2][:], scalar1=ssv[:, s_:s_ + 1], scalar2=None, op0=ALU.mult), [vgb[v2], ssvb[s_]], [vhb[v2]])

                    def sB(s_):
                        v2 = s_ % 2
                        pm, pmb = pp.get()
                        for gg in range(4):
                            mm(pm[:, gg * 128:(gg + 1) * 128], vh[v2][:, gg * 128:(gg + 1) * 128], wspT[:, gg, :], True, True, [vhb[v2], wres], [pmb])
                        for gg in range(4):
                            dve(lambda gg=gg: nc.vector.scalar_tensor_tensor(out=mx[gg % 2][:], in0=pm[:, gg * 128:(gg + 1) * 128], scalar=sguw[:, gg:gg + 1], in1=bspb[:, gg * 128:(gg + 1) * 128], op0=ALU.mult, op1=ALU.add), [pmb, wres], [mxb[gg % 2]])
                            dve(lambda gg=gg: nc.vector.tensor_tensor(out=sgo[:, gg, s_ * 128:(s_ + 1) * 128], in0=mx[gg % 2][:], in1=u_t[:, gg, s_ * 128:(s_ + 1) * 128], op=ALU.mult), [mxb[gg % 2], ub], [sgob])
                    sA(0); sA(1); sA2(0); sB(0); sA(2); sA2(1); sB(1); sA(3); sA2(2); sB(2); sA2(3); sB(3)
                    spdma(SGUsc[:, tok0:tok0 + TT].rearrange("(g p) t -> p g t", p=128), sgo[:], [sgob], [])
                    if i + 1 < NT:
                        xload(i + 1, 1)
                    stage(10)
                    wG, wGb = wp.load(w_in[l][:, OFF_CONV:OFF_CONV + 512], 16, 512)
                    for j in range(4):
                        ps, psb = pp.get()
                        zgroup(wG, wGb, j * 128, 128, ps, psb)
                        p2 = j % 2
                        act(lambda ps=ps, p2=p2: nc.scalar.copy(out=ptmp[p2][:], in_=ps[:, :]), [psb], [ptmpb[p2]])
                        spdma(CONVb[j * 128:(j + 1) * 128, tok0:tok0 + TT], ptmp[p2][:], [ptmpb[p2]], [])
                    wH, wHb = wp.load(w_in[l][:, OFF_CONV + 512:OFF_CONV + 1024], 16, 512)
                    for j in range(4):
                        ps, psb = pp.get()
                        zgroup(wH, wHb, j * 128, 128, ps, psb)
                        act(lambda ps=ps, j=j: nc.scalar.copy(out=cg[:, j, :], in_=ps[:, :]), [psb], [cgb])
                    wI, wIb = wp.load(w_in[l][:, OFF_CONV + 1024:OFF_CONV + 1536], 16, 512)
                    for j in range(4):
                        ps, psb = pp.get()
                        zgroup(wI, wIb, j * 128, 128, ps, psb)
                        p2 = j % 2
                        dve(lambda ps=ps, p2=p2, j=j: nc.vector.tensor_tensor(out=ptmp[p2][:], in0=ps[:, :], in1=cg[:, j, :], op=ALU.mult), [psb, cgb], [ptmpb[p2]])
                        spdma(CONVy[j * 128:(j + 1) * 128, tok0:tok0 + TT], ptmp[p2][:], [ptmpb[p2]], [])
                    if i + 1 < NT:
                        xload(i + 1, 2)
                    stage(11)
                except _Stop:
                  pass
                S.flush()

        brKk = [8, 4, 4, 4]

        def phase2(l):
            oc = 0
            with ExitStack() as es:
                ppS = PsumPool(es, 5)
                ppO = PsumPool(es, 2)
                ppD = PsumPool(es, 1)
                Vall = T(es, "Vall", [128, NKB, H * 65], BF16); Vb = Buf()
                SCall = T(es, "SCall", [128, NKB, H], F32); SCb = Buf()
                Kt = [T(es, "Kt%d" % i, [96, NKEY], BF16) for i in range(2)]; Ktb = [Buf(), Buf()]
                Qt = [T(es, "Qt%d" % i, [96, NTOK], BF16) for i in range(2)]; Qtb = [Buf(), Buf()]
                pT = [T(es, "pT%d" % i, [128, 512], BF16) for i in range(4)]; pTb = [Buf() for _ in range(4)]
                osb = [T(es, "osb%d" % i, [65, 512], F32) for i in range(2)]; osbb = [Buf(), Buf()]
                rden = T(es, "rden", [64, 512], F32); rdenb = Buf()
                on = [T(es, "on%d" % i, [64, 512], BF16) for i in range(2)]; onb = [Buf(), Buf()]
                for c in range(2):
                    spdma(Vall[:, c * 19:(c + 1) * 19, :], Vsc[c * 19:(c + 1) * 19].rearrange("kb p c -> p kb c"), [], [Vb])
                spdma(SCall[:], SCsc.rearrange("kb p h -> p kb h"), [], [SCb])
                spdma(Kt[0][:], Ksc[:, 0, :], [], [Ktb[0]])
                spdma(Qt[0][:], Qsc[:, 0, :], [], [Qtb[0]])
                for r0 in range(0, D, 128):
                    pldma(WG[r0:r0 + 128, :], w_in[l][r0:r0 + 128, OFF_GATE:OFF_GATE + 4 * D], [Vb, SCb, Ktb[0], Qtb[0]] if r0 == 0 else [], [])
                for br in range(4):
                    for r0 in range(0, brKk[br] * 128, 128):
                        pldma(WBR[BR_OFF[br] + r0:BR_OFF[br] + r0 + 128, :], w_br[br][l][r0:r0 + 128, :], [], [])
                for r0 in range(0, D, 128):
                    pldma(WO[r0:r0 + 128, :], w_out[l][r0:r0 + 128, :], [], [])
                for r0 in range(0, D, 128):
                    pldma(WFI[r0:r0 + 128, :], w_ffn_in[l][r0:r0 + 128, :], [], [])
                for r0 in range(0, DFF, 128):
                    pldma(WFO[r0:r0 + 128, :], w_ffn_out[l][r0:r0 + 128, :], [], [])
                jobs = [(qt * 512, 512, list(range(34))) for qt in range(8)]
                jobs += [(4096 + 256 * s, 256, [34 + 2 * s, 35 + 2 * s]) for s in range(2)]
                pc = 0
                LA = 4
                pending = [None]

                def run_pending():
                    if pending[0] is not None:
                        pending[0]()
                        pending[0] = None
                for h in range(H):
                    hb2 = h % 2
                    for ji, (q0, nq, kbs) in enumerate(jobs):
                        if ji == 1 and h + 1 < H:
                            spdma(Kt[1 - hb2][:], Ksc[:, h + 1, :], [], [Ktb[1 - hb2]])
                            spdma(Qt[1 - hb2][:], Qsc[:, h + 1, :], [], [Qtb[1 - hb2]])
                        po, pob = ppO.get()
                        sps = {}

                        def smm(idx, kbs=kbs, q0=q0, nq=nq, hb2=hb2, sps=sps):
                            ps, psb = ppS.get()
                            kb = kbs[idx]
                            mm(ps[:, 0:nq], Kt[hb2][:, kb * 128:(kb + 1) * 128], Qt[hb2][:, q0:q0 + nq], True, True, [Ktb[hb2], Qtb[hb2]], [psb])
                            sps[idx] = (ps, psb)
                        for k0 in range(min(LA, len(kbs))):
                            smm(k0)
                        run_pending()
                        for idx, kb in enumerate(kbs):
                            ps, psb = sps.pop(idx)
                            r = pc % 4; pc += 1
                            act(lambda ps=ps, r=r, kb=kb, h=h, nq=nq: nc.scalar.activation(out=pT[r][:, 0:nq], in_=ps[:, 0:nq], func=AF.Exp, scale=SCall[:, kb, h:h + 1]), [psb, SCb], [pTb[r]])
                            mm(po[0:65, 0:nq], Vall[:, kb, h * 65:(h + 1) * 65], pT[r][:, 0:nq], idx == 0, idx == len(kbs) - 1, [Vb, pTb[r]], [pob])
                            if idx + LA < len(kbs):
                                smm(idx + LA)

                        def norm(po=po, pob=pob, nq=nq, q0=q0, h=h):
                            nonlocal oc
                            o2 = oc % 2; oc += 1
                            dve(lambda: nc.vector.tensor_copy(out=osb[o2][:, 0:nq], in_=po[0:65, 0:nq]), [pob], [osbb[o2]])
                            pd, pdb = ppD.get()
                            mm(pd[0:64, 0:nq], sel_f[:, :], osb[o2][:, 0:nq], True, True, [osbb[o2], constb], [pdb])
                            dve(lambda: nc.vector.reciprocal(out=rden[:, 0:nq], in_=pd[0:64, 0:nq]), [pdb], [rdenb])
                            dve(lambda: nc.vector.tensor_tensor(out=on[o2][:, 0:nq], in0=osb[o2][0:64, 0:nq], in1=rden[:, 0:nq], op=ALU.mult), [osbb[o2], rdenb], [onb[o2]])
                            spdma(ATTNsc[h * 64:(h + 1) * 64, q0:q0 + nq], on[o2][:, 0:nq], [onb[o2]], [])
                        pending[0] = norm
                run_pending()
                S.flush()

        def phase3(l):
            XIN = xT_in if l == 0 else XT1
            XOUT = XT1 if l == 0 else yT
            with ExitStack() as es:
                pp = PsumPool(es, 7)
                ppx = PsumPool(es, 1)
                pss3, pss3b = ppx.get()
                wp = WPool(es, 3)
                xT = T(es, "xT3", [128, 16, 512], F32); xb = [Buf() for _ in range(16)]
                hT = T(es, "hT3", [128, 16, 512], BF16); hb = [Buf() for _ in range(16)]
                sq = [T(es, "sq3%d" % i, [128, 512], BF16) for i in range(2)]; sqb = [Buf(), Buf()]
                rs = T(es, "rs3", [128, 512], F32); rsb = Buf()
                tmpf = [T(es, "tmpf3%d" % i, [128, 512], F32) for i in range(2)]; tmpb = [Buf(), Buf()]
                R = T(es, "R3", [128, 22800], BF16)
                actT = R[:, 0:22528].rearrange("p (k t) -> p k t", t=512); actb = Buf()
                Rf = R[:, :].bitcast(F32)
                zp = Rf[:, 0:2112].rearrange("p (g t) -> p g t", t=528); zpb = Buf()
                pa = [Rf[:, 2112 + k * 528:2112 + (k + 1) * 528] for k in range(4)]; pab = Buf()
                cy = Rf[:, 4224:6280].rearrange("p (g t) -> p g t", t=514); cyb = Buf()
                cb_ = Rf[:, 6280:8328].rearrange("p (g t) -> p g t", t=512); cbb = Buf()
                ctmp = Rf[:, 8328:8840]; ctb = Buf()
                ptm = Rf[:, 8840:9352]; ptmb = Buf()
                invt = Rf[:, 9352:11400].rearrange("p (g t) -> p g t", t=512); invb = Buf()
                brT_attn = T(es, "brA", [128, 8, 512], BF16)
                brT_pool = T(es, "brP", [128, 4, 512], BF16)
                brT_sgu = T(es, "brS", [128, 4, 512], BF16)
                brT_conv = T(es, "brC", [128, 4, 512], BF16)
                brb = [Buf(), Buf(), Buf(), Buf()]
                pl = T(es, "pl", [128, 512], BF16); plb = Buf()
                mT = T(es, "mT", [128, 16, 512], BF16); mb_ = Buf()
                macc = T(es, "macc", [128, 512], F32); maccb = Buf()
                maccs = [T(es, "maccs%d" % i, [128, 512], F32) for i in range(4)]; maccsb = [Buf() for _ in range(4)]
                gt = [T(es, "gt%d" % i, [128, 512], F32) for i in range(2)]; gtb = [Buf(), Buf()]
                wpl = T(es, "wpl", [128, 4, 128], BF16)
                psc = T(es, "psc", [128, 4], F32); cw = T(es, "cw", [128, 12], F32); bg = T(es, "bg", [128, 64], F32)
                wres = Buf()
                pldma(wpl[:], w_pool[l].rearrange("g i o -> i g o"), [], [wres])
                spdma(psc[:], pscaleT[l], [], [wres]); spdma(cw[:], convwT[l], [], [wres]); spdma(bg[:], b_gateT[l], [], [wres])
                brT = [brT_attn, brT_pool, brT_sgu, brT_conv]
                brK = [8, 4, 4, 4]
                gc = 0
                for i in range(NT):
                    g = 0 if i < 8 else 1
                    tok0 = i * TT
                    a1 = mvec(l, g, 0); b1 = mvec(l, g, 1); ga = mvec(l, g, 2)
                    a2 = mvec(l, g, 3); b2 = mvec(l, g, 4); gf = mvec(l, g, 5)
                    if i == 0:
                        for g4 in range(4):
                            spdma(xT[:, 4 * g4:4 * g4 + 4, :], XIN[g4 * 512:(g4 + 1) * 512, tok0:tok0 + TT].rearrange("(kc p) t -> p kc t", p=128), [], xb[4 * g4:4 * g4 + 4])
                    spdma(brT_attn[:], ATTNsc[:, tok0:tok0 + TT].rearrange("(kc p) t -> p kc t", p=128), [], [brb[0]])
                    spdma(brT_sgu[:], SGUsc[:, tok0:tok0 + TT].rearrange("(kc p) t -> p kc t", p=128), [], [brb[2]])
                    segs = [(tok0, 512, i > 0, i < 7)] if i < 8 else [(tok0, 256, False, False), (tok0 + 256, 256, False, False)]
                    kind = (0 if i == 0 else (2 if i == 7 else 1)) if i < 8 else 3
                    spdma(invt, inv_in[kind], [], [invb, actb])
                    for (c0, W, lh, rh) in segs:
                        so = c0 - tok0
                        dve(lambda: nc.vector.memset(zp, 0.0), [], [zpb, actb])
                        dve(lambda: nc.vector.memset(cy, 0.0), [], [cyb, actb])
                        lo = c0 - (8 if lh else 0); hi = c0 + W + (8 if rh else 0)
                        spdma(zp[:, :, 8 - (c0 - lo):8 + (hi - c0)], POOLsc[:, lo:hi].rearrange("(g p) t -> p g t", p=128), [], [zpb])
                        lo = c0 - (1 if lh else 0); hi = c0 + W + (1 if rh else 0)
                        spdma(cy[:, :, 1 - (c0 - lo):1 + (hi - c0)], CONVy[:, lo:hi].rearrange("(g p) t -> p g t", p=128), [], [cyb])
                        spdma(cb_[:, :, 0:W], CONVb[:, c0:c0 + W].rearrange("(g p) t -> p g t", p=128), [], [cbb, actb])
                        n = W + 16
                        for gg in range(4):
                            X = zp[:, gg, :]
                            dve(lambda X=X, n=n: nc.vector.tensor_tensor(out=pa[0][:, 1:n], in0=X[:, 1:n], in1=X[:, 0:n - 1], op=ALU.add), [zpb], [pab, actb])
                            E = pa[0][:, 8:8 + W]
                            if gg >= 1:
                                dve(lambda n=n: nc.vector.tensor_tensor(out=pa[1][:, 3:n], in0=pa[0][:, 3:n], in1=pa[0][:, 1:n - 2], op=ALU.add), [pab], [pab])
                                E = pa[1][:, 9:9 + W]
                            if gg >= 2:
                                dve(lambda n=n: nc.vector.tensor_tensor(out=pa[2][:, 7:n], in0=pa[1][:, 7:n], in1=pa[1][:, 3:n - 4], op=ALU.add), [pab], [pab])
                                E = pa[2][:, 11:11 + W]
                            if gg >= 3:
                                dve(lambda n=n: nc.vector.tensor_tensor(out=pa[3][:, 15:n], in0=pa[2][:, 15:n], in1=pa[2][:, 7:n - 8], op=ALU.add), [pab], [pab])
                                E = pa[3][:, 15:15 + W]
                            dve(lambda E=E, gg=gg, W=W, so=so: nc.vector.tensor_tensor(out=ptm[:, 0:W], in0=E, in1=invt[:, gg, so:so + W], op=ALU.mult), [pab, invb], [ptmb, actb])
                            dve(lambda X=X, W=W: nc.vector.tensor_tensor(out=pl[:, 0:W], in0=ptm[:, 0:W], in1=X[:, 8:8 + W], op=ALU.subtract), [ptmb, zpb], [plb])
                            ps, psb = pp.get()
                            mm(ps[:, 0:W], wpl[:, gg, :], pl[:, 0:W], True, True, [plb, wres], [psb])
                            act(lambda ps=ps, gg=gg, W=W, so=so: nc.scalar.activation(out=brT_pool[:, gg, so:so + W], in_=ps[:, 0:W], func=AF.Copy, scale=psc[:, gg:gg + 1]), [psb, wres], [brb[1]])
                        for j in range(4):
                            Y = cy[:, j, :]
                            dve(lambda Y=Y, j=j, W=W: nc.vector.tensor_scalar(out=ctmp[:, 0:W], in0=Y[:, 0:W], scalar1=cw[:, j:j + 1], scalar2=None, op0=ALU.mult), [cyb, wres], [ctb, actb])
                            dve(lambda Y=Y, j=j, W=W: nc.vector.scalar_tensor_tensor(out=ctmp[:, 0:W], in0=Y[:, 1:1 + W], scalar=cw[:, 4 + j:5 + j], in1=ctmp[:, 0:W], op0=ALU.mult, op1=ALU.add), [cyb, ctb, wres], [ctb])
                            dve(lambda Y=Y, j=j, W=W: nc.vector.scalar_tensor_tensor(out=ctmp[:, 0:W], in0=Y[:, 2:2 + W], scalar=cw[:, 8 + j:9 + j], in1=ctmp[:, 0:W], op0=ALU.mult, op1=ALU.add), [cyb, ctb, wres], [ctb])
                            dve(lambda j=j, W=W, so=so: nc.vector.tensor_tensor(out=brT_conv[:, j, so:so + W], in0=ctmp[:, 0:W], in1=cb_[:, j, 0:W], op=ALU.mult), [ctb, cbb], [brb[3]])
                    norm_mod(pp, xT, xb, hT, hb, a1, b1, sq, sqb, rs, rsb, tmpf, tmpb)
                    for jb in range(4):
                        for br in range(4):
                            c0 = br * D + jb * 512
                            wg, wgb = wp.load(WG[:, c0:c0 + 512], 16, 512)
                            wb_, wbb = wp.load(WBR[BR_OFF[br]:BR_OFF[br] + brK[br] * 128, jb * 512:(jb + 1) * 512], brK[br], 512)
                            for jj in range(4):
                                j = jb * 4 + jj
                                pg, pgb = pp.get()
                                for kc in range(16):
                                    mm(pg[:, :], wg[:, kc, jj * 128:(jj + 1) * 128], hT[:, kc, :], kc == 0, kc == 15, [wgb, hb[kc]], [pgb])
                                pb, pbb = pp.get()
                                for kc in range(brK[br]):
                                    mm(pb[:, :], wb_[:, kc, jj * 128:(jj + 1) * 128], brT[br][:, kc, :], kc == 0, kc == brK[br] - 1, [wbb, brb[br]], [pbb])
                                g2 = gc % 2; gc += 1
                                act(lambda pg=pg, g2=g2, br=br, j=j: nc.scalar.activation(out=gt[g2][:], in_=pg[:, :], func=AF.Sigmoid, bias=bg[:, br * 16 + j:br * 16 + j + 1], scale=1.0), [pgb, wres], [gtb[g2]])
                                if br == 0:
                                    dve(lambda pb=pb, g2=g2, jj=jj: nc.vector.tensor_tensor(out=maccs[jj][:], in0=gt[g2][:], in1=pb[:, :], op=ALU.mult), [gtb[g2], pbb], [maccsb[jj]])
                                else:
                                    dve(lambda pb=pb, g2=g2: nc.vector.tensor_tensor(out=macc[:], in0=gt[g2][:], in1=pb[:, :], op=ALU.mult), [gtb[g2], pbb], [maccb])
                                    if br < 3:
                                        dve(lambda jj=jj: nc.vector.tensor_tensor(out=maccs[jj][:], in0=maccs[jj][:], in1=macc[:], op=ALU.add), [maccb, maccsb[jj]], [maccsb[jj]])
                                    else:
                                        dve(lambda jj=jj, j=j: nc.vector.tensor_tensor(out=mT[:, j, :], in0=maccs[jj][:], in1=macc[:], op=ALU.add), [maccb, maccsb[jj]], [mb_])
                    for jb in range(4):
                        wo, wob = wp.load(WO[:, jb * 512:(jb + 1) * 512], 16, 512)
                        for jj in range(4):
                            j = jb * 4 + jj
                            ps, psb = pp.get()
                            for kc in range(16):
                                mm(ps[:, :], wo[:, kc, jj * 128:(jj + 1) * 128], mT[:, kc, :], kc == 0, kc == 15, [wob, mb_], [psb])
                            dve(lambda ps=ps, j=j, ga=ga: nc.vector.scalar_tensor_tensor(out=xT[:, j, :], in0=ps[:, :], scalar=ga[:, j:j + 1], in1=xT[:, j, :], op0=ALU.mult, op1=ALU.add), [psb, xb[j], MVb], [xb[j]])
                            if j >= 1:
                                norm_ss_chunk(pss3, pss3b, xT, xb, sq, sqb, j - 1)
                    norm_ss_chunk(pss3, pss3b, xT, xb, sq, sqb, 15)
                    norm_apply(pss3, pss3b, xT, xb, hT, hb, a2, b2, rs, rsb, tmpf, tmpb)
                    for jb in range(11):
                        wa, wab = wp.load(WFI[:, jb * 512:(jb + 1) * 512], 16, 512)
                        wl, wlb = wp.load(WFI[:, DFF + jb * 512:DFF + (jb + 1) * 512], 16, 512)
                        for jj in range(4):
                            j = jb * 4 + jj
                            pa_, pab_ = pp.get()
                            for kc in range(16):
                                mm(pa_[:, :], wa[:, kc, jj * 128:(jj + 1) * 128], hT[:, kc, :], kc == 0, kc == 15, [wab, hb[kc]], [pab_])
                            pl_, plb_ = pp.get()
                            for kc in range(16):
                                mm(pl_[:, :], wl[:, kc, jj * 128:(jj + 1) * 128], hT[:, kc, :], kc == 0, kc == 15, [wlb, hb[kc]], [plb_])
                            g2 = gc % 2; gc += 1
                            act(lambda pa_=pa_, g2=g2: nc.scalar.activation(out=gt[g2][:], in_=pa_[:, :], func=AF.Silu), [pab_], [gtb[g2]])
                            dve(lambda pl_=pl_, g2=g2, j=j: nc.vector.tensor_tensor(out=actT[:, j, :], in0=gt[g2][:], in1=pl_[:, :], op=ALU.mult), [gtb[g2], plb_], [actb, zpb, pab, cyb, cbb, ctb, ptmb, invb])
                    for jb in range(4):
                        accs = [pp.get() for _ in range(4)]
                        for kb0, nk in ((0, 16), (16, 16), (32, 12)):
                            wf, wfb = wp.load(WFO[kb0 * 128:(kb0 + nk) * 128, jb * 512:(jb + 1) * 512], nk, 512)
                            for jj in range(4):
                                ps, psb = accs[jj]
                                for kc in range(nk):
                                    kg = kb0 + kc
                                    mm(ps[:, :], wf[:, kc, jj * 128:(jj + 1) * 128], actT[:, kg, :], kg == 0, kg == 43, [wfb, actb], [psb])
                        for jj in range(4):
                            j = jb * 4 + jj
                            ps, psb = accs[jj]
                            dve(lambda ps=ps, j=j, gf=gf: nc.vector.scalar_tensor_tensor(out=xT[:, j, :], in0=ps[:, :], scalar=gf[:, j:j + 1], in1=xT[:, j, :], op0=ALU.mult, op1=ALU.add), [psb, xb[j], MVb], [xb[j]])
                        spdma(XOUT[jb * 512:(jb + 1) * 512, tok0:tok0 + TT].rearrange("(kc p) t -> p kc t", p=128), xT[:, 4 * jb:4 * jb + 4, :], xb[4 * jb:4 * jb + 4], [])
                        if i + 1 < NT:
                            spdma(xT[:, 4 * jb:4 * jb + 4, :], XIN[jb * 512:(jb + 1) * 512, tok0 + TT:tok0 + 2 * TT].rearrange("(kc p) t -> p kc t", p=128), [], xb[4 * jb:4 * jb + 4])
                S.flush()

        S.flush()
        for l in range(nlayers):
            phase_mod(l)
        if dbg and "dbg_MV" in dbg:
            dmv = nc.dram_tensor("dbg_MV", [128, L * 12, 16], F32, kind="ExternalOutput").ap()
            spdma(dmv, MV[:], [MVb], [])
            S.flush()
        if stop_after == "p0":
            nlayers = 0
        for l in range(nlayers):
            phase1(l)
            if stop_after == ("p1", l):
                break
            phase2(l)
            if stop_after == ("p2", l):
                break
            phase3(l)
            if stop_after == ("p3", l):
                break
    return nc


def _rope_tables():
    T_ = NS
    rows = T_ // 64
    row = np.repeat(np.arange(rows), 64).astype(np.float32)
    col = np.tile(np.arange(64), rows).astype(np.float32)
    inv = (10000.0 ** (-np.arange(8, dtype=np.float32) / 8)).astype(np.float32)
    ang_r = row[:, None] * inv
    ang_c = col[:, None] * inv
    ang = np.stack([ang_r, ang_r, ang_c, ang_c], axis=1).reshape(T_, 32)
    cos = np.ones((96, NTOK), np.float32)
    sin = np.zeros((96, NTOK), np.float32)
    cos[64:, :NS] = np.cos(ang).T
    sin[64:, :NS] = np.sin(ang).T
    return cos, sin


def _consts():
    cos, sin = _rope_tables()
    prot = np.zeros((96, 96), np.float32)
    for a in range(2):
        for f in range(8):
            i0 = 64 + a * 16 + f
            i1 = 64 + a * 16 + 8 + f
            prot[i1, i0] = -1.0
            prot[i0, i1] = 1.0
    sel = np.zeros((65, 64), np.float32)
    sel[64, :] = 1.0

    def inv_cnt(Tseq):
        t = np.arange(Tseq)
        out = np.zeros((4, Tseq), np.float32)
        for gi, w in enumerate((2, 4, 8, 16)):
            lo = np.clip(t - w // 2, 0, Tseq - 1)
            hi = np.clip(t + w // 2 - 1, 0, Tseq - 1)
            out[gi] = 1.0 / (hi - lo + 1).astype(np.float32)
        return out
    ic = inv_cnt(NS)
    ip = inv_cnt(256)
    tab = np.zeros((4, 4, 512), np.float32)
    tab[0] = ic[:, 0:512]
    tab[1] = ic[:, 512:1024]
    tab[2] = ic[:, NS - 512:NS]
    tab[3] = np.concatenate([ip, ip], axis=1)
    invtab = np.ascontiguousarray(np.broadcast_to(tab[:, None], (4, 128, 4, 512)))
    return cos, sin, prot, sel, invtab


def _vecT(v, n):
    return np.ascontiguousarray(v.reshape(v.shape[0], n, 128).transpose(0, 2, 1))


def make_in_maps(inp):
    f = lambda a: np.ascontiguousarray(np.asarray(a, dtype=np.float32))
    cos, sin, prot, sel, invtab = _consts()
    shared = {
        "w_mod": f(inp["w_mod"]), "b_modT": _vecT(f(inp["b_mod"]), 96),
        "nmixT": _vecT(f(inp["norm_mix_w"]), 16), "nffnT": _vecT(f(inp["norm_ffn_w"]), 16),
        "w_in": f(inp["w_in"]), "b_gateT": _vecT(f(inp["b_gate"]), 64),
        "qawT": _vecT(f(inp["q_a_norm_w"]), 6), "kvawT": _vecT(f(inp["kv_a_norm_w"]), 2),
        "kvaw_row": f(inp["kv_a_norm_w"]),
        "w_uq": f(inp["w_uq"]), "w_ukv": f(inp["w_ukv"]),
        "qnw": f(inp["q_norm_w"]).reshape(L, 96, 1), "knw": f(inp["k_norm_w"]).reshape(L, 96, 1),
        "w_pool": f(inp["w_pool"]), "pscaleT": _vecT(f(inp["pool_scale"]), 4),
        "sguwT": _vecT(f(inp["sgu_norm_w"]), 4),
        "w_spT": np.ascontiguousarray(f(inp["w_spatial"]).transpose(0, 1, 3, 2)),
        "b_sp": f(inp["b_spatial"]).reshape(L, 512),
        "convwT": np.ascontiguousarray(f(inp["conv_w"]).reshape(L, 3, 4, 128).transpose(0, 3, 1, 2).reshape(L, 128, 12)),
        "w_br_attn": f(inp["w_br_attn"]), "w_br_pool": f(inp["w_br_pool"]),
        "w_br_sgu": f(inp["w_br_sgu"]), "w_br_conv": f(inp["w_br_conv"]),
        "w_out": f(inp["w_out"]), "w_ffn_in": f(inp["w_ffn_in"]), "w_ffn_out": f(inp["w_ffn_out"]),
        "cosT": cos, "sinT": sin, "prot": prot, "sel": sel, "invtab": invtab,
    }
    xs = f(inp["x_sample"]); xp = f(inp["x_prompt"])
    cckv = f(inp["cache_ckv"]); ckpe = f(inp["cache_kpe"])
    c = f(inp["c"]); cctx = f(inp["c_ctx"])
    maps = []
    for b in range(8):
        xT = np.concatenate([xs[b].T, xp[2 * b].T, xp[2 * b + 1].T], axis=1)
        cv = np.stack([c[b], cctx], axis=1)
        cT = cv.reshape(16, 128, 2).transpose(1, 0, 2)
        m = dict(shared)
        m["xT"] = np.ascontiguousarray(xT)
        m["cT"] = np.ascontiguousarray(cT)
        m["cckvT"] = np.ascontiguousarray(cckv[b].transpose(0, 2, 1))
        m["ckpe"] = np.ascontiguousarray(ckpe[b])
        m["ckpeT"] = np.ascontiguousarray(ckpe[b].transpose(0, 2, 1))
        maps.append(m)
    return maps


def kernel(**inputs):
    nc = build()
    maps = make_in_maps(inputs)
    res = run_bass_kernel_spmd(nc, maps, core_ids=list(range(8)))
    y_prompt = np.zeros((16, 256, D), np.float32)
    y_sample = np.zeros((8, NS, D), np.float32)
    s_ckv = np.zeros((16, L, 256, 256), np.float32)
    s_kpe = np.zeros((16, L, 256, 32), np.float32)
    for b in range(8):
        r = res.results[b]
        yT = np.asarray(r["yT"])
        y_sample[b] = yT[:, :NS].T
        y_prompt[2 * b] = yT[:, NS:NS + 256].T
        y_prompt[2 * b + 1] = yT[:, NS + 256:].T
        s_ckv[2 * b:2 * b + 2] = np.asarray(r["st_ckv"])
        s_kpe[2 * b:2 * b + 2] = np.asarray(r["st_kpe"])
    return (y_prompt, y_sample, s_ckv, s_kpe)
```
